# Optimizing a Trainium2 kernel written in Bass

```python
import jax, jax.numpy as jnp
from jax import lax
import numpy as np

D_MODEL = 1024
BATCH = 16
SEQ = 2048
DEPTH = 1

ROPE_THETA = 500000.0
EPS = 1e-6
Q_BLOCK = 128
A_HEADS = 8
A_KV_HEADS = 2
A_HEAD_DIM = 64
A_ROT_DIM = A_HEAD_DIM // 4
IDX_HEADS = 8
IDX_DIM = 64
IDX_ROT_DIM = IDX_DIM // 4
INDEX_TOPK_MAX = 256
B_HEADS = 8
B_Q_RANK = 256
B_KV_RANK = 128
B_NOPE_DIM = 64
B_ROPE_DIM = 32
B_V_DIM = 64
D_FF = 2816
CONV_WIDTH = 3

A_Q_W = A_HEADS * A_HEAD_DIM
A_KV_W = A_KV_HEADS * A_HEAD_DIM
IDX_Q_W = IDX_HEADS * IDX_DIM
B_OUT_W = B_HEADS * B_V_DIM
IN_SPLITS = [A_Q_W, A_KV_W, A_KV_W, IDX_Q_W, IDX_DIM, IDX_HEADS,
             B_Q_RANK, B_KV_RANK, B_ROPE_DIM, D_MODEL, D_MODEL]
IN_PROJ_W = int(sum(IN_SPLITS))
IN_OFFSETS = [int(o) for o in np.cumsum(IN_SPLITS)[:-1]]

kernel_name = "hybrid_dsa_mla_gated_convffn"


def rmsnorm(x, g):
    xf = x.astype(jnp.float32)
    y = xf * lax.rsqrt(jnp.mean(xf * xf, axis=-1, keepdims=True) + EPS)
    return (y * g.astype(jnp.float32)).astype(x.dtype)


def rope_tables(positions, rot_dim):
    inv = ROPE_THETA ** (-jnp.arange(0, rot_dim, 2, dtype=jnp.float32) / rot_dim)
    ang = positions.astype(jnp.float32)[..., None] * inv
    return jnp.cos(ang), jnp.sin(ang)


def apply_rope(x, cos, sin, rot_dim):
    xr = x[..., :rot_dim].astype(jnp.float32)
    x1, x2 = jnp.split(xr, 2, axis=-1)
    c = cos[:, :, None, :]
    s = sin[:, :, None, :]
    rot = jnp.concatenate([x1 * c - x2 * s, x1 * s + x2 * c], axis=-1).astype(x.dtype)
    return jnp.concatenate([rot, x[..., rot_dim:]], axis=-1)


def to_blocks(t, nb):
    b = t.shape[0]
    return jnp.moveaxis(t.reshape(b, nb, Q_BLOCK, *t.shape[2:]), 1, 0)


def indexed_sparse_attention(q, k, v, qi, ki, wi, pos, topk):
    b, s = pos.shape
    nb = s // Q_BLOCK
    group = A_HEADS // A_KV_HEADS
    scale = A_HEAD_DIM ** -0.5
    bidx = jnp.arange(b)[:, None, None]

    def one_block(args):
        qb, qib, wib, posb = args
        dots = jnp.einsum('bqhd,bsd->bqhs', qib, ki, preferred_element_type=jnp.float32)
        iscore = jnp.einsum('bqh,bqhs->bqs', wib.astype(jnp.float32), jax.nn.relu(dots))
        causal = pos[:, None, :] <= posb[:, :, None]
        iscore = jnp.where(causal, iscore, -jnp.inf)
        _, idx = lax.top_k(iscore, topk)
        ksel = k[bidx, idx]
        vsel = v[bidx, idx]
        valid = pos[bidx, idx] <= posb[:, :, None]
        qg = qb.reshape(b, Q_BLOCK, A_KV_HEADS, group, A_HEAD_DIM)
        sc = jnp.einsum('bqgrd,bqkgd->bqgrk', qg, ksel, preferred_element_type=jnp.float32) * scale
        sc = jnp.where(valid[:, :, None, None, :], sc, -jnp.inf)
        p = jax.nn.softmax(sc, axis=-1)
        o = jnp.einsum('bqgrk,bqkgd->bqgrd', p.astype(v.dtype), vsel)
        return o.reshape(b, Q_BLOCK, A_HEADS * A_HEAD_DIM)

    out = lax.map(one_block, (to_blocks(q, nb), to_blocks(qi, nb), to_blocks(wi, nb), to_blocks(pos, nb)))
    return jnp.moveaxis(out, 0, 1).reshape(b, s, A_HEADS * A_HEAD_DIM)


def latent_attention(q_nope, q_rope, k_nope, k_rope, v, pos):
    b, s = pos.shape
    nb = s // Q_BLOCK
    scale = (B_NOPE_DIM + B_ROPE_DIM) ** -0.5

    def one_block(args):
        qn, qr, posb = args
        sc = (jnp.einsum('bqhd,bshd->bhqs', qn, k_nope, preferred_element_type=jnp.float32)
              + jnp.einsum('bqhd,bsd->bhqs', qr, k_rope, preferred_element_type=jnp.float32)) * scale
        causal = pos[:, None, None, :] <= posb[:, None, :, None]
        sc = jnp.where(causal, sc, -jnp.inf)
        p = jax.nn.softmax(sc, axis=-1)
        o = jnp.einsum('bhqs,bshd->bqhd', p.astype(v.dtype), v)
        return o.reshape(b, Q_BLOCK, B_HEADS * B_V_DIM)

    out = lax.map(one_block, (to_blocks(q_nope, nb), to_blocks(q_rope, nb), to_blocks(pos, nb)))
    return jnp.moveaxis(out, 0, 1).reshape(b, s, B_HEADS * B_V_DIM)


def causal_depthwise_conv(u, w, bias):
    s = u.shape[1]
    up = jnp.pad(u, ((0, 0), (CONV_WIDTH - 1, 0), (0, 0)))
    y = bias
    for j in range(CONV_WIDTH):
        y = y + w[j] * up[:, j:j + s]
    return y


def setup_inputs(seed: int = 0) -> dict:
    key = jax.random.key(seed)
    ks = jax.random.split(key, 20)
    f32 = jnp.float32
    L = DEPTH

    def nrm(k, shape, fan_in):
        return jax.random.normal(k, shape, f32) * (fan_in ** -0.5)

    def gain(k, shape):
        return 1.0 + 0.02 * jax.random.normal(k, shape, f32)

    return {
        "x": jax.random.normal(ks[0], (BATCH, SEQ, D_MODEL), f32),
        "positions": jnp.broadcast_to(jnp.arange(SEQ, dtype=jnp.int32), (BATCH, SEQ)),
        "norm_mix_g": gain(ks[1], (L, D_MODEL)),
        "w_in": nrm(ks[2], (L, D_MODEL, IN_PROJ_W), D_MODEL),
        "idx_k_norm_g": gain(ks[3], (L, IDX_DIM)),
        "q_a_norm_g": gain(ks[4], (L, B_Q_RANK)),
        "kv_a_norm_g": gain(ks[5], (L, B_KV_RANK)),
        "w_uq": nrm(ks[6], (L, B_Q_RANK, B_HEADS, B_NOPE_DIM + B_ROPE_DIM), B_Q_RANK),
        "w_uk": nrm(ks[7], (L, B_KV_RANK, B_HEADS, B_NOPE_DIM), B_KV_RANK),
        "w_uv": nrm(ks[8], (L, B_KV_RANK, B_HEADS, B_V_DIM), B_KV_RANK),
        "w_branch_a": nrm(ks[9], (L, A_Q_W, D_MODEL), A_Q_W),
        "w_branch_b": nrm(ks[10], (L, B_OUT_W, D_MODEL), B_OUT_W),
        "w_out": nrm(ks[11], (L, D_MODEL, D_MODEL), D_MODEL),
        "norm_ffn_g": gain(ks[12], (L, D_MODEL)),
        "w_up": nrm(ks[13], (L, D_MODEL, 2 * D_FF), D_MODEL),
        "conv_w": nrm(ks[14], (L, CONV_WIDTH, 2 * D_FF), CONV_WIDTH),
        "conv_b": 0.01 * jax.random.normal(ks[15], (L, 2 * D_FF), f32),
        "w_down": nrm(ks[16], (L, D_FF, D_MODEL), D_FF),
        "norm_final_g": gain(ks[17], (D_MODEL,)),
    }


def reference(x, positions, norm_mix_g, w_in, idx_k_norm_g, q_a_norm_g, kv_a_norm_g,
              w_uq, w_uk, w_uv, w_branch_a, w_branch_b, w_out, norm_ffn_g,
              w_up, conv_w, conv_b, w_down, norm_final_g):
    b, s, _ = x.shape
    topk = min(INDEX_TOPK_MAX, s // 4)
    cos_a, sin_a = rope_tables(positions, A_ROT_DIM)
    cos_i, sin_i = rope_tables(positions, IDX_ROT_DIM)
    cos_b, sin_b = rope_tables(positions, B_ROPE_DIM)
    h = x
    for l in range(DEPTH):
        hn = rmsnorm(h, norm_mix_g[l])
        proj = jnp.einsum('bsd,de->bse', hn, w_in[l])
        (a_q, a_k, a_v, i_q, i_k, i_w, b_cq, b_ckv, b_kr, g_a, g_b) = jnp.split(proj, IN_OFFSETS, axis=-1)

        a_q = apply_rope(a_q.reshape(b, s, A_HEADS, A_HEAD_DIM), cos_a, sin_a, A_ROT_DIM)
        a_k = apply_rope(a_k.reshape(b, s, A_KV_HEADS, A_HEAD_DIM), cos_a, sin_a, A_ROT_DIM)
        a_v = a_v.reshape(b, s, A_KV_HEADS, A_HEAD_DIM)
        i_q = apply_rope(i_q.reshape(b, s, IDX_HEADS, IDX_DIM), cos_i, sin_i, IDX_ROT_DIM) * (IDX_DIM ** -0.5)
        i_k = apply_rope(rmsnorm(i_k, idx_k_norm_g[l])[:, :, None, :], cos_i, sin_i, IDX_ROT_DIM)[:, :, 0, :]
        i_w = i_w * (IDX_HEADS ** -0.5)
        o_a = indexed_sparse_attention(a_q, a_k, a_v, i_q, i_k, i_w, positions, topk)

        c_q = rmsnorm(b_cq, q_a_norm_g[l])
        q_b = jnp.einsum('bsr,rhd->bshd', c_q, w_uq[l])
        q_nope = q_b[..., :B_NOPE_DIM]
        q_rope = apply_rope(q_b[..., B_NOPE_DIM:], cos_b, sin_b, B_ROPE_DIM)
        c_kv = rmsnorm(b_ckv, kv_a_norm_g[l])
        k_nope = jnp.einsum('bsr,rhd->bshd', c_kv, w_uk[l])
        v_b = jnp.einsum('bsr,rhd->bshd', c_kv, w_uv[l])
        k_rope = apply_rope(b_kr[:, :, None, :], cos_b, sin_b, B_ROPE_DIM)[:, :, 0, :]
        o_b = latent_attention(q_nope, q_rope, k_nope, k_rope, v_b, positions)

        y_a = jnp.einsum('bsc,cd->bsd', o_a, w_branch_a[l])
        y_b = jnp.einsum('bsc,cd->bsd', o_b, w_branch_b[l])
        merged = jax.nn.sigmoid(g_a) * y_a + jax.nn.sigmoid(g_b) * y_b
        h = h + jnp.einsum('bsd,de->bse', merged, w_out[l])

        hn = rmsnorm(h, norm_ffn_g[l])
        u = jnp.einsum('bsd,df->bsf', hn, w_up[l])
        u = causal_depthwise_conv(u, conv_w[l], conv_b[l])
        gate, val = jnp.split(u, 2, axis=-1)
        h = h + jnp.einsum('bsf,fd->bsd', jax.nn.silu(gate) * val, w_down[l])
    return rmsnorm(h, norm_final_g)
```

```python
import math
from contextlib import ExitStack

import ml_dtypes
import numpy as np

import concourse.bass as bass
import concourse.mybir as mybir
from concourse.bass_utils import run_bass_kernel_spmd

F32, BF16, I32 = mybir.dt.float32, mybir.dt.bfloat16, mybir.dt.int32
AF = mybir.ActivationFunctionType
ALU = mybir.AluOpType
AX = mybir.AxisListType

D = 1024
S = 2048
G = 512
NCORES = 8
NSEQ = 2
THETA = 500000.0
EPS = 1e-6
NIT = 16
TOPK = 256
DFF = 2816
NEG = -1.0e30
TAPG = 0

CB_ID, CB_PA, CB_PB, CB_TRI, CB_ONE, CB_N = 0, 128, 256, 288, 416, 544
CF_ONE, CF_BD, CF_NTRI, CF_POW, CF_INVA, CF_INVB, CF_NPI, CF_N = 0, 128, 256, 384, 416, 417, 418, 420
SF_GMIX, SF_GFFN, SF_GIK, SF_GQ, SF_GKV, SF_CW, SF_CB, SF_N = 0, 8, 16, 17, 19, 20, 152, 196


class Tk:
    def __init__(self, t, name=""):
        self.t = t
        self.name = name
        self.w = None
        self.r = {}

    def __getitem__(self, k):
        return self.t[k]


class Eng:
    def __init__(self, name):
        self.name = name
        self.ops = []
        self.cnt = 0
        self.seen = {}


class _Rec:
    def __getattr__(self, name):
        return lambda *a, **k: (name, a, k)


class Prog:
    CE = ("pe", "act", "dve", "pool")

    def __init__(self, nc):
        self.nc = nc
        self.E = {n: Eng(n) for n in ("pe", "act", "dve", "pool", "sp")}
        self.ndma = 0
        self.dma_cnt = {}
        self.semnames = set(self.CE)

    def _deps(self, eng, reads, writes):
        deps = {}

        def add(d):
            if d is None:
                return
            k, v = d
            if deps.get(k, 0) < v:
                deps[k] = v
        for t in reads:
            add(t.w)
            if getattr(t, "psum", False):
                for d in t.r.values():
                    if d[0] != eng.name:
                        add(d)
        same_ok = eng.name == "pe"
        for t in writes:
            if t.w is not None and (t.w[0] != eng.name or not same_ok):
                add(t.w)
            for d in t.r.values():
                if d[0] != eng.name or not same_ok:
                    add(d)
        for k, v in deps.items():
            if eng.seen.get(k, 0) < v:
                if k in self.CE:
                    assert v <= self.E[k].cnt, (k, v, self.E[k].cnt)
                eng.ops.append(("wait", k, v))
                eng.seen[k] = v

    def op(self, en, fn, reads=(), writes=(), inc=True):
        eng = self.E[en]
        name_, a_, k_ = fn(_Rec())
        fn = lambda h, name_=name_, a_=a_, k_=k_: getattr(h, name_)(*a_, **k_)
        self._deps(eng, reads, writes)
        if inc:
            eng.cnt += 1
            idx = eng.cnt
            eng.ops.append(("op", fn, en, 1))
        else:
            idx = eng.cnt + 1
            eng.ops.append(("op", fn, None, 0))
        d = (en, idx)
        for t in reads:
            t.r[en] = d
        for t in writes:
            t.w = d
            t.r = {}

    def dma(self, qn, out_ap, in_ap, reads=(), writes=()):
        eng = self.E[qn]
        self._deps(eng, reads, writes)
        tgt = writes[0] if writes else reads[0]
        if not hasattr(tgt, "dsem"):
            tgt.dsem = {}
        if qn not in tgt.dsem:
            tgt.dsem[qn] = "d%d" % self.ndma
            self.ndma += 1
        sem = tgt.dsem[qn]
        self.semnames.add(sem)
        v = self.dma_cnt.get(sem, 0) + 16
        self.dma_cnt[sem] = v
        eng.ops.append(("op", lambda h, o=out_ap, i=in_ap: h.dma_start(out=o, in_=i), sem, 16))
        d = (sem, v)
        for t in reads:
            t.r[sem] = d
        for t in writes:
            t.w = d
            t.r = {}
        return d

    def barrier(self, extra=()):
        for a in self.CE:
            ea = self.E[a]
            for t in extra:
                for d in list(t.r.values()) + ([t.w] if t.w else []):
                    if d[0] not in self.CE and ea.seen.get(d[0], 0) < d[1]:
                        ea.ops.append(("wait", d[0], d[1]))
                        ea.seen[d[0]] = d[1]
            for b in self.CE:
                if a == b:
                    continue
                v = self.E[b].cnt
                if v > 0 and ea.seen.get(b, 0) < v:
                    ea.ops.append(("wait", b, v))
                    ea.seen[b] = v

    def wait_all(self, en, deps):
        eng = self.E[en]
        for k, v in deps:
            if eng.seen.get(k, 0) < v:
                eng.ops.append(("wait", k, v))
                eng.seen[k] = v

    def replay(self, es):
        nc = self.nc
        sems = {}
        for n in sorted(self.semnames):
            sems[n] = es.enter_context(nc.semaphore("s_" + n))
        block = es.enter_context(nc.Block())

        def run(eng):
            def f(h):
                for o in eng.ops:
                    if o[0] == "wait":
                        h.wait_ge(sems[o[1]], o[2])
                    else:
                        ins = o[1](h)
                        if o[2] is not None:
                            ins.then_inc(sems[o[2]], o[3])
            return f
        block.tensor(run(self.E["pe"]))
        block.scalar(run(self.E["act"]))
        block.vector(run(self.E["dve"]))
        block.gpsimd(run(self.E["pool"]))
        block.sync(run(self.E["sp"]))


class Ring:
    def __init__(self, items):
        self.items = items
        self.i = 0

    def next(self):
        t = self.items[self.i % len(self.items)]
        self.i += 1
        return t


class _Stop(Exception):
    pass


def build_program(nseq=NSEQ, ngroups=S // G, taps=None, stage=99):
    nc = bass.Bass("TRN2", target_bir_lowering=False)
    SL = ngroups * G

    def din(name, shape, dt=F32):
        return nc.dram_tensor(name, list(shape), dt, kind="ExternalInput").ap()

    x_d = din("x", [nseq, SL, D])
    pos_d = din("pos", [nseq, SL], I32)
    constb_d = din("constb", [128, CB_N])
    constf_d = din("constf", [128, CF_N])
    smallf_d = din("smallf", [128, SF_N])
    gfin_d = din("gfin", [1, D])
    win_d = din("win", [30, 128, 8 * 128])
    wmisc_d = din("wmisc", [128, 8 * 168])
    wuq_d = din("wuq", [128, 2 * 8 * 96])
    wukT_d = din("wukT", [64, 8 * 128])
    wuv_d = din("wuv", [128, 512])
    wba_d = din("wba", [8, 64, 8 * 128])
    wbb_d = din("wbb", [8, 64, 8 * 128])
    wo_d = din("wo", [4, 128, 8 * 256])
    wup_d = din("wup", [22, 128, 8 * 256])
    wd_d = din("wd", [8, 128, 11 * 256])
    out_d = nc.dram_tensor("out", [nseq, SL, D], F32, kind="ExternalOutput").ap()
    tap_d = {}
    if taps:
        for k, shp in taps.items():
            tap_d[k] = nc.dram_tensor("tap_" + k, list(shp), F32, kind="ExternalOutput").ap()

    es = ExitStack()
    with es:
        P = Prog(nc)

        def SB(name, shape, dt):
            return Tk(es.enter_context(nc.sbuf_tensor("sb_" + name, list(shape), dt)), name)

        def PSM(name, shape, dt):
            t = Tk(es.enter_context(nc.psum_tensor(name, list(shape), dt)), name)
            t.psum = True
            return t

        constb = SB("constb", [128, CB_N], BF16)
        constf = SB("constf", [128, CF_N], F32)
        smallf = SB("smallf", [128, SF_N], F32)
        gfin = SB("gfin", [128, D], F32)
        wmisc = SB("wmisc", [128, 8, 168], BF16)
        wuq = SB("wuq", [128, 2, 8, 96], BF16)
        wukT = SB("wukT", [64, 8, 128], BF16)
        wuv = SB("wuv", [128, 512], BF16)
        AKT = SB("AKT", [128, 2, S], BF16)
        IKT = SB("IKT", [128, S], BF16)
        AV = SB("AV", [128, 16, 2, 65], BF16)
        CKVS = SB("CKVS", [128, S], BF16)
        KRT = SB("KRT", [32, S], BF16)
        VB = SB("VB", [128, 16, 8, 65], BF16)
        IW = SB("IW", [128, 16, 8], F32)
        CARRY = SB("CARRY", [128, 44, 2], F32)
        XH = SB("XH", [128, 4, D], F32)
        XHr = [Tk(XH.t, "XH%d" % i) for i in range(4)]
        hnT = SB("hnT", [128, 8, G], BF16)
        OAT = SB("OAT", [64, 8, G], BF16)
        OBT = SB("OBT", [64, 8, G], BF16)
        stats = SB("stats", [128, 16], F32)
        wsmall = Ring([SB("wsm%d" % i, [128, 1024], BF16) for i in range(3)])
        wmid = Ring([SB("wmid%d" % i, [128, 2816], BF16) for i in range(3)])
        SCRN = 20864
        scr = es.enter_context(nc.sbuf_tensor("scr", [128, SCRN], F32))
        off = [0]
        hiw = [0]

        def carve(name, nwords_unused, dt, shape):
            nel = int(np.prod(shape[1:]))
            nwords = (nel + 1) // 2 if dt == BF16 else nel
            a = off[0]
            off[0] += nwords
            assert off[0] <= SCRN, (name, off[0])
            hiw[0] = max(hiw[0], off[0])
            v = scr[:, a:a + nwords]
            if dt == BF16:
                v = v.bitcast(BF16)
            return Tk(_View(v, shape), name)

        class _View:
            def __init__(self, ap, shape):
                self.ap = ap
                if len(shape) == 3:
                    self.ap = ap.rearrange("p (a b) -> p a b", a=shape[1])
                elif len(shape) == 4:
                    self.ap = ap.rearrange("p (a b c) -> p a b c", a=shape[1], b=shape[2])

            def __getitem__(self, k):
                return self.ap[k]

        off[0] = 0
        AQT = carve("AQT", 2048, BF16, [128, 4, G])
        IQT = carve("IQT", 2048, BF16, [128, 4, G])
        MT = carve("MT", 4096, BF16, [128, 16, G])
        ACC = carve("ACC", 2048, F32, [128, S])
        MQ = carve("MQ", 1024, BF16, [128, S])
        PTr = Ring([carve("PT%d" % i, 256, BF16, [128, G]) for i in range(3)])
        CA = carve("CA", 512, F32, [128, G])
        SA = carve("SA", 512, F32, [128, G])
        CBt = carve("CBt", 512, F32, [128, G])
        SBt = carve("SBt", 512, F32, [128, G])
        RXB = carve("RXB", 256, BF16, [128, G])
        RT1 = carve("RT1", 512, F32, [128, G])
        RT2 = carve("RT2", 512, F32, [128, G])
        xf_off = [off[0]]
        XF = carve("XF", 1024, F32, [128, 2, G])
        SQ = carve("SQ", 512, F32, [128, G])
        RSTD = carve("RSTD", 512, F32, [128, G])
        rstd_end = [off[0]]
        CQT = carve("CQT", 512, BF16, [128, 2, G])
        RLr = Ring([carve("RL%d" % i, 512, F32, [128, G]) for i in range(2)])
        QNT = carve("QNT", 256, BF16, [128, G])
        QABr = Ring([carve("QAB%d" % i, 256, BF16, [128, G]) for i in range(2)])
        QRr = Ring([carve("QR%d" % i, 256, BF16, [128, G]) for i in range(2)])
        OSB = carve("OSB", 512, F32, [128, G])
        REC = carve("REC", 512, F32, [128, G])
        NRB = carve("NRB", 256, BF16, [128, G])
        RXB2 = carve("RXB2", 256, BF16, [128, G])
        QRF = carve("QRF", 512, F32, [128, G])
        POSF, TT1, TT2 = SQ, RT1, RT2
        HNTOK = carve("HNTOK", 512, BF16, [128, D])
        BIS = carve("BIS", 64, F32, [128, 64])
        BIS2 = carve("BIS2", 64, F32, [128, 64])
        p1_end = off[0]
        off[0] = xf_off[0]
        ACC2 = carve("ACC2", 2048, F32, [128, S])
        assert off[0] == rstd_end[0], (off[0], rstd_end[0])
        off[0] = p1_end
        off[0] = 0
        MRG = carve("MRG", 2048, BF16, [128, 8, G])
        hn2T = carve("hn2T", 2048, BF16, [128, 8, G])
        ACT_ = carve("ACTT", 2816, BF16, [128, 11, G])
        UGr = Ring([carve("UG%d" % i, 516, F32, [128, 516]) for i in range(2)])
        UVr = Ring([carve("UV%d" % i, 516, F32, [128, 516]) for i in range(2)])
        AGr = Ring([carve("AG%d" % i, 512, F32, [128, G]) for i in range(2)])
        AVr = Ring([carve("AVV%d" % i, 512, F32, [128, G]) for i in range(2)])
        SGr = Ring([carve("SG%d" % i, 512, F32, [128, G]) for i in range(2)])
        GT1 = carve("GT1", 512, F32, [128, G])
        GT2 = carve("GT2", 512, F32, [128, G])
        SGA = carve("SGA", 256, BF16, [128, G])
        SGB = carve("SGB", 256, BF16, [128, G])
        HN2TOK = carve("HN2TOK", 512, BF16, [128, D])
        ONORM = carve("ONORM", 1024, F32, [128, D])
        ONORM2 = carve("ONORM2", 1024, F32, [128, D])
        ONr = [ONORM, ONORM2]

        _banks = [PSM("psf%d" % i, [128, 512], F32) for i in range(4)]
        psf = Ring(list(_banks))
        psq = Ring(_banks[0:2])
        pso = Ring([PSM("pso%d" % i, [128, 512], F32) for i in range(2)])
        psb = Ring([PSM("psb%d" % i, [128, 1024], BF16) for i in range(2)])

        posi = SB("posi", [128, G], I32)
        tint = SB("tint", [128, G], I32)

        ident = constb[:, CB_ID:CB_ID + 128]

        P.dma("sp", constf[:], constf_d[:, :], writes=[constf])
        P.dma("sp", smallf[:], smallf_d[:, :], writes=[smallf])
        P.dma("sp", gfin[:], gfin_d[0:1, :].partition_broadcast(128), writes=[gfin])
        P.dma("pool", constb[:], constb_d[:, :], writes=[constb])
        P.dma("pool", wmisc[:].rearrange("p a b -> p (a b)"), wmisc_d[:, :], writes=[wmisc])
        P.dma("pool", wuq[:].rearrange("p a b c -> p (a b c)"), wuq_d[:, :], writes=[wuq])
        P.dma("pool", wukT[:].rearrange("p a b -> p (a b)"), wukT_d[:, :], writes=[wukT])
        P.dma("pool", wuv[:], wuv_d[:, :], writes=[wuv])
        for tix in range(16):
            for hh in range(2):
                P.op("dve", lambda h, tix=tix, hh=hh: h.memset(AV[:, tix, hh, 64:65], 1.0), writes=[AV])
            for hh in range(8):
                P.op("dve", lambda h, tix=tix, hh=hh: h.memset(VB[:, tix, hh, 64:65], 1.0), writes=[VB])

        P.op("dve", lambda h: h.memset(stats[:, :], 0.0), writes=[stats])

        def mm(ps_t, out_ap, lhsT_ap, rhs_ap, reads, start=True, stop=True):
            P.op("pe", lambda h: h.matmul(out_ap, lhsT_ap, rhs_ap, start=start, stop=stop),
                 reads=reads, writes=[ps_t], inc=stop)

        wsrc = {"win": win_d, "wba": wba_d, "wbb": wbb_d, "wo": wo_d, "wup": wup_d, "wd": wd_d}
        wscr = {}
        pending_conv = []
        for nm_ in ("win", "wba", "wbb", "wo", "wup", "wd"):
            shp = list(wsrc[nm_].shape)
            scr_ap = nc.dram_tensor(nm_ + "_bf", shp, BF16).ap()
            wscr[nm_] = (scr_ap, Tk(None, nm_ + "_bf"))
            for i_ in range(shp[0]):
                pending_conv.append((nm_, i_))
        first_group = [True]

        def conv_one():
            if pending_conv:
                nm_, i_ = pending_conv.pop(0)
                scr_ap, tk_ = wscr[nm_]
                P.dma("pool", scr_ap[i_, :, :], wsrc[nm_][i_, :, :], writes=[tk_])

        def load_w(ring, view_fn, nm_, i_):
            w = ring.next()
            if first_group[0]:
                P.dma("pool", view_fn(w), wsrc[nm_][i_, :, :], writes=[w])
                conv_one()
            else:
                scr_ap, tk_ = wscr[nm_]
                P.dma("sp", view_fn(w), scr_ap[i_, :, :], reads=[tk_], writes=[w])
            return w

        def rmsnorm_tok(src_t, src_ap, gcol, dstT, r, tokbuf):
            st = stats
            P.op("act", lambda h: h.activation(out=tokbuf[:, :], in_=src_ap, func=AF.Square,
                                               accum_out=st[:, 0:1]),
                 reads=[src_t], writes=[tokbuf, st])
            P.op("act", lambda h: h.activation(out=st[:, 8:9], in_=st[:, 9:10], func=AF.Copy), reads=[], writes=[st])
            P.op("act", lambda h: h.activation(out=st[:, 2:3], in_=st[:, 0:1], func=AF.Sqrt, scale=1.0 / D,
                                               bias=constf[:, CF_POW + 30:CF_POW + 31]), reads=[st, constf], writes=[st])
            P.op("dve", lambda h: h.reciprocal(out=st[:, 2:3], in_=st[:, 2:3]), reads=[st], writes=[st])
            P.op("dve", lambda h: h.tensor_scalar(out=tokbuf[:, :], in0=src_ap, scalar1=st[:, 2:3], scalar2=None,
                                                  op0=ALU.mult), reads=[src_t, st], writes=[tokbuf])
            pb = psb.next()
            for kc in range(8):
                P.op("pe", lambda h, kc=kc: h.transpose(out=pb[:, kc * 128:(kc + 1) * 128],
                                                        in_=tokbuf[:, kc * 128:(kc + 1) * 128], identity=ident),
                     reads=[tokbuf, constb], writes=[pb], inc=(kc == 7))
            for kc in range(8):
                eng = "act" if kc % 2 == 0 else "dve"
                if eng == "act":
                    P.op("act", lambda h, kc=kc: h.activation(out=dstT[:, kc, r * 128:(r + 1) * 128],
                                                              in_=pb[:, kc * 128:(kc + 1) * 128], func=AF.Copy,
                                                              scale=smallf[:, gcol + kc:gcol + kc + 1]),
                         reads=[pb, smallf], writes=[dstT])
                else:
                    P.op("dve", lambda h, kc=kc: h.tensor_scalar(out=dstT[:, kc, r * 128:(r + 1) * 128],
                                                                 in0=pb[:, kc * 128:(kc + 1) * 128],
                                                                 scalar1=smallf[:, gcol + kc:gcol + kc + 1], scalar2=None,
                                                                 op0=ALU.mult),
                         reads=[pb, smallf], writes=[dstT])

        def rope(src_t, src_ap, np_, Pm_ap, Ct, St, dst_t, dst_ap):
            P.op("act", lambda h: h.activation(out=RXB[0:np_, :], in_=src_ap, func=AF.Copy),
                 reads=[src_t], writes=[RXB])
            ckpt(1.011)
            ps2 = psf.next()
            mm(ps2, ps2[0:np_, :], Pm_ap, RXB[0:np_, :], [constb, RXB])
            ckpt(1.012)
            P.op("dve", lambda h: h.tensor_tensor(out=RT1[0:np_, :], in0=src_ap, in1=Ct[0:np_, :], op=ALU.mult),
                 reads=[src_t, Ct, RXB], writes=[RT1])
            ckpt(1.013)
            P.op("dve", lambda h: h.tensor_tensor(out=RT2[0:np_, :], in0=ps2[0:np_, :], in1=St[0:np_, :], op=ALU.mult),
                 reads=[ps2, St], writes=[RT2])
            ckpt(1.014)
            P.op("dve", lambda h: h.tensor_tensor(out=dst_ap, in0=RT1[0:np_, :], in1=RT2[0:np_, :], op=ALU.add),
                 reads=[RT1, RT2], writes=[dst_t])

        def rope_a(src_t, src_ap, np_, rxb):
            P.op("act", lambda h: h.activation(out=rxb[0:np_, :], in_=src_ap, func=AF.Copy), reads=[src_t], writes=[rxb])

        def rope_b(src_t, src_ap, np_, Pm_ap, Ct, St, dst_t, dst_ap, rxb):
            ps2 = psf.next()
            mm(ps2, ps2[0:np_, :], Pm_ap, rxb[0:np_, :], [constb, rxb])
            P.op("dve", lambda h: h.tensor_tensor(out=RT1[0:np_, :], in0=src_ap, in1=Ct[0:np_, :], op=ALU.mult),
                 reads=[src_t, Ct, rxb], writes=[RT1])
            P.op("dve", lambda h: h.tensor_tensor(out=RT2[0:np_, :], in0=ps2[0:np_, :], in1=St[0:np_, :], op=ALU.mult),
                 reads=[ps2, St], writes=[RT2])
            P.op("dve", lambda h: h.tensor_tensor(out=dst_ap, in0=RT1[0:np_, :], in1=RT2[0:np_, :], op=ALU.add),
                 reads=[RT1, RT2], writes=[dst_t])


        def trig_table(dst, invcol, offs):
            P.op("dve", lambda h: h.tensor_scalar(out=TT1[:, :], in0=POSF[:, :], scalar1=constf[:, invcol:invcol + 1],
                                                  scalar2=offs, op0=ALU.mult, op1=ALU.add),
                 reads=[POSF, constf], writes=[TT1])
            P.op("dve", lambda h: h.tensor_copy(out=tint[:], in_=TT1[:, :]), reads=[TT1], writes=[tint])
            P.op("dve", lambda h: h.tensor_tensor(out=TT2[:, :], in0=TT1[:, :], in1=tint[:], op=ALU.subtract),
                 reads=[TT1, tint], writes=[TT2])
            P.op("dve", lambda h: h.scalar_tensor_tensor(out=TT1[:, :], in0=TT2[:, :], scalar=0.0, in1=TT2[:, :],
                                                         op0=ALU.is_lt, op1=ALU.add), reads=[TT2], writes=[TT1])
            P.op("act", lambda h: h.activation(out=dst[:, :], in_=TT1[:, :], func=AF.Sin,
                                               bias=constf[:, CF_NPI:CF_NPI + 1], scale=2.0 * math.pi * (1.0 - 2e-6)),
                 reads=[TT1, constf], writes=[dst])

        def normalize_heads(po, dstT, h_):
            P.op("act", lambda h: h.activation(out=OSB[64:65, :], in_=po[64:65, :], func=AF.Ln), reads=[po], writes=[OSB])
            P.op("act", lambda h: h.activation(out=NRB[64:65, :], in_=OSB[64:65, :], func=AF.Exp, scale=-1.0),
                 reads=[OSB], writes=[NRB])
            pd = psf.next()
            mm(pd, pd[0:64, :], constb[64:65, CB_ONE:CB_ONE + 64], NRB[64:65, :], [constb, NRB])
            P.op("act", lambda h: h.activation(out=REC[0:64, :], in_=pd[0:64, :], func=AF.Copy), reads=[pd], writes=[REC])
            P.op("dve", lambda h: h.tensor_tensor(out=dstT[0:64, h_, :], in0=po[0:64, :], in1=REC[0:64, :], op=ALU.mult),
                 reads=[po, REC], writes=[dstT])


        out_deps = []

        def ckpt(n):
            if stage <= n:
                raise _Stop()
        try:
          for b in range(nseq):
              P.op("dve", lambda h: h.memset(CARRY[:, :, :], 0.0), writes=[CARRY])
              for g in range(ngroups):
                  t0 = g * G
                  P.barrier(extra=[ONORM, ONORM2])
                  psf.items = list(_banks)
                  for r in range(4):
                      P.dma("sp", XH[:, r, :], x_d[b, t0 + r * 128:t0 + (r + 1) * 128, :], writes=[XHr[r]])
                  P.dma("sp", posi[:], pos_d[b:b + 1, t0:t0 + G].partition_broadcast(128), writes=[posi])
                  P.op("dve", lambda h: h.tensor_copy(out=POSF[:, :], in_=posi[:]), reads=[posi], writes=[POSF])
                  for r in range(4):
                      rmsnorm_tok(XHr[r], XH[:, r, :], SF_GMIX, hnT, r, HNTOK)
                  trig_table(SA, CF_INVA, 0.5)
                  trig_table(CA, CF_INVA, 0.75)
                  trig_table(SBt, CF_INVB, 0.5)
                  trig_table(CBt, CF_INVB, 0.75)

                  ckpt(1)

                  def proj_fm(piece, M=128):
                      w = load_w(wsmall, lambda w: w[:, :], "win", piece)
                      ps = psf.next()
                      for kc in range(8):
                          mm(ps, ps[0:M, :], w[:, kc * 128:kc * 128 + M], hnT[:, kc, :], [w, hnT],
                             start=(kc == 0), stop=(kc == 7))
                      return ps

                  PA_ap = constb[:, CB_PA:CB_PA + 128]
                  PB_ap = constb[0:32, CB_PB:CB_PB + 32]
                  ckpt(1.1)
                  for c in range(4):
                      ps = proj_fm(6 + c)
                      rope(ps, ps[:, :], 128, PA_ap, CA, SA, IQT, IQT[:, c, :])
                  ckpt(1.2)
                  ps = proj_fm(10)
                  P.op("act", lambda h, ps=ps: h.activation(out=SQ[:, :], in_=ps[:, :], func=AF.Square), reads=[ps], writes=[SQ])
                  P.op("act", lambda h, ps=ps: h.activation(out=XF[:, 0, :], in_=ps[:, :], func=AF.Copy), reads=[ps], writes=[XF])
                  pn = psf.next()
                  mm(pn, pn[:, :], constf[:, CF_BD:CF_BD + 128], SQ[:, :], [constf, SQ])
                  P.op("act", lambda h, pn=pn: h.activation(out=RSTD[:, :], in_=pn[:, :], func=AF.Sqrt, scale=1.0 / 64,
                                                            bias=constf[:, CF_POW + 30:CF_POW + 31]), reads=[pn, constf], writes=[RSTD])
                  P.op("dve", lambda h: h.reciprocal(out=RSTD[:, :], in_=RSTD[:, :]), reads=[RSTD], writes=[RSTD])
                  P.op("dve", lambda h: h.scalar_tensor_tensor(out=XF[:, 1, :], in0=XF[:, 0, :], scalar=smallf[:, SF_GIK:SF_GIK + 1],
                                                               in1=RSTD[:, :], op0=ALU.mult, op1=ALU.mult),
                       reads=[XF, smallf, RSTD], writes=[XF])
                  rope(XF, XF[:, 1, :], 128, PA_ap, CA, SA, IKT, IKT[:, t0:t0 + G])
                  ckpt(1.3)
                  pn = pso.next()
                  for c in range(2):
                      ps = proj_fm(11 + c)
                      P.op("act", lambda h, ps=ps: h.activation(out=SQ[:, :], in_=ps[:, :], func=AF.Square), reads=[ps], writes=[SQ])
                      P.op("act", lambda h, ps=ps, c=c: h.activation(out=XF[:, c, :], in_=ps[:, :], func=AF.Copy), reads=[ps], writes=[XF])
                      mm(pn, pn[:, :], constf[:, CF_ONE:CF_ONE + 128], SQ[:, :], [constf, SQ], start=(c == 0), stop=(c == 1))
                  P.op("act", lambda h, pn=pn: h.activation(out=RSTD[:, :], in_=pn[:, :], func=AF.Sqrt, scale=1.0 / 256,
                                                            bias=constf[:, CF_POW + 30:CF_POW + 31]), reads=[pn, constf], writes=[RSTD])
                  P.op("dve", lambda h: h.reciprocal(out=RSTD[:, :], in_=RSTD[:, :]), reads=[RSTD], writes=[RSTD])
                  for c in range(2):
                      P.op("dve", lambda h, c=c: h.scalar_tensor_tensor(out=CQT[:, c, :], in0=XF[:, c, :],
                                                                        scalar=smallf[:, SF_GQ + c:SF_GQ + c + 1],
                                                                        in1=RSTD[:, :], op0=ALU.mult, op1=ALU.mult),
                           reads=[XF, smallf, RSTD], writes=[CQT])
                  ckpt(1.4)
                  ps = proj_fm(13)
                  P.op("act", lambda h, ps=ps: h.activation(out=SQ[:, :], in_=ps[:, :], func=AF.Square), reads=[ps], writes=[SQ])
                  P.op("act", lambda h, ps=ps: h.activation(out=XF[:, 0, :], in_=ps[:, :], func=AF.Copy), reads=[ps], writes=[XF])
                  pn = psf.next()
                  mm(pn, pn[:, :], constf[:, CF_ONE:CF_ONE + 128], SQ[:, :], [constf, SQ])
                  P.op("act", lambda h, pn=pn: h.activation(out=RSTD[:, :], in_=pn[:, :], func=AF.Sqrt, scale=1.0 / 128,
                                                            bias=constf[:, CF_POW + 30:CF_POW + 31]), reads=[pn, constf], writes=[RSTD])
                  P.op("dve", lambda h: h.reciprocal(out=RSTD[:, :], in_=RSTD[:, :]), reads=[RSTD], writes=[RSTD])
                  P.op("dve", lambda h: h.scalar_tensor_tensor(out=CKVS[:, t0:t0 + G], in0=XF[:, 0, :], scalar=smallf[:, SF_GKV:SF_GKV + 1],
                                                               in1=RSTD[:, :], op0=ALU.mult, op1=ALU.mult),
                       reads=[XF, smallf, RSTD], writes=[CKVS])
                  ckpt(1.5)
                  ps = psf.next()
                  for kc in range(8):
                      mm(ps, ps[0:32, :], wmisc[:, kc, 0:32], hnT[:, kc, :], [wmisc, hnT], start=(kc == 0), stop=(kc == 7))
                  rope(ps, ps[0:32, :], 32, PB_ap, CBt, SBt, KRT, KRT[0:32, t0:t0 + G])
                  ckpt(1.6)
                  for r in range(4):
                      tix = g * 4 + r
                      ps = psf.next()
                      for kc in range(8):
                          mm(ps, ps[:, 0:136], hnT[:, kc, r * 128:(r + 1) * 128], wmisc[:, kc, 32:168], [wmisc, hnT],
                             start=(kc == 0), stop=(kc == 7))
                      P.op("act", lambda h, ps=ps, tix=tix: h.activation(
                          out=AV[:, tix, :, 0:64], in_=ps[:, 0:128].rearrange("p (a b) -> p a b", a=2), func=AF.Copy),
                          reads=[ps], writes=[AV])
                      P.op("dve", lambda h, ps=ps, tix=tix: h.tensor_scalar(
                          out=IW[:, tix, :], in0=ps[:, 128:136], scalar1=0.125 * 8 ** -0.5, scalar2=None, op0=ALU.mult),
                          reads=[ps], writes=[IW])
                      ps2 = psf.next()
                      mm(ps2, ps2[:, :], CKVS[:, t0 + r * 128:t0 + (r + 1) * 128], wuv[:, :], [CKVS, wuv])
                      P.op("act", lambda h, ps2=ps2, tix=tix: h.activation(
                          out=VB[:, tix, :, 0:64], in_=ps2[:, :].rearrange("p (a b) -> p a b", a=8), func=AF.Copy),
                          reads=[ps2], writes=[VB])

                  ckpt(2)
                  nkt_g = 4 * g + 4

                  def gen_scores(r, ACCx):
                      qt = 4 * g + r
                      nk = 128 * (qt + 1)
                      for kb in range(0, nk, 512):
                          kw = min(512, nk - kb)
                          for hh in range(8):
                              c = hh // 2
                              pb_ = 64 * (hh % 2)
                              ps = psf.next()
                              mm(ps, ps[:, 0:kw], IQT[pb_:pb_ + 64, c, r * 128:(r + 1) * 128], IKT[pb_:pb_ + 64, kb:kb + kw],
                                 [IQT, IKT])
                              rl = RLr.next()
                              P.op("act", lambda h: h.activation(out=rl[:, 0:kw], in_=ps[:, 0:kw], func=AF.Relu),
                                   reads=[ps], writes=[rl])
                              if hh == 0:
                                  P.op("dve", lambda h: h.tensor_scalar(
                                      out=ACCx[:, kb:kb + kw], in0=rl[:, 0:kw], scalar1=IW[:, qt, hh:hh + 1], scalar2=None,
                                      op0=ALU.mult), reads=[rl, IW], writes=[ACCx])
                              else:
                                  P.op("dve", lambda h: h.scalar_tensor_tensor(
                                      out=ACCx[:, kb:kb + kw], in0=rl[:, 0:kw], scalar=IW[:, qt, hh:hh + 1],
                                      in1=ACCx[:, kb:kb + kw], op0=ALU.mult, op1=ALU.add), reads=[rl, IW, ACCx], writes=[ACCx])
                              yield 0.45

                  def gen_bisect(r, ACCx, BISx):
                      qt = 4 * g + r
                      nk = 128 * (qt + 1)
                      P.op("dve", lambda h: h.tensor_reduce(out=BISx[:, 0:1], in_=ACCx[:, 0:nk], axis=AX.X, op=ALU.max),
                           reads=[ACCx], writes=[BISx])
                      P.op("dve", lambda h: h.tensor_reduce(out=BISx[:, 1:2], in_=ACCx[:, 0:nk], axis=AX.X, op=ALU.min),
                           reads=[ACCx], writes=[BISx])
                      P.op("dve", lambda h: h.tensor_tensor(out=ACCx[:, nk - 128:nk], in0=ACCx[:, nk - 128:nk],
                                                            in1=constf[:, CF_NTRI:CF_NTRI + 128], op=ALU.add),
                           reads=[ACCx, constf], writes=[ACCx])
                      P.op("dve", lambda h: h.tensor_tensor(out=BISx[:, 2:3], in0=BISx[:, 0:1], in1=BISx[:, 1:2], op=ALU.subtract),
                           reads=[BISx], writes=[BISx])
                      P.op("dve", lambda h: h.scalar_tensor_tensor(out=BISx[:, 3:4], in0=BISx[:, 2:3], scalar=0.5, in1=BISx[:, 1:2],
                                                                   op0=ALU.mult, op1=ALU.add), reads=[BISx], writes=[BISx])
                      P.op("dve", lambda h: h.tensor_scalar(out=BISx[:, 8:8 + NIT], in0=constf[:, CF_POW:CF_POW + NIT],
                                                            scalar1=BISx[:, 2:3], scalar2=None, op0=ALU.mult),
                           reads=[BISx, constf], writes=[BISx])
                      P.op("dve", lambda h: h.tensor_scalar(out=BISx[:, 32:32 + NIT], in0=constf[:, CF_POW:CF_POW + NIT],
                                                            scalar1=BISx[:, 2:3], scalar2=-0.5, op0=ALU.mult, op1=ALU.mult),
                           reads=[BISx, constf], writes=[BISx])
                      for it in range(NIT):
                          yield 1.1
                          P.op("dve", lambda h: h.tensor_scalar(out=MQ[:, 0:nk], in0=ACCx[:, 0:nk], scalar1=BISx[:, 3:4],
                                                                scalar2=None, op0=ALU.is_ge, op1=ALU.add,
                                                                accum_out=BISx[:, 4:5]),
                               reads=[ACCx, BISx], writes=[MQ, BISx])
                          P.op("dve", lambda h: h.tensor_copy(out=BISx[:, 6:7], in_=BISx[:, 3:4]), reads=[], writes=[BISx])
                          P.op("dve", lambda h: h.tensor_scalar(out=BISx[:, 5:6], in0=BISx[:, 4:5], scalar1=float(TOPK) - 0.5,
                                                                scalar2=BISx[:, 8 + it:9 + it], op0=ALU.is_ge, op1=ALU.mult),
                               reads=[BISx], writes=[BISx])
                          if it < NIT - 1:
                              P.op("dve", lambda h: h.scalar_tensor_tensor(out=BISx[:, 3:4], in0=BISx[:, 5:6],
                                                                           scalar=BISx[:, 32 + it:33 + it], in1=BISx[:, 3:4],
                                                                           op0=ALU.add, op1=ALU.add),
                                   reads=[BISx], writes=[BISx])
                          else:
                              P.op("dve", lambda h: h.scalar_tensor_tensor(out=BISx[:, 3:4], in0=BISx[:, 5:6],
                                                                           scalar=BISx[:, 8 + it:9 + it], in1=BISx[:, 3:4],
                                                                           op0=ALU.subtract, op1=ALU.add),
                                   reads=[BISx], writes=[BISx])
                      P.op("dve", lambda h: h.tensor_scalar(out=MQ[:, 0:nk], in0=ACCx[:, 0:nk], scalar1=BISx[:, 3:4],
                                                            scalar2=None, op0=ALU.is_ge), reads=[ACCx, BISx], writes=[MQ])
                      for j0 in range(0, qt + 1, 8):
                          jn = min(8, qt + 1 - j0)
                          pb = psb.next()
                          for jj in range(jn):
                              j = j0 + jj
                              P.op("pe", lambda h: h.transpose(out=pb[:, jj * 128:(jj + 1) * 128],
                                                               in_=MQ[:, j * 128:(j + 1) * 128], identity=ident),
                                   reads=[MQ, constb], writes=[pb], inc=(jj == jn - 1))
                          P.op("act", lambda h: h.activation(
                              out=MT[:, j0:j0 + jn, r * 128:(r + 1) * 128],
                              in_=pb[:, 0:jn * 128].rearrange("p (a b) -> p a b", a=jn), func=AF.Copy),
                              reads=[pb], writes=[MT])
                      yield 0.5

                  def gen_indexer():
                      tiles = []
                      for r in range(4):
                          qt = 4 * g + r
                          if qt < 2:
                              for j in range(qt + 1):
                                  src = constb[:, CB_TRI:CB_TRI + 128] if j == qt else constb[:, CB_ONE:CB_ONE + 128]
                                  P.op("act", lambda h: h.activation(out=MT[:, j, r * 128:(r + 1) * 128], in_=src, func=AF.Copy),
                                       reads=[constb], writes=[MT])
                          else:
                              tiles.append(r)
                      for p0 in range(0, len(tiles), 2):
                          pair = [(r, (ACC, ACC2)[k], (BIS, BIS2)[k]) for k, r in enumerate(tiles[p0:p0 + 2])]
                          for (r, A_, B_) in pair:
                              for cst in gen_scores(r, A_):
                                  yield cst
                          sub = [[gen_bisect(*pr), 0.0, True] for pr in pair]
                          while any(x[2] for x in sub):
                              x = min([y for y in sub if y[2]], key=lambda q: q[1])
                              try:
                                  cst = next(x[0])
                                  x[1] += cst
                                  yield cst
                              except StopIteration:
                                  x[2] = False
                      yield 0.0

                  def gen_mla():
                      SK = 1
                      st = {}

                      def proj_stage(hh, stage):
                          if stage == 0:
                              ps = psf.next()
                              for rc in range(2):
                                  mm(ps, ps[0:64, :], wuq[:, rc, hh, 0:64], CQT[:, rc, :], [wuq, CQT], start=(rc == 0), stop=(rc == 1))
                              P.op("act", lambda h, ps=ps: h.activation(out=QNT[0:64, :], in_=ps[0:64, :], func=AF.Copy), reads=[ps], writes=[QNT])
                              psr = psf.next()
                              for rc in range(2):
                                  mm(psr, psr[0:32, :], wuq[:, rc, hh, 64:96], CQT[:, rc, :], [wuq, CQT], start=(rc == 0), stop=(rc == 1))
                              rope_a(psr, psr[0:32, :], 32, RXB2)
                              P.op("act", lambda h, psr=psr: h.activation(out=QRF[0:32, :], in_=psr[0:32, :], func=AF.Copy), reads=[psr], writes=[QRF])
                              st[hh] = {}
                          elif stage == 1:
                              ps = psf.next()
                              mm(ps, ps[:, :], wukT[0:64, hh, :], QNT[0:64, :], [wukT, QNT])
                              qab = QABr.next()
                              P.op("act", lambda h, ps=ps, qab=qab: h.activation(out=qab[:, :], in_=ps[:, :], func=AF.Copy), reads=[ps], writes=[qab])
                              qr = QRr.next()
                              rope_b(QRF, QRF[0:32, :], 32, PB_ap, CBt, SBt, qr, qr[0:32, :], RXB2)
                              st[hh].update(qab=qab, qr=qr)

                      proj_stage(0, 0)
                      proj_stage(0, 1)
                      pending = None
                      for hh in range(8):
                          qab, qr = st[hh]["qab"], st[hh]["qr"]
                          po = pso.next()
                          q = []
                          for jx in range(nkt_g + SK):
                              if jx < nkt_g:
                                  j = jx
                                  c0 = 128 * max(0, j - 4 * g)
                                  ps = psq.next()
                                  mm(ps, ps[:, c0:G], CKVS[:, j * 128:(j + 1) * 128], qab[:, c0:G], [CKVS, qab], start=True, stop=False)
                                  mm(ps, ps[:, c0:G], KRT[0:32, j * 128:(j + 1) * 128], qr[0:32, c0:G], [KRT, qr], start=False, stop=True)
                                  q.append((j, ps, c0))
                              if jx >= SK:
                                  j, ps, c0 = q.pop(0)
                                  pt = PTr.next()
                                  P.op("act", lambda h, ps=ps, pt=pt, c0=c0: h.activation(out=pt[:, c0:G], in_=ps[:, c0:G], func=AF.Exp,
                                                                                          scale=96 ** -0.5), reads=[ps], writes=[pt])
                                  if j >= 4 * g:
                                      P.op("dve", lambda h, pt=pt, c0=c0: h.tensor_tensor(out=pt[:, c0:c0 + 128], in0=pt[:, c0:c0 + 128],
                                                                                          in1=constb[:, CB_TRI:CB_TRI + 128], op=ALU.mult),
                                           reads=[pt, constb], writes=[pt])
                                  mm(po, po[0:65, c0:G], VB[:, j, hh, :], pt[:, c0:G], [VB, pt], start=(j == 0), stop=(j == nkt_g - 1))
                              if jx == 1 and hh + 1 < 8:
                                  proj_stage(hh + 1, 0)
                              if jx == 2 and pending is not None:
                                  pending()
                                  pending = None
                              if jx == 3 and hh + 1 < 8:
                                  proj_stage(hh + 1, 1)
                              yield 1.0
                          pending = (lambda po=po, hh=hh: normalize_heads(po, OBT, hh))
                      pending()
                      for c in range(4):
                          ps = proj_fm(c)
                          rope(ps, ps[:, :], 128, PA_ap, CA, SA, AQT, AQT[:, c, :])
                          yield 4.0
                      for kg in range(2):
                          ps = proj_fm(4 + kg)
                          rope(ps, ps[:, :], 128, PA_ap, CA, SA, AKT, AKT[:, kg, t0:t0 + G])
                          yield 4.0
                      yield 0.0

                  P.barrier()
                  psf.items = _banks[2:4]
                  gens = [[gen_indexer(), 0.0, True], [gen_mla(), 0.0, True]]
                  while any(gv[2] for gv in gens):
                      live = [gv for gv in gens if gv[2]]
                      gv = min(live, key=lambda q: q[1])
                      try:
                          gv[1] += next(gv[0])
                      except StopIteration:
                          gv[2] = False
                  ckpt(3)
                  SK = 2
                  psq.items = _banks[0:3]
                  psf.items = _banks[3:4]
                  pending = None
                  for hh in range(8):
                      c = hh // 2
                      pb_ = 64 * (hh % 2)
                      kg = hh // 4
                      po = pso.next()
                      q = []
                      for jx in range(nkt_g + SK):
                          if jx < nkt_g:
                              j = jx
                              c0 = 128 * max(0, j - 4 * g)
                              ps = psq.next()
                              mm(ps, ps[:, c0:G], AKT[pb_:pb_ + 64, kg, j * 128:(j + 1) * 128], AQT[pb_:pb_ + 64, c, c0:G], [AKT, AQT])
                              q.append((j, ps, c0))
                          if jx >= SK:
                              j, ps, c0 = q.pop(0)
                              pt = PTr.next()
                              P.op("act", lambda h, ps=ps, pt=pt, c0=c0: h.activation(out=pt[:, c0:G], in_=ps[:, c0:G], func=AF.Exp,
                                                                                      scale=0.125), reads=[ps], writes=[pt])
                              P.op("dve", lambda h, pt=pt, c0=c0, j=j: h.tensor_tensor(out=pt[:, c0:G], in0=pt[:, c0:G],
                                                                                       in1=MT[:, j, c0:G], op=ALU.mult),
                                   reads=[pt, MT], writes=[pt])
                              mm(po, po[0:65, c0:G], AV[:, j, kg, :], pt[:, c0:G], [AV, pt], start=(j == 0), stop=(j == nkt_g - 1))
                          if jx == 2 and pending is not None:
                              pending()
                              pending = None
                      pending = (lambda po=po, hh=hh: normalize_heads(po, OAT, hh))
                  pending()


                  psf.items = list(_banks)
                  psq.items = _banks[0:2]
                  ckpt(4)
                  if taps and b == nseq - 1 and g == TAPG:
                      P.dma("pool", tap_d["oat"][:, :], OAT[:].rearrange("p a b -> p (a b)"), reads=[OAT])
                      P.dma("pool", tap_d["obt"][:, :], OBT[:].rearrange("p a b -> p (a b)"), reads=[OBT])
                      P.dma("pool", tap_d["mt"][:, :], MT[:, :, :].rearrange("p a b -> p (a b)"), reads=[MT])
                      if "ckvs" in tap_d:
                          P.dma("pool", tap_d["ckvs"][:, :], CKVS[:, :], reads=[CKVS])
                          P.dma("pool", tap_d["krt"][:, :], KRT[:, :], reads=[KRT])
                          P.dma("pool", tap_d["vb"][:, :], VB[:].rearrange("p a b c -> p (a b c)"), reads=[VB])
                          P.dma("pool", tap_d["qab"][:, :], qab[:, :], reads=[qab])
                          P.dma("pool", tap_d["qr"][:, :], qr[0:32, :], reads=[qr])
                          P.dma("pool", tap_d["cqt"][:, :], CQT[:, :, :].rearrange("p a b -> p (a b)"), reads=[CQT])
                          P.dma("sp", tap_d["osb"][:, :], OSB[:, :], reads=[OSB])
                          P.dma("sp", tap_d["rec"][:, :], REC[:, :], reads=[REC])
                          P.dma("sp", tap_d["rstd"][:, :], RSTD[:, :], reads=[RSTD])
                          P.dma("sp", tap_d["xf0"][:, :], XF[:, 0, :], reads=[XF])
                          P.dma("sp", tap_d["sq"][:, :], SQ[:, :], reads=[SQ])
                  ckpt(5)
                  P.barrier()
                  psf.items = list(_banks) + list(pso.items)
                  for c in range(8):
                      wa = load_w(wsmall, lambda w: w[0:64, :], "wba", c)
                      pa = psf.next()
                      for hh in range(8):
                          mm(pa, pa[:, :], wa[0:64, hh * 128:(hh + 1) * 128], OAT[0:64, hh, :], [wa, OAT], start=(hh == 0), stop=(hh == 7))
                      wb = load_w(wsmall, lambda w: w[0:64, :], "wbb", c)
                      pb2 = psf.next()
                      for hh in range(8):
                          mm(pb2, pb2[:, :], wb[0:64, hh * 128:(hh + 1) * 128], OBT[0:64, hh, :], [wb, OBT], start=(hh == 0), stop=(hh == 7))
                      wga = load_w(wsmall, lambda w: w[:, :], "win", 14 + c)
                      pga = psf.next()
                      for kc in range(8):
                          mm(pga, pga[:, :], wga[:, kc * 128:(kc + 1) * 128], hnT[:, kc, :], [wga, hnT], start=(kc == 0), stop=(kc == 7))
                      P.op("act", lambda h, pga=pga: h.activation(out=SGA[:, :], in_=pga[:, :], func=AF.Sigmoid), reads=[pga], writes=[SGA])
                      wgb = load_w(wsmall, lambda w: w[:, :], "win", 22 + c)
                      pgb = psf.next()
                      for kc in range(8):
                          mm(pgb, pgb[:, :], wgb[:, kc * 128:(kc + 1) * 128], hnT[:, kc, :], [wgb, hnT], start=(kc == 0), stop=(kc == 7))
                      P.op("act", lambda h, pgb=pgb: h.activation(out=SGB[:, :], in_=pgb[:, :], func=AF.Sigmoid), reads=[pgb], writes=[SGB])
                      P.op("dve", lambda h, pa=pa: h.tensor_tensor(out=GT1[:, :], in0=pa[:, :], in1=SGA[:, :], op=ALU.mult),
                           reads=[pa, SGA], writes=[GT1])
                      P.op("dve", lambda h, pb2=pb2: h.tensor_tensor(out=GT2[:, :], in0=pb2[:, :], in1=SGB[:, :], op=ALU.mult),
                           reads=[pb2, SGB], writes=[GT2])
                      P.op("dve", lambda h, c=c: h.tensor_tensor(out=MRG[:, c, :], in0=GT1[:, :], in1=GT2[:, :], op=ALU.add),
                           reads=[GT1, GT2], writes=[MRG])
                  ckpt(6)
                  for nq in range(4):
                      w = load_w(wmid, lambda w: w[:, 0:2048], "wo", nq)
                      for r in range(4):
                          ps = psf.next()
                          for kc in range(8):
                              mm(ps, ps[:, 0:256], MRG[:, kc, r * 128:(r + 1) * 128], w[:, kc * 256:(kc + 1) * 256], [MRG, w],
                                 start=(kc == 0), stop=(kc == 7))
                          P.op("dve", lambda h, ps=ps, r=r, nq=nq: h.tensor_tensor(
                              out=XH[:, r, nq * 256:(nq + 1) * 256], in0=XH[:, r, nq * 256:(nq + 1) * 256], in1=ps[:, 0:256], op=ALU.add),
                              reads=[XHr[r], ps], writes=[XHr[r]])
                  for r in range(4):
                      rmsnorm_tok(XHr[r], XH[:, r, :], SF_GFFN, hn2T, r, HN2TOK)
                  ckpt(7)
                  for fh in range(2):
                      for fi in range(11):
                          f = fh * 11 + fi
                          w = load_w(wmid, lambda w: w[:, 0:2048], "wup", f)
                          pg = psf.next()
                          for kc in range(8):
                              mm(pg, pg[:, :], w[:, kc * 256:kc * 256 + 128], hn2T[:, kc, :], [w, hn2T], start=(kc == 0), stop=(kc == 7))
                          pv = psf.next()
                          for kc in range(8):
                              mm(pv, pv[:, :], w[:, kc * 256 + 128:kc * 256 + 256], hn2T[:, kc, :], [w, hn2T], start=(kc == 0), stop=(kc == 7))
                          res = []
                          for (pp, ch, ring_u, ring_a) in ((pg, f, UGr, AGr), (pv, 22 + f, UVr, AVr)):
                              ue = ring_u.next()
                              ac = ring_a.next()
                              cw = SF_CW + ch * 3
                              P.op("act", lambda h: h.activation(out=ue[:, 0:2], in_=CARRY[:, ch, :], func=AF.Copy),
                                   reads=[CARRY], writes=[ue])
                              P.op("act", lambda h: h.activation(out=ue[:, 2:514], in_=pp[:, :], func=AF.Copy),
                                   reads=[pp], writes=[ue])
                              P.op("act", lambda h: h.activation(out=CARRY[:, ch, :], in_=ue[:, 512:514], func=AF.Copy),
                                   reads=[ue], writes=[CARRY])
                              P.op("dve", lambda h: h.tensor_scalar(out=ac[:, :], in0=ue[:, 2:514], scalar1=smallf[:, cw + 2:cw + 3],
                                                                    scalar2=None, op0=ALU.mult), reads=[ue, smallf], writes=[ac])
                              P.op("dve", lambda h: h.scalar_tensor_tensor(
                                  out=ac[:, :], in0=ue[:, 1:513], scalar=smallf[:, cw + 1:cw + 2], in1=ac[:, :],
                                  op0=ALU.mult, op1=ALU.add), reads=[ue, smallf, ac], writes=[ac])
                              P.op("dve", lambda h: h.scalar_tensor_tensor(
                                  out=ac[:, :], in0=ue[:, 0:512], scalar=smallf[:, cw:cw + 1], in1=ac[:, :],
                                  op0=ALU.mult, op1=ALU.add), reads=[ue, smallf, ac], writes=[ac])
                              res.append(ac)
                          sg = SGr.next()
                          P.op("act", lambda h: h.activation(out=sg[:, :], in_=res[0][:, :], func=AF.Silu,
                                                             bias=smallf[:, SF_CB + f:SF_CB + f + 1]),
                               reads=[res[0], smallf], writes=[sg])
                          P.op("dve", lambda h: h.scalar_tensor_tensor(
                              out=ACT_[:, fi, :], in0=res[1][:, :], scalar=smallf[:, SF_CB + 22 + f:SF_CB + 22 + f + 1], in1=sg[:, :],
                              op0=ALU.add, op1=ALU.mult), reads=[sg, res[1], smallf], writes=[ACT_])
                      for nq in range(4):
                          w = load_w(wmid, lambda w: w[:, :], "wd", fh * 4 + nq)
                          for r in range(4):
                              ps = psf.next()
                              for fi in range(11):
                                  mm(ps, ps[:, 0:256], ACT_[:, fi, r * 128:(r + 1) * 128], w[:, fi * 256:(fi + 1) * 256], [ACT_, w],
                                     start=(fi == 0), stop=(fi == 10))
                              P.op("dve", lambda h, ps=ps, r=r, nq=nq: h.tensor_tensor(
                                  out=XH[:, r, nq * 256:(nq + 1) * 256], in0=XH[:, r, nq * 256:(nq + 1) * 256], in1=ps[:, 0:256], op=ALU.add),
                                  reads=[XHr[r], ps], writes=[XHr[r]])
                  ckpt(8)
                  for r in range(4):
                      st = stats
                      P.op("act", lambda h, r=r: h.activation(out=HN2TOK[:, :], in_=XH[:, r, :], func=AF.Square, accum_out=st[:, 4:5]),
                           reads=[XHr[r]], writes=[HN2TOK, st])
                      P.op("act", lambda h: h.activation(out=st[:, 8:9], in_=st[:, 9:10], func=AF.Copy), reads=[], writes=[st])
                      P.op("act", lambda h: h.activation(out=st[:, 6:7], in_=st[:, 4:5], func=AF.Sqrt, scale=1.0 / D,
                                                         bias=constf[:, CF_POW + 30:CF_POW + 31]), reads=[st, constf], writes=[st])
                      P.op("dve", lambda h: h.reciprocal(out=st[:, 6:7], in_=st[:, 6:7]), reads=[st], writes=[st])
                      on = ONr[r % 2]
                      P.op("dve", lambda h, r=r: h.scalar_tensor_tensor(out=on[:, :], in0=XH[:, r, :], scalar=st[:, 6:7], in1=gfin[:, :],
                                                                        op0=ALU.mult, op1=ALU.mult), reads=[XHr[r], st, gfin], writes=[on])
                      d = P.dma("sp", out_d[b, t0 + r * 128:t0 + (r + 1) * 128, :], on[:, :], reads=[on])
                      out_deps.append(d)
                  if first_group[0]:
                      while pending_conv:
                          conv_one()
                      first_group[0] = False
        except _Stop:
            pass
        if taps and "c_wuq" in tap_d:
            P.dma("pool", tap_d["c_wuq"][:, :], wuq[:].rearrange("p a b c -> p (a b c)"), reads=[wuq])
            P.dma("pool", tap_d["c_wukT"][:, :], wukT[:].rearrange("p a b -> p (a b)"), reads=[wukT])
            P.dma("pool", tap_d["c_wuv"][:, :], wuv[:], reads=[wuv])
            P.dma("pool", tap_d["c_wmisc"][:, :], wmisc[:].rearrange("p a b -> p (a b)"), reads=[wmisc])
            P.dma("pool", tap_d["c_constb"][:, :], constb[:], reads=[constb])
            P.dma("sp", tap_d["c_constf"][:, :], constf[:], reads=[constf])
            P.dma("sp", tap_d["c_smallf"][:, :], smallf[:], reads=[smallf])
            P.dma("sp", tap_d["c_gfin"][:, :], gfin[:], reads=[gfin])
            out_deps += [(k, v) for k, v in P.dma_cnt.items()]
        P.wait_all("sp", out_deps)
        P.replay(es)
    return nc


def _consts():
    cb = np.zeros((128, CB_N), np.float32)
    cb[:, CB_ID:CB_ID + 128] = np.eye(128)
    pa = np.zeros((128, 128), np.float32)
    for hb in (0, 64):
        for j in range(8):
            pa[hb + j + 8, hb + j] = -1.0
            pa[hb + j, hb + j + 8] = 1.0
    cb[:, CB_PA:CB_PA + 128] = pa
    pb = np.zeros((32, 32), np.float32)
    for j in range(16):
        pb[j + 16, j] = -1.0
        pb[j, j + 16] = 1.0
    cb[0:32, CB_PB:CB_PB + 32] = pb
    k = np.arange(128)[:, None]
    q = np.arange(128)[None, :]
    cb[:, CB_TRI:CB_TRI + 128] = (k <= q).astype(np.float32)
    cb[:, CB_ONE:CB_ONE + 128] = 1.0
    cf = np.zeros((128, CF_N), np.float32)
    cf[:, CF_ONE:CF_ONE + 128] = 1.0
    bd = np.zeros((128, 128), np.float32)
    bd[0:64, 0:64] = 1.0
    bd[64:128, 64:128] = 1.0
    cf[:, CF_BD:CF_BD + 128] = bd
    cf[:, CF_NTRI:CF_NTRI + 128] = np.where(q.T >= k.T * 0 + np.arange(128)[None, :], 0.0, 0.0)
    qq = np.arange(128)[:, None]
    kk = np.arange(128)[None, :]
    cf[:, CF_NTRI:CF_NTRI + 128] = np.where(kk <= qq, 0.0, NEG)
    cf[:, CF_POW:CF_POW + 30] = 2.0 ** -(np.arange(30) + 1.0)
    cf[:, CF_POW + 30] = EPS
    cf[:, CF_POW + 31] = -0.5
    p = np.arange(128)
    j = p % 64
    inva = np.where(j < 16, THETA ** (-(2.0 * (j % 8)) / 16.0), 0.0)
    cf[:, CF_INVA] = inva / (2 * math.pi)
    invb = np.where(p < 32, THETA ** (-(2.0 * (p % 16)) / 32.0), 0.0)
    cf[:, CF_INVB] = invb / (2 * math.pi)
    cf[:, CF_NPI] = -math.pi * (1.0 - 2e-6)
    return cb, cf


def _prep_weights(norm_mix_g, w_in, idx_k_norm_g, q_a_norm_g, kv_a_norm_g, w_uq, w_uk, w_uv,
                  w_branch_a, w_branch_b, w_out, norm_ffn_g, w_up, conv_w, conv_b, w_down, norm_final_g):
    f = lambda a: np.ascontiguousarray(np.asarray(a, dtype=np.float32))
    w_in = f(w_in)[0]
    cols = []
    for c in range(4):
        cols.append(np.arange(c * 128, (c + 1) * 128))
    for kg in range(2):
        base = 512 + kg * 64
        cols.append(np.concatenate([np.arange(base, base + 64)] * 2))
    for c in range(4):
        cols.append(np.arange(768 + c * 128, 768 + (c + 1) * 128))
    cols.append(np.concatenate([np.arange(1280, 1344)] * 2))
    for c in range(2):
        cols.append(np.arange(1352 + c * 128, 1352 + (c + 1) * 128))
    cols.append(np.arange(1608, 1736))
    for c in range(8):
        cols.append(np.arange(1768 + c * 128, 1768 + (c + 1) * 128))
    for c in range(8):
        cols.append(np.arange(2792 + c * 128, 2792 + (c + 1) * 128))
    assert len(cols) == 30

    def fm(wc):
        M = wc.shape[1]
        return np.ascontiguousarray(wc.reshape(8, 128, M).transpose(1, 0, 2)).reshape(128, 8 * M)
    win = np.stack([fm(w_in[:, c]) for c in cols])
    misc_cols = np.concatenate([np.arange(1736, 1768), np.arange(640, 768), np.arange(1344, 1352)])
    wmisc = fm(w_in[:, misc_cols])
    wuq = np.ascontiguousarray(f(w_uq)[0].reshape(2, 128, 8, 96).transpose(1, 0, 2, 3)).reshape(128, 2 * 8 * 96)
    wukT = np.ascontiguousarray(f(w_uk)[0].transpose(2, 1, 0)).reshape(64, 8 * 128)
    wuv = f(w_uv)[0].reshape(128, 512)

    def br(w):
        w = f(w)[0].reshape(8, 64, 8, 128)
        return np.ascontiguousarray(w.transpose(2, 1, 0, 3)).reshape(8, 64, 8 * 128)
    wba, wbb = br(w_branch_a), br(w_branch_b)
    wo = np.ascontiguousarray(f(w_out)[0].reshape(8, 128, 4, 256).transpose(2, 1, 0, 3)).reshape(4, 128, 8 * 256)
    wu = f(w_up)[0].reshape(8, 128, 2, 22, 128)
    wup = np.ascontiguousarray(wu.transpose(3, 1, 0, 2, 4)).reshape(22, 128, 8 * 256)
    wdn = f(w_down)[0].reshape(2, 11, 128, 4, 256)
    wd = np.ascontiguousarray(wdn.transpose(0, 3, 2, 1, 4)).reshape(8, 128, 11 * 256)
    sf = np.zeros((128, SF_N), np.float32)
    sf[:, SF_GMIX:SF_GMIX + 8] = f(norm_mix_g)[0].reshape(8, 128).T
    sf[:, SF_GFFN:SF_GFFN + 8] = f(norm_ffn_g)[0].reshape(8, 128).T
    sf[:, SF_GIK] = np.concatenate([f(idx_k_norm_g)[0]] * 2)
    sf[:, SF_GQ:SF_GQ + 2] = f(q_a_norm_g)[0].reshape(2, 128).T
    sf[:, SF_GKV] = f(kv_a_norm_g)[0]
    cw = f(conv_w)[0].reshape(3, 44, 128)
    sf[:, SF_CW:SF_CW + 132] = cw.transpose(2, 1, 0).reshape(128, 132)
    sf[:, SF_CB:SF_CB + 44] = f(conv_b)[0].reshape(44, 128).T
    gfin = f(norm_final_g).reshape(1, D)
    return dict(win=win, wmisc=wmisc, wuq=wuq, wukT=wukT, wuv=wuv, wba=wba, wbb=wbb, wo=wo, wup=wup, wd=wd,
                smallf=sf, gfin=gfin)


_CACHE = {}


def kernel(x, positions, **weights):
    x = np.asarray(x, dtype=np.float32)
    positions = np.asarray(positions, dtype=np.int32)
    shared = _prep_weights(**weights)
    cb, cf = _consts()
    shared["constb"] = cb
    shared["constf"] = cf
    if "nc" not in _CACHE:
        _CACHE["nc"] = build_program()
    nc = _CACHE["nc"]
    in_maps = []
    for c in range(NCORES):
        m = dict(shared)
        m["x"] = np.ascontiguousarray(x[c * NSEQ:(c + 1) * NSEQ])
        m["pos"] = np.ascontiguousarray(positions[c * NSEQ:(c + 1) * NSEQ])
        in_maps.append(m)
    res = run_bass_kernel_spmd(nc, in_maps, core_ids=list(range(NCORES)))
    return np.concatenate([r["out"] for r in res.results], axis=0).astype(np.float32)
```

```python
import math
from contextlib import ExitStack

import ml_dtypes
import numpy as np

import concourse.bass as bass
import concourse.mybir as mybir
from concourse.bass_utils import run_bass_kernel_spmd

F32, BF16, I32 = mybir.dt.float32, mybir.dt.bfloat16, mybir.dt.int32
AF = mybir.ActivationFunctionType
ALU = mybir.AluOpType
AX = mybir.AxisListType

D = 1024
S = 2048
G = 512
NCORES = 8
NSEQ = 2
THETA = 500000.0
EPS = 1e-6
NIT = 16
TOPK = 256
DFF = 2816
NEG = -1.0e30
TAPG = 0

CB_ID, CB_PA, CB_PB, CB_TRI, CB_ONE, CB_N = 0, 128, 256, 288, 416, 544
CF_ONE, CF_BD, CF_NTRI, CF_POW, CF_INVA, CF_INVB, CF_NPI, CF_N = 0, 128, 256, 384, 416, 417, 418, 420
SF_GMIX, SF_GFFN, SF_GIK, SF_GQ, SF_GKV, SF_CW, SF_CB, SF_N = 0, 8, 16, 17, 19, 20, 152, 196


class Tk:
    def __init__(self, t, name=""):
        self.t = t
        self.name = name
        self.w = None
        self.r = {}

    def __getitem__(self, k):
        return self.t[k]


class Eng:
    def __init__(self, name):
        self.name = name
        self.ops = []
        self.cnt = 0
        self.seen = {}


class _Rec:
    def __getattr__(self, name):
        return lambda *a, **k: (name, a, k)


class Prog:
    CE = ("pe", "act", "dve", "pool")

    def __init__(self, nc):
        self.nc = nc
        self.E = {n: Eng(n) for n in ("pe", "act", "dve", "pool", "sp")}
        self.ndma = 0
        self.dma_cnt = {}
        self.semnames = set(self.CE)

    def _deps(self, eng, reads, writes):
        deps = {}

        def add(d):
            if d is None:
                return
            k, v = d
            if deps.get(k, 0) < v:
                deps[k] = v
        for t in reads:
            add(t.w)
            if getattr(t, "psum", False):
                for d in t.r.values():
                    if d[0] != eng.name:
                        add(d)
        same_ok = eng.name == "pe"
        for t in writes:
            if t.w is not None and (t.w[0] != eng.name or not same_ok):
                add(t.w)
            for d in t.r.values():
                if d[0] != eng.name or not same_ok:
                    add(d)
        for k, v in deps.items():
            if eng.seen.get(k, 0) < v:
                if k in self.CE:
                    assert v <= self.E[k].cnt, (k, v, self.E[k].cnt)
                eng.ops.append(("wait", k, v))
                eng.seen[k] = v

    def op(self, en, fn, reads=(), writes=(), inc=True):
        eng = self.E[en]
        name_, a_, k_ = fn(_Rec())
        fn = lambda h, name_=name_, a_=a_, k_=k_: getattr(h, name_)(*a_, **k_)
        self._deps(eng, reads, writes)
        if inc:
            eng.cnt += 1
            idx = eng.cnt
            eng.ops.append(("op", fn, en, 1))
        else:
            idx = eng.cnt + 1
            eng.ops.append(("op", fn, None, 0))
        d = (en, idx)
        for t in reads:
            t.r[en] = d
        for t in writes:
            t.w = d
            t.r = {}

    def dma(self, qn, out_ap, in_ap, reads=(), writes=()):
        eng = self.E[qn]
        self._deps(eng, reads, writes)
        tgt = writes[0] if writes else reads[0]
        if not hasattr(tgt, "dsem"):
            tgt.dsem = {}
        if qn not in tgt.dsem:
            tgt.dsem[qn] = "d%d" % self.ndma
            self.ndma += 1
        sem = tgt.dsem[qn]
        self.semnames.add(sem)
        v = self.dma_cnt.get(sem, 0) + 16
        self.dma_cnt[sem] = v
        eng.ops.append(("op", lambda h, o=out_ap, i=in_ap: h.dma_start(out=o, in_=i), sem, 16))
        d = (sem, v)
        for t in reads:
            t.r[sem] = d
        for t in writes:
            t.w = d
            t.r = {}
        return d

    def barrier(self, extra=()):
        for a in self.CE:
            ea = self.E[a]
            for t in extra:
                for d in list(t.r.values()) + ([t.w] if t.w else []):
                    if d[0] not in self.CE and ea.seen.get(d[0], 0) < d[1]:
                        ea.ops.append(("wait", d[0], d[1]))
                        ea.seen[d[0]] = d[1]
            for b in self.CE:
                if a == b:
                    continue
                v = self.E[b].cnt
                if v > 0 and ea.seen.get(b, 0) < v:
                    ea.ops.append(("wait", b, v))
                    ea.seen[b] = v

    def wait_all(self, en, deps):
        eng = self.E[en]
        for k, v in deps:
            if eng.seen.get(k, 0) < v:
                eng.ops.append(("wait", k, v))
                eng.seen[k] = v

    def replay(self, es):
        nc = self.nc
        sems = {}
        for n in sorted(self.semnames):
            sems[n] = es.enter_context(nc.semaphore("s_" + n))
        block = es.enter_context(nc.Block())

        def run(eng):
            def f(h):
                for o in eng.ops:
                    if o[0] == "wait":
                        h.wait_ge(sems[o[1]], o[2])
                    else:
                        ins = o[1](h)
                        if o[2] is not None:
                            ins.then_inc(sems[o[2]], o[3])
            return f
        block.tensor(run(self.E["pe"]))
        block.scalar(run(self.E["act"]))
        block.vector(run(self.E["dve"]))
        block.gpsimd(run(self.E["pool"]))
        block.sync(run(self.E["sp"]))


class Ring:
    def __init__(self, items):
        self.items = items
        self.i = 0

    def next(self):
        t = self.items[self.i % len(self.items)]
        self.i += 1
        return t


class _Stop(Exception):
    pass


def build_program(nseq=NSEQ, ngroups=S // G, taps=None, stage=99):
    nc = bass.Bass("TRN2", target_bir_lowering=False)
    SL = ngroups * G

    def din(name, shape, dt=F32):
        return nc.dram_tensor(name, list(shape), dt, kind="ExternalInput").ap()

    x_d = din("x", [nseq, SL, D])
    pos_d = din("pos", [nseq, SL], I32)
    constb_d = din("constb", [128, CB_N])
    constf_d = din("constf", [128, CF_N])
    smallf_d = din("smallf", [128, SF_N])
    gfin_d = din("gfin", [1, D])
    win_d = din("win", [30, 128, 8 * 128])
    wmisc_d = din("wmisc", [128, 8 * 168])
    wuq_d = din("wuq", [128, 2 * 8 * 96])
    wukT_d = din("wukT", [64, 8 * 128])
    wuv_d = din("wuv", [128, 512])
    wba_d = din("wba", [8, 64, 8 * 128])
    wbb_d = din("wbb", [8, 64, 8 * 128])
    wo_d = din("wo", [4, 128, 8 * 256])
    wup_d = din("wup", [22, 128, 8 * 256])
    wd_d = din("wd", [8, 128, 11 * 256])
    out_d = nc.dram_tensor("out", [nseq, SL, D], F32, kind="ExternalOutput").ap()
    tap_d = {}
    if taps:
        for k, shp in taps.items():
            tap_d[k] = nc.dram_tensor("tap_" + k, list(shp), F32, kind="ExternalOutput").ap()

    es = ExitStack()
    with es:
        P = Prog(nc)

        def SB(name, shape, dt):
            return Tk(es.enter_context(nc.sbuf_tensor("sb_" + name, list(shape), dt)), name)

        def PSM(name, shape, dt):
            t = Tk(es.enter_context(nc.psum_tensor(name, list(shape), dt)), name)
            t.psum = True
            return t

        constb = SB("constb", [128, CB_N], BF16)
        constf = SB("constf", [128, CF_N], F32)
        smallf = SB("smallf", [128, SF_N], F32)
        gfin = SB("gfin", [128, D], F32)
        wmisc = SB("wmisc", [128, 8, 168], BF16)
        wuq = SB("wuq", [128, 2, 8, 96], BF16)
        wukT = SB("wukT", [64, 8, 128], BF16)
        wuv = SB("wuv", [128, 512], BF16)
        AKT = SB("AKT", [128, 2, S], BF16)
        IKT = SB("IKT", [128, S], BF16)
        AV = SB("AV", [128, 16, 2, 65], BF16)
        CKVS = SB("CKVS", [128, S], BF16)
        KRT = SB("KRT", [32, S], BF16)
        VB = SB("VB", [128, 16, 8, 65], BF16)
        IW = SB("IW", [128, 16, 8], F32)
        CARRY = SB("CARRY", [128, 44, 2], F32)
        XH = SB("XH", [128, 4, D], F32)
        XHr = [Tk(XH.t, "XH%d" % i) for i in range(4)]
        hnT = SB("hnT", [128, 8, G], BF16)
        OAT = SB("OAT", [64, 8, G], BF16)
        OBT = SB("OBT", [64, 8, G], BF16)
        stats = SB("stats", [128, 16], F32)
        wsmall = Ring([SB("wsm%d" % i, [128, 1024], BF16) for i in range(3)])
        wmid = Ring([SB("wmid%d" % i, [128, 2816], BF16) for i in range(3)])
        SCRN = 20864
        scr = es.enter_context(nc.sbuf_tensor("scr", [128, SCRN], F32))
        off = [0]
        hiw = [0]

        def carve(name, nwords_unused, dt, shape):
            nel = int(np.prod(shape[1:]))
            nwords = (nel + 1) // 2 if dt == BF16 else nel
            a = off[0]
            off[0] += nwords
            assert off[0] <= SCRN, (name, off[0])
            hiw[0] = max(hiw[0], off[0])
            v = scr[:, a:a + nwords]
            if dt == BF16:
                v = v.bitcast(BF16)
            return Tk(_View(v, shape), name)

        class _View:
            def __init__(self, ap, shape):
                self.ap = ap
                if len(shape) == 3:
                    self.ap = ap.rearrange("p (a b) -> p a b", a=shape[1])
                elif len(shape) == 4:
                    self.ap = ap.rearrange("p (a b c) -> p a b c", a=shape[1], b=shape[2])

            def __getitem__(self, k):
                return self.ap[k]

        off[0] = 0
        AQT = carve("AQT", 2048, BF16, [128, 4, G])
        IQT = carve("IQT", 2048, BF16, [128, 4, G])
        MT = carve("MT", 4096, BF16, [128, 16, G])
        ACC = carve("ACC", 2048, F32, [128, S])
        MQ = carve("MQ", 1024, BF16, [128, S])
        PTr = Ring([carve("PT%d" % i, 256, BF16, [128, G]) for i in range(3)])
        CA = carve("CA", 512, F32, [128, G])
        SA = carve("SA", 512, F32, [128, G])
        CBt = carve("CBt", 512, F32, [128, G])
        SBt = carve("SBt", 512, F32, [128, G])
        RXB = carve("RXB", 256, BF16, [128, G])
        RT1 = carve("RT1", 512, F32, [128, G])
        RT2 = carve("RT2", 512, F32, [128, G])
        xf_off = [off[0]]
        XF = carve("XF", 1024, F32, [128, 2, G])
        SQ = carve("SQ", 512, F32, [128, G])
        RSTD = carve("RSTD", 512, F32, [128, G])
        rstd_end = [off[0]]
        CQT = carve("CQT", 512, BF16, [128, 2, G])
        RLr = Ring([carve("RL%d" % i, 512, F32, [128, G]) for i in range(2)])
        QNT = carve("QNT", 256, BF16, [128, G])
        QABr = Ring([carve("QAB%d" % i, 256, BF16, [128, G]) for i in range(2)])
        QRr = Ring([carve("QR%d" % i, 256, BF16, [128, G]) for i in range(2)])
        OSB = carve("OSB", 512, F32, [128, G])
        REC = carve("REC", 512, F32, [128, G])
        NRB = carve("NRB", 256, BF16, [128, G])
        RXB2 = carve("RXB2", 256, BF16, [128, G])
        QRF = carve("QRF", 512, F32, [128, G])
        POSF, TT1, TT2 = SQ, RT1, RT2
        HNTOK = carve("HNTOK", 512, BF16, [128, D])
        BIS = carve("BIS", 64, F32, [128, 64])
        BIS2 = carve("BIS2", 64, F32, [128, 64])
        p1_end = off[0]
        off[0] = xf_off[0]
        ACC2 = carve("ACC2", 2048, F32, [128, S])
        assert off[0] == rstd_end[0], (off[0], rstd_end[0])
        off[0] = p1_end
        off[0] = 0
        MRG = carve("MRG", 2048, BF16, [128, 8, G])
        hn2T = carve("hn2T", 2048, BF16, [128, 8, G])
        ACT_ = carve("ACTT", 2816, BF16, [128, 11, G])
        UGr = Ring([carve("UG%d" % i, 516, F32, [128, 516]) for i in range(2)])
        UVr = Ring([carve("UV%d" % i, 516, F32, [128, 516]) for i in range(2)])
        AGr = Ring([carve("AG%d" % i, 512, F32, [128, G]) for i in range(2)])
        AVr = Ring([carve("AVV%d" % i, 512, F32, [128, G]) for i in range(2)])
        SGr = Ring([carve("SG%d" % i, 512, F32, [128, G]) for i in range(2)])
        GT1 = carve("GT1", 512, F32, [128, G])
        GT2 = carve("GT2", 512, F32, [128, G])
        SGA = carve("SGA", 256, BF16, [128, G])
        SGB = carve("SGB", 256, BF16, [128, G])
        HN2TOK = carve("HN2TOK", 512, BF16, [128, D])
        ONORM = carve("ONORM", 1024, F32, [128, D])
        ONORM2 = carve("ONORM2", 1024, F32, [128, D])
        ONr = [ONORM, ONORM2]

        _banks = [PSM("psf%d" % i, [128, 512], F32) for i in range(4)]
        psf = Ring(list(_banks))
        psq = Ring(_banks[0:2])
        pso = Ring([PSM("pso%d" % i, [128, 512], F32) for i in range(2)])
        psb = Ring([PSM("psb%d" % i, [128, 1024], BF16) for i in range(2)])

        posi = SB("posi", [128, G], I32)
        tint = SB("tint", [128, G], I32)

        ident = constb[:, CB_ID:CB_ID + 128]

        P.dma("sp", constf[:], constf_d[:, :], writes=[constf])
        P.dma("sp", smallf[:], smallf_d[:, :], writes=[smallf])
        P.dma("sp", gfin[:], gfin_d[0:1, :].partition_broadcast(128), writes=[gfin])
        P.dma("pool", constb[:], constb_d[:, :], writes=[constb])
        P.dma("pool", wmisc[:].rearrange("p a b -> p (a b)"), wmisc_d[:, :], writes=[wmisc])
        P.dma("pool", wuq[:].rearrange("p a b c -> p (a b c)"), wuq_d[:, :], writes=[wuq])
        P.dma("pool", wukT[:].rearrange("p a b -> p (a b)"), wukT_d[:, :], writes=[wukT])
        P.dma("pool", wuv[:], wuv_d[:, :], writes=[wuv])
        for tix in range(16):
            for hh in range(2):
                P.op("dve", lambda h, tix=tix, hh=hh: h.memset(AV[:, tix, hh, 64:65], 1.0), writes=[AV])
            for hh in range(8):
                P.op("dve", lambda h, tix=tix, hh=hh: h.memset(VB[:, tix, hh, 64:65], 1.0), writes=[VB])

        P.op("dve", lambda h: h.memset(stats[:, :], 0.0), writes=[stats])

        def mm(ps_t, out_ap, lhsT_ap, rhs_ap, reads, start=True, stop=True):
            P.op("pe", lambda h: h.matmul(out_ap, lhsT_ap, rhs_ap, start=start, stop=stop),
                 reads=reads, writes=[ps_t], inc=stop)

        wsrc = {"win": win_d, "wba": wba_d, "wbb": wbb_d, "wo": wo_d, "wup": wup_d, "wd": wd_d}
        wscr = {}
        pending_conv = []
        for nm_ in ("win", "wba", "wbb", "wo", "wup", "wd"):
            shp = list(wsrc[nm_].shape)
            scr_ap = nc.dram_tensor(nm_ + "_bf", shp, BF16).ap()
            wscr[nm_] = (scr_ap, Tk(None, nm_ + "_bf"))
            for i_ in range(shp[0]):
                pending_conv.append((nm_, i_))
        first_group = [True]

        def conv_one():
            if pending_conv:
                nm_, i_ = pending_conv.pop(0)
                scr_ap, tk_ = wscr[nm_]
                P.dma("pool", scr_ap[i_, :, :], wsrc[nm_][i_, :, :], writes=[tk_])

        def load_w(ring, view_fn, nm_, i_):
            w = ring.next()
            if first_group[0]:
                P.dma("pool", view_fn(w), wsrc[nm_][i_, :, :], writes=[w])
                conv_one()
            else:
                scr_ap, tk_ = wscr[nm_]
                P.dma("sp", view_fn(w), scr_ap[i_, :, :], reads=[tk_], writes=[w])
            return w

        def rmsnorm_tok(src_t, src_ap, gcol, dstT, r, tokbuf):
            st = stats
            P.op("act", lambda h: h.activation(out=tokbuf[:, :], in_=src_ap, func=AF.Square,
                                               accum_out=st[:, 0:1]),
                 reads=[src_t], writes=[tokbuf, st])
            P.op("act", lambda h: h.activation(out=st[:, 8:9], in_=st[:, 9:10], func=AF.Copy), reads=[], writes=[st])
            P.op("act", lambda h: h.activation(out=st[:, 2:3], in_=st[:, 0:1], func=AF.Sqrt, scale=1.0 / D,
                                               bias=constf[:, CF_POW + 30:CF_POW + 31]), reads=[st, constf], writes=[st])
            P.op("dve", lambda h: h.reciprocal(out=st[:, 2:3], in_=st[:, 2:3]), reads=[st], writes=[st])
            P.op("dve", lambda h: h.tensor_scalar(out=tokbuf[:, :], in0=src_ap, scalar1=st[:, 2:3], scalar2=None,
                                                  op0=ALU.mult), reads=[src_t, st], writes=[tokbuf])
            pb = psb.next()
            for kc in range(8):
                P.op("pe", lambda h, kc=kc: h.transpose(out=pb[:, kc * 128:(kc + 1) * 128],
                                                        in_=tokbuf[:, kc * 128:(kc + 1) * 128], identity=ident),
                     reads=[tokbuf, constb], writes=[pb], inc=(kc == 7))
            for kc in range(8):
                eng = "act" if kc % 2 == 0 else "dve"
                if eng == "act":
                    P.op("act", lambda h, kc=kc: h.activation(out=dstT[:, kc, r * 128:(r + 1) * 128],
                                                              in_=pb[:, kc * 128:(kc + 1) * 128], func=AF.Copy,
                                                              scale=smallf[:, gcol + kc:gcol + kc + 1]),
                         reads=[pb, smallf], writes=[dstT])
                else:
                    P.op("dve", lambda h, kc=kc: h.tensor_scalar(out=dstT[:, kc, r * 128:(r + 1) * 128],
                                                                 in0=pb[:, kc * 128:(kc + 1) * 128],
                                                                 scalar1=smallf[:, gcol + kc:gcol + kc + 1], scalar2=None,
                                                                 op0=ALU.mult),
                         reads=[pb, smallf], writes=[dstT])

        def rope(src_t, src_ap, np_, Pm_ap, Ct, St, dst_t, dst_ap):
            P.op("act", lambda h: h.activation(out=RXB[0:np_, :], in_=src_ap, func=AF.Copy),
                 reads=[src_t], writes=[RXB])
            ckpt(1.011)
            ps2 = psf.next()
            mm(ps2, ps2[0:np_, :], Pm_ap, RXB[0:np_, :], [constb, RXB])
            ckpt(1.012)
            P.op("dve", lambda h: h.tensor_tensor(out=RT1[0:np_, :], in0=src_ap, in1=Ct[0:np_, :], op=ALU.mult),
                 reads=[src_t, Ct, RXB], writes=[RT1])
            ckpt(1.013)
            P.op("dve", lambda h: h.tensor_tensor(out=RT2[0:np_, :], in0=ps2[0:np_, :], in1=St[0:np_, :], op=ALU.mult),
                 reads=[ps2, St], writes=[RT2])
            ckpt(1.014)
            P.op("dve", lambda h: h.tensor_tensor(out=dst_ap, in0=RT1[0:np_, :], in1=RT2[0:np_, :], op=ALU.add),
                 reads=[RT1, RT2], writes=[dst_t])

        def rope_a(src_t, src_ap, np_, rxb):
            P.op("act", lambda h: h.activation(out=rxb[0:np_, :], in_=src_ap, func=AF.Copy), reads=[src_t], writes=[rxb])

        def rope_b(src_t, src_ap, np_, Pm_ap, Ct, St, dst_t, dst_ap, rxb):
            ps2 = psf.next()
            mm(ps2, ps2[0:np_, :], Pm_ap, rxb[0:np_, :], [constb, rxb])
            P.op("dve", lambda h: h.tensor_tensor(out=RT1[0:np_, :], in0=src_ap, in1=Ct[0:np_, :], op=ALU.mult),
                 reads=[src_t, Ct, rxb], writes=[RT1])
            P.op("dve", lambda h: h.tensor_tensor(out=RT2[0:np_, :], in0=ps2[0:np_, :], in1=St[0:np_, :], op=ALU.mult),
                 reads=[ps2, St], writes=[RT2])
            P.op("dve", lambda h: h.tensor_tensor(out=dst_ap, in0=RT1[0:np_, :], in1=RT2[0:np_, :], op=ALU.add),
                 reads=[RT1, RT2], writes=[dst_t])


        def trig_table(dst, invcol, offs):
            P.op("dve", lambda h: h.tensor_scalar(out=TT1[:, :], in0=POSF[:, :], scalar1=constf[:, invcol:invcol + 1],
                                                  scalar2=offs, op0=ALU.mult, op1=ALU.add),
                 reads=[POSF, constf], writes=[TT1])
            P.op("dve", lambda h: h.tensor_copy(out=tint[:], in_=TT1[:, :]), reads=[TT1], writes=[tint])
            P.op("dve", lambda h: h.tensor_tensor(out=TT2[:, :], in0=TT1[:, :], in1=tint[:], op=ALU.subtract),
                 reads=[TT1, tint], writes=[TT2])
            P.op("dve", lambda h: h.scalar_tensor_tensor(out=TT1[:, :], in0=TT2[:, :], scalar=0.0, in1=TT2[:, :],
                                                         op0=ALU.is_lt, op1=ALU.add), reads=[TT2], writes=[TT1])
            P.op("act", lambda h: h.activation(out=dst[:, :], in_=TT1[:, :], func=AF.Sin,
                                               bias=constf[:, CF_NPI:CF_NPI + 1], scale=2.0 * math.pi * (1.0 - 2e-6)),
                 reads=[TT1, constf], writes=[dst])

        def normalize_heads(po, dstT, h_):
            P.op("act", lambda h: h.activation(out=OSB[64:65, :], in_=po[64:65, :], func=AF.Ln), reads=[po], writes=[OSB])
            P.op("act", lambda h: h.activation(out=NRB[64:65, :], in_=OSB[64:65, :], func=AF.Exp, scale=-1.0),
                 reads=[OSB], writes=[NRB])
            pd = psf.next()
            mm(pd, pd[0:64, :], constb[64:65, CB_ONE:CB_ONE + 64], NRB[64:65, :], [constb, NRB])
            P.op("act", lambda h: h.activation(out=REC[0:64, :], in_=pd[0:64, :], func=AF.Copy), reads=[pd], writes=[REC])
            P.op("dve", lambda h: h.tensor_tensor(out=dstT[0:64, h_, :], in0=po[0:64, :], in1=REC[0:64, :], op=ALU.mult),
                 reads=[po, REC], writes=[dstT])


        out_deps = []

        def ckpt(n):
            if stage <= n:
                raise _Stop()
        try:
          for b in range(nseq):
              P.op("dve", lambda h: h.memset(CARRY[:, :, :], 0.0), writes=[CARRY])
              for g in range(ngroups):
                  t0 = g * G
                  P.barrier(extra=[ONORM, ONORM2])
                  psf.items = list(_banks)
                  for r in range(4):
                      P.dma("sp", XH[:, r, :], x_d[b, t0 + r * 128:t0 + (r + 1) * 128, :], writes=[XHr[r]])
                  P.dma("sp", posi[:], pos_d[b:b + 1, t0:t0 + G].partition_broadcast(128), writes=[posi])
                  P.op("dve", lambda h: h.tensor_copy(out=POSF[:, :], in_=posi[:]), reads=[posi], writes=[POSF])
                  for r in range(4):
                      rmsnorm_tok(XHr[r], XH[:, r, :], SF_GMIX, hnT, r, HNTOK)
                  trig_table(SA, CF_INVA, 0.5)
                  trig_table(CA, CF_INVA, 0.75)
                  trig_table(SBt, CF_INVB, 0.5)
                  trig_table(CBt, CF_INVB, 0.75)

                  ckpt(1)

                  def proj_fm(piece, M=128):
                      w = load_w(wsmall, lambda w: w[:, :], "win", piece)
                      ps = psf.next()
                      for kc in range(8):
                          mm(ps, ps[0:M, :], w[:, kc * 128:kc * 128 + M], hnT[:, kc, :], [w, hnT],
                             start=(kc == 0), stop=(kc == 7))
                      return ps

                  PA_ap = constb[:, CB_PA:CB_PA + 128]
                  PB_ap = constb[0:32, CB_PB:CB_PB + 32]
                  ckpt(1.1)
                  for c in range(4):
                      ps = proj_fm(6 + c)
                      rope(ps, ps[:, :], 128, PA_ap, CA, SA, IQT, IQT[:, c, :])
                  ckpt(1.2)
                  ps = proj_fm(10)
                  P.op("act", lambda h, ps=ps: h.activation(out=SQ[:, :], in_=ps[:, :], func=AF.Square), reads=[ps], writes=[SQ])
                  P.op("act", lambda h, ps=ps: h.activation(out=XF[:, 0, :], in_=ps[:, :], func=AF.Copy), reads=[ps], writes=[XF])
                  pn = psf.next()
                  mm(pn, pn[:, :], constf[:, CF_BD:CF_BD + 128], SQ[:, :], [constf, SQ])
                  P.op("act", lambda h, pn=pn: h.activation(out=RSTD[:, :], in_=pn[:, :], func=AF.Sqrt, scale=1.0 / 64,
                                                            bias=constf[:, CF_POW + 30:CF_POW + 31]), reads=[pn, constf], writes=[RSTD])
                  P.op("dve", lambda h: h.reciprocal(out=RSTD[:, :], in_=RSTD[:, :]), reads=[RSTD], writes=[RSTD])
                  P.op("dve", lambda h: h.scalar_tensor_tensor(out=XF[:, 1, :], in0=XF[:, 0, :], scalar=smallf[:, SF_GIK:SF_GIK + 1],
                                                               in1=RSTD[:, :], op0=ALU.mult, op1=ALU.mult),
                       reads=[XF, smallf, RSTD], writes=[XF])
                  rope(XF, XF[:, 1, :], 128, PA_ap, CA, SA, IKT, IKT[:, t0:t0 + G])
                  ckpt(1.3)
                  pn = pso.next()
                  for c in range(2):
                      ps = proj_fm(11 + c)
                      P.op("act", lambda h, ps=ps: h.activation(out=SQ[:, :], in_=ps[:, :], func=AF.Square), reads=[ps], writes=[SQ])
                      P.op("act", lambda h, ps=ps, c=c: h.activation(out=XF[:, c, :], in_=ps[:, :], func=AF.Copy), reads=[ps], writes=[XF])
                      mm(pn, pn[:, :], constf[:, CF_ONE:CF_ONE + 128], SQ[:, :], [constf, SQ], start=(c == 0), stop=(c == 1))
                  P.op("act", lambda h, pn=pn: h.activation(out=RSTD[:, :], in_=pn[:, :], func=AF.Sqrt, scale=1.0 / 256,
                                                            bias=constf[:, CF_POW + 30:CF_POW + 31]), reads=[pn, constf], writes=[RSTD])
                  P.op("dve", lambda h: h.reciprocal(out=RSTD[:, :], in_=RSTD[:, :]), reads=[RSTD], writes=[RSTD])
                  for c in range(2):
                      P.op("dve", lambda h, c=c: h.scalar_tensor_tensor(out=CQT[:, c, :], in0=XF[:, c, :],
                                                                        scalar=smallf[:, SF_GQ + c:SF_GQ + c + 1],
                                                                        in1=RSTD[:, :], op0=ALU.mult, op1=ALU.mult),
                           reads=[XF, smallf, RSTD], writes=[CQT])
                  ckpt(1.4)
                  ps = proj_fm(13)
                  P.op("act", lambda h, ps=ps: h.activation(out=SQ[:, :], in_=ps[:, :], func=AF.Square), reads=[ps], writes=[SQ])
                  P.op("act", lambda h, ps=ps: h.activation(out=XF[:, 0, :], in_=ps[:, :], func=AF.Copy), reads=[ps], writes=[XF])
                  pn = psf.next()
                  mm(pn, pn[:, :], constf[:, CF_ONE:CF_ONE + 128], SQ[:, :], [constf, SQ])
                  P.op("act", lambda h, pn=pn: h.activation(out=RSTD[:, :], in_=pn[:, :], func=AF.Sqrt, scale=1.0 / 128,
                                                            bias=constf[:, CF_POW + 30:CF_POW + 31]), reads=[pn, constf], writes=[RSTD])
                  P.op("dve", lambda h: h.reciprocal(out=RSTD[:, :], in_=RSTD[:, :]), reads=[RSTD], writes=[RSTD])
                  P.op("dve", lambda h: h.scalar_tensor_tensor(out=CKVS[:, t0:t0 + G], in0=XF[:, 0, :], scalar=smallf[:, SF_GKV:SF_GKV + 1],
                                                               in1=RSTD[:, :], op0=ALU.mult, op1=ALU.mult),
                       reads=[XF, smallf, RSTD], writes=[CKVS])
                  ckpt(1.5)
                  ps = psf.next()
                  for kc in range(8):
                      mm(ps, ps[0:32, :], wmisc[:, kc, 0:32], hnT[:, kc, :], [wmisc, hnT], start=(kc == 0), stop=(kc == 7))
                  rope(ps, ps[0:32, :], 32, PB_ap, CBt, SBt, KRT, KRT[0:32, t0:t0 + G])
                  ckpt(1.6)
                  for r in range(4):
                      tix = g * 4 + r
                      ps = psf.next()
                      for kc in range(8):
                          mm(ps, ps[:, 0:136], hnT[:, kc, r * 128:(r + 1) * 128], wmisc[:, kc, 32:168], [wmisc, hnT],
                             start=(kc == 0), stop=(kc == 7))
                      P.op("act", lambda h, ps=ps, tix=tix: h.activation(
                          out=AV[:, tix, :, 0:64], in_=ps[:, 0:128].rearrange("p (a b) -> p a b", a=2), func=AF.Copy),
                          reads=[ps], writes=[AV])
                      P.op("dve", lambda h, ps=ps, tix=tix: h.tensor_scalar(
                          out=IW[:, tix, :], in0=ps[:, 128:136], scalar1=0.125 * 8 ** -0.5, scalar2=None, op0=ALU.mult),
                          reads=[ps], writes=[IW])
                      ps2 = psf.next()
                      mm(ps2, ps2[:, :], CKVS[:, t0 + r * 128:t0 + (r + 1) * 128], wuv[:, :], [CKVS, wuv])
                      P.op("act", lambda h, ps2=ps2, tix=tix: h.activation(
                          out=VB[:, tix, :, 0:64], in_=ps2[:, :].rearrange("p (a b) -> p a b", a=8), func=AF.Copy),
                          reads=[ps2], writes=[VB])

                  ckpt(2)
                  nkt_g = 4 * g + 4

                  def gen_scores(r, ACCx):
                      qt = 4 * g + r
                      nk = 128 * (qt + 1)
                      for kb in range(0, nk, 512):
                          kw = min(512, nk - kb)
                          for hh in range(8):
                              c = hh // 2
                              pb_ = 64 * (hh % 2)
                              ps = psf.next()
                              mm(ps, ps[:, 0:kw], IQT[pb_:pb_ + 64, c, r * 128:(r + 1) * 128], IKT[pb_:pb_ + 64, kb:kb + kw],
                                 [IQT, IKT])
                              rl = RLr.next()
                              P.op("act", lambda h: h.activation(out=rl[:, 0:kw], in_=ps[:, 0:kw], func=AF.Relu),
                                   reads=[ps], writes=[rl])
                              if hh == 0:
                                  P.op("dve", lambda h: h.tensor_scalar(
                                      out=ACCx[:, kb:kb + kw], in0=rl[:, 0:kw], scalar1=IW[:, qt, hh:hh + 1], scalar2=None,
                                      op0=ALU.mult), reads=[rl, IW], writes=[ACCx])
                              else:
                                  P.op("dve", lambda h: h.scalar_tensor_tensor(
                                      out=ACCx[:, kb:kb + kw], in0=rl[:, 0:kw], scalar=IW[:, qt, hh:hh + 1],
                                      in1=ACCx[:, kb:kb + kw], op0=ALU.mult, op1=ALU.add), reads=[rl, IW, ACCx], writes=[ACCx])
                              yield 0.45

                  def gen_bisect(r, ACCx, BISx):
                      qt = 4 * g + r
                      nk = 128 * (qt + 1)
                      P.op("dve", lambda h: h.tensor_reduce(out=BISx[:, 0:1], in_=ACCx[:, 0:nk], axis=AX.X, op=ALU.max),
                           reads=[ACCx], writes=[BISx])
                      P.op("dve", lambda h: h.tensor_reduce(out=BISx[:, 1:2], in_=ACCx[:, 0:nk], axis=AX.X, op=ALU.min),
                           reads=[ACCx], writes=[BISx])
                      P.op("dve", lambda h: h.tensor_tensor(out=ACCx[:, nk - 128:nk], in0=ACCx[:, nk - 128:nk],
                                                            in1=constf[:, CF_NTRI:CF_NTRI + 128], op=ALU.add),
                           reads=[ACCx, constf], writes=[ACCx])
                      P.op("dve", lambda h: h.tensor_tensor(out=BISx[:, 2:3], in0=BISx[:, 0:1], in1=BISx[:, 1:2], op=ALU.subtract),
                           reads=[BISx], writes=[BISx])
                      P.op("dve", lambda h: h.scalar_tensor_tensor(out=BISx[:, 3:4], in0=BISx[:, 2:3], scalar=0.5, in1=BISx[:, 1:2],
                                                                   op0=ALU.mult, op1=ALU.add), reads=[BISx], writes=[BISx])
                      P.op("dve", lambda h: h.tensor_scalar(out=BISx[:, 8:8 + NIT], in0=constf[:, CF_POW:CF_POW + NIT],
                                                            scalar1=BISx[:, 2:3], scalar2=None, op0=ALU.mult),
                           reads=[BISx, constf], writes=[BISx])
                      P.op("dve", lambda h: h.tensor_scalar(out=BISx[:, 32:32 + NIT], in0=constf[:, CF_POW:CF_POW + NIT],
                                                            scalar1=BISx[:, 2:3], scalar2=-0.5, op0=ALU.mult, op1=ALU.mult),
                           reads=[BISx, constf], writes=[BISx])
                      for it in range(NIT):
                          yield 1.1
                          P.op("dve", lambda h: h.tensor_scalar(out=MQ[:, 0:nk], in0=ACCx[:, 0:nk], scalar1=BISx[:, 3:4],
                                                                scalar2=None, op0=ALU.is_ge, op1=ALU.add,
                                                                accum_out=BISx[:, 4:5]),
                               reads=[ACCx, BISx], writes=[MQ, BISx])
                          P.op("dve", lambda h: h.tensor_copy(out=BISx[:, 6:7], in_=BISx[:, 3:4]), reads=[], writes=[BISx])
                          P.op("dve", lambda h: h.tensor_scalar(out=BISx[:, 5:6], in0=BISx[:, 4:5], scalar1=float(TOPK) - 0.5,
                                                                scalar2=BISx[:, 8 + it:9 + it], op0=ALU.is_ge, op1=ALU.mult),
                               reads=[BISx], writes=[BISx])
                          if it < NIT - 1:
                              P.op("dve", lambda h: h.scalar_tensor_tensor(out=BISx[:, 3:4], in0=BISx[:, 5:6],
                                                                           scalar=BISx[:, 32 + it:33 + it], in1=BISx[:, 3:4],
                                                                           op0=ALU.add, op1=ALU.add),
                                   reads=[BISx], writes=[BISx])
                          else:
                              P.op("dve", lambda h: h.scalar_tensor_tensor(out=BISx[:, 3:4], in0=BISx[:, 5:6],
                                                                           scalar=BISx[:, 8 + it:9 + it], in1=BISx[:, 3:4],
                                                                           op0=ALU.subtract, op1=ALU.add),
                                   reads=[BISx], writes=[BISx])
                      P.op("dve", lambda h: h.tensor_scalar(out=MQ[:, 0:nk], in0=ACCx[:, 0:nk], scalar1=BISx[:, 3:4],
                                                            scalar2=None, op0=ALU.is_ge), reads=[ACCx, BISx], writes=[MQ])
                      for j0 in range(0, qt + 1, 8):
                          jn = min(8, qt + 1 - j0)
                          pb = psb.next()
                          for jj in range(jn):
                              j = j0 + jj
                              P.op("pe", lambda h: h.transpose(out=pb[:, jj * 128:(jj + 1) * 128],
                                                               in_=MQ[:, j * 128:(j + 1) * 128], identity=ident),
                                   reads=[MQ, constb], writes=[pb], inc=(jj == jn - 1))
                          P.op("act", lambda h: h.activation(
                              out=MT[:, j0:j0 + jn, r * 128:(r + 1) * 128],
                              in_=pb[:, 0:jn * 128].rearrange("p (a b) -> p a b", a=jn), func=AF.Copy),
                              reads=[pb], writes=[MT])
                      yield 0.5

                  def gen_indexer():
                      tiles = []
                      for r in range(4):
                          qt = 4 * g + r
                          if qt < 2:
                              for j in range(qt + 1):
                                  src = constb[:, CB_TRI:CB_TRI + 128] if j == qt else constb[:, CB_ONE:CB_ONE + 128]
                                  P.op("act", lambda h: h.activation(out=MT[:, j, r * 128:(r + 1) * 128], in_=src, func=AF.Copy),
                                       reads=[constb], writes=[MT])
                          else:
                              tiles.append(r)
                      for p0 in range(0, len(tiles), 2):
                          pair = [(r, (ACC, ACC2)[k], (BIS, BIS2)[k]) for k, r in enumerate(tiles[p0:p0 + 2])]
                          for (r, A_, B_) in pair:
                              for cst in gen_scores(r, A_):
                                  yield cst
                          sub = [[gen_bisect(*pr), 0.0, True] for pr in pair]
                          while any(x[2] for x in sub):
                              x = min([y for y in sub if y[2]], key=lambda q: q[1])
                              try:
                                  cst = next(x[0])
                                  x[1] += cst
                                  yield cst
                              except StopIteration:
                                  x[2] = False
                      yield 0.0

                  def gen_mla():
                      SK = 1
                      st = {}

                      def proj_stage(hh, stage):
                          if stage == 0:
                              ps = psf.next()
                              for rc in range(2):
                                  mm(ps, ps[0:64, :], wuq[:, rc, hh, 0:64], CQT[:, rc, :], [wuq, CQT], start=(rc == 0), stop=(rc == 1))
                              P.op("act", lambda h, ps=ps: h.activation(out=QNT[0:64, :], in_=ps[0:64, :], func=AF.Copy), reads=[ps], writes=[QNT])
                              psr = psf.next()
                              for rc in range(2):
                                  mm(psr, psr[0:32, :], wuq[:, rc, hh, 64:96], CQT[:, rc, :], [wuq, CQT], start=(rc == 0), stop=(rc == 1))
                              rope_a(psr, psr[0:32, :], 32, RXB2)
                              P.op("act", lambda h, psr=psr: h.activation(out=QRF[0:32, :], in_=psr[0:32, :], func=AF.Copy), reads=[psr], writes=[QRF])
                              st[hh] = {}
                          elif stage == 1:
                              ps = psf.next()
                              mm(ps, ps[:, :], wukT[0:64, hh, :], QNT[0:64, :], [wukT, QNT])
                              qab = QABr.next()
                              P.op("act", lambda h, ps=ps, qab=qab: h.activation(out=qab[:, :], in_=ps[:, :], func=AF.Copy), reads=[ps], writes=[qab])
                              qr = QRr.next()
                              rope_b(QRF, QRF[0:32, :], 32, PB_ap, CBt, SBt, qr, qr[0:32, :], RXB2)
                              st[hh].update(qab=qab, qr=qr)

                      proj_stage(0, 0)
                      proj_stage(0, 1)
                      pending = None
                      for hh in range(8):
                          qab, qr = st[hh]["qab"], st[hh]["qr"]
                          po = pso.next()
                          q = []
                          for jx in range(nkt_g + SK):
                              if jx < nkt_g:
                                  j = jx
                                  c0 = 128 * max(0, j - 4 * g)
                                  ps = psq.next()
                                  mm(ps, ps[:, c0:G], CKVS[:, j * 128:(j + 1) * 128], qab[:, c0:G], [CKVS, qab], start=True, stop=False)
                                  mm(ps, ps[:, c0:G], KRT[0:32, j * 128:(j + 1) * 128], qr[0:32, c0:G], [KRT, qr], start=False, stop=True)
                                  q.append((j, ps, c0))
                              if jx >= SK:
                                  j, ps, c0 = q.pop(0)
                                  pt = PTr.next()
                                  P.op("act", lambda h, ps=ps, pt=pt, c0=c0: h.activation(out=pt[:, c0:G], in_=ps[:, c0:G], func=AF.Exp,
                                                                                          scale=96 ** -0.5), reads=[ps], writes=[pt])
                                  if j >= 4 * g:
                                      P.op("dve", lambda h, pt=pt, c0=c0: h.tensor_tensor(out=pt[:, c0:c0 + 128], in0=pt[:, c0:c0 + 128],
                                                                                          in1=constb[:, CB_TRI:CB_TRI + 128], op=ALU.mult),
                                           reads=[pt, constb], writes=[pt])
                                  mm(po, po[0:65, c0:G], VB[:, j, hh, :], pt[:, c0:G], [VB, pt], start=(j == 0), stop=(j == nkt_g - 1))
                              if jx == 1 and hh + 1 < 8:
                                  proj_stage(hh + 1, 0)
                              if jx == 2 and pending is not None:
                                  pending()
                                  pending = None
                              if jx == 3 and hh + 1 < 8:
                                  proj_stage(hh + 1, 1)
                              yield 1.0
                          pending = (lambda po=po, hh=hh: normalize_heads(po, OBT, hh))
                      pending()
                      for c in range(4):
                          ps = proj_fm(c)
                          rope(ps, ps[:, :], 128, PA_ap, CA, SA, AQT, AQT[:, c, :])
                          yield 4.0
                      for kg in range(2):
                          ps = proj_fm(4 + kg)
                          rope(ps, ps[:, :], 128, PA_ap, CA, SA, AKT, AKT[:, kg, t0:t0 + G])
                          yield 4.0
                      yield 0.0

                  P.barrier()
                  psf.items = _banks[2:4]
                  gens = [[gen_indexer(), 0.0, True], [gen_mla(), 0.0, True]]
                  while any(gv[2] for gv in gens):
                      live = [gv for gv in gens if gv[2]]
                      gv = min(live, key=lambda q: q[1])
                      try:
                          gv[1] += next(gv[0])
                      except StopIteration:
                          gv[2] = False
                  ckpt(3)
                  SK = 2
                  psq.items = _banks[0:3]
                  psf.items = _banks[3:4]
                  pending = None
                  for hh in range(8):
                      c = hh // 2
                      pb_ = 64 * (hh % 2)
                      kg = hh // 4
                      po = pso.next()
                      q = []
                      for jx in range(nkt_g + SK):
                          if jx < nkt_g:
                              j = jx
                              c0 = 128 * max(0, j - 4 * g)
                              ps = psq.next()
                              mm(ps, ps[:, c0:G], AKT[pb_:pb_ + 64, kg, j * 128:(j + 1) * 128], AQT[pb_:pb_ + 64, c, c0:G], [AKT, AQT])
                              q.append((j, ps, c0))
                          if jx >= SK:
                              j, ps, c0 = q.pop(0)
                              pt = PTr.next()
                              P.op("act", lambda h, ps=ps, pt=pt, c0=c0: h.activation(out=pt[:, c0:G], in_=ps[:, c0:G], func=AF.Exp,
                                                                                      scale=0.125), reads=[ps], writes=[pt])
                              P.op("dve", lambda h, pt=pt, c0=c0, j=j: h.tensor_tensor(out=pt[:, c0:G], in0=pt[:, c0:G],
                                                                                       in1=MT[:, j, c0:G], op=ALU.mult),
                                   reads=[pt, MT], writes=[pt])
                              mm(po, po[0:65, c0:G], AV[:, j, kg, :], pt[:, c0:G], [AV, pt], start=(j == 0), stop=(j == nkt_g - 1))
                          if jx == 2 and pending is not None:
                              pending()
                              pending = None
                      pending = (lambda po=po, hh=hh: normalize_heads(po, OAT, hh))
                  pending()


                  psf.items = list(_banks)
                  psq.items = _banks[0:2]
                  ckpt(4)
                  if taps and b == nseq - 1 and g == TAPG:
                      P.dma("pool", tap_d["oat"][:, :], OAT[:].rearrange("p a b -> p (a b)"), reads=[OAT])
                      P.dma("pool", tap_d["obt"][:, :], OBT[:].rearrange("p a b -> p (a b)"), reads=[OBT])
                      P.dma("pool", tap_d["mt"][:, :], MT[:, :, :].rearrange("p a b -> p (a b)"), reads=[MT])
                      if "ckvs" in tap_d:
                          P.dma("pool", tap_d["ckvs"][:, :], CKVS[:, :], reads=[CKVS])
                          P.dma("pool", tap_d["krt"][:, :], KRT[:, :], reads=[KRT])
                          P.dma("pool", tap_d["vb"][:, :], VB[:].rearrange("p a b c -> p (a b c)"), reads=[VB])
                          P.dma("pool", tap_d["qab"][:, :], qab[:, :], reads=[qab])
                          P.dma("pool", tap_d["qr"][:, :], qr[0:32, :], reads=[qr])
                          P.dma("pool", tap_d["cqt"][:, :], CQT[:, :, :].rearrange("p a b -> p (a b)"), reads=[CQT])
                          P.dma("sp", tap_d["osb"][:, :], OSB[:, :], reads=[OSB])
                          P.dma("sp", tap_d["rec"][:, :], REC[:, :], reads=[REC])
                          P.dma("sp", tap_d["rstd"][:, :], RSTD[:, :], reads=[RSTD])
                          P.dma("sp", tap_d["xf0"][:, :], XF[:, 0, :], reads=[XF])
                          P.dma("sp", tap_d["sq"][:, :], SQ[:, :], reads=[SQ])
                  ckpt(5)
                  P.barrier()
                  psf.items = list(_banks) + list(pso.items)
                  for c in range(8):
                      wa = load_w(wsmall, lambda w: w[0:64, :], "wba", c)
                      pa = psf.next()
                      for hh in range(8):
                          mm(pa, pa[:, :], wa[0:64, hh * 128:(hh + 1) * 128], OAT[0:64, hh, :], [wa, OAT], start=(hh == 0), stop=(hh == 7))
                      wb = load_w(wsmall, lambda w: w[0:64, :], "wbb", c)
                      pb2 = psf.next()
                      for hh in range(8):
                          mm(pb2, pb2[:, :], wb[0:64, hh * 128:(hh + 1) * 128], OBT[0:64, hh, :], [wb, OBT], start=(hh == 0), stop=(hh == 7))
                      wga = load_w(wsmall, lambda w: w[:, :], "win", 14 + c)
                      pga = psf.next()
                      for kc in range(8):
                          mm(pga, pga[:, :], wga[:, kc * 128:(kc + 1) * 128], hnT[:, kc, :], [wga, hnT], start=(kc == 0), stop=(kc == 7))
                      P.op("act", lambda h, pga=pga: h.activation(out=SGA[:, :], in_=pga[:, :], func=AF.Sigmoid), reads=[pga], writes=[SGA])
                      wgb = load_w(wsmall, lambda w: w[:, :], "win", 22 + c)
                      pgb = psf.next()
                      for kc in range(8):
                          mm(pgb, pgb[:, :], wgb[:, kc * 128:(kc + 1) * 128], hnT[:, kc, :], [wgb, hnT], start=(kc == 0), stop=(kc == 7))
                      P.op("act", lambda h, pgb=pgb: h.activation(out=SGB[:, :], in_=pgb[:, :], func=AF.Sigmoid), reads=[pgb], writes=[SGB])
                      P.op("dve", lambda h, pa=pa: h.tensor_tensor(out=GT1[:, :], in0=pa[:, :], in1=SGA[:, :], op=ALU.mult),
                           reads=[pa, SGA], writes=[GT1])
                      P.op("dve", lambda h, pb2=pb2: h.tensor_tensor(out=GT2[:, :], in0=pb2[:, :], in1=SGB[:, :], op=ALU.mult),
                           reads=[pb2, SGB], writes=[GT2])
                      P.op("dve", lambda h, c=c: h.tensor_tensor(out=MRG[:, c, :], in0=GT1[:, :], in1=GT2[:, :], op=ALU.add),
                           reads=[GT1, GT2], writes=[MRG])
                  ckpt(6)
                  for nq in range(4):
                      w = load_w(wmid, lambda w: w[:, 0:2048], "wo", nq)
                      for r in range(4):
                          ps = psf.next()
                          for kc in range(8):
                              mm(ps, ps[:, 0:256], MRG[:, kc, r * 128:(r + 1) * 128], w[:, kc * 256:(kc + 1) * 256], [MRG, w],
                                 start=(kc == 0), stop=(kc == 7))
                          P.op("dve", lambda h, ps=ps, r=r, nq=nq: h.tensor_tensor(
                              out=XH[:, r, nq * 256:(nq + 1) * 256], in0=XH[:, r, nq * 256:(nq + 1) * 256], in1=ps[:, 0:256], op=ALU.add),
                              reads=[XHr[r], ps], writes=[XHr[r]])
                  for r in range(4):
                      rmsnorm_tok(XHr[r], XH[:, r, :], SF_GFFN, hn2T, r, HN2TOK)
                  ckpt(7)
                  for fh in range(2):
                      for fi in range(11):
                          f = fh * 11 + fi
                          w = load_w(wmid, lambda w: w[:, 0:2048], "wup", f)
                          pg = psf.next()
                          for kc in range(8):
                              mm(pg, pg[:, :], w[:, kc * 256:kc * 256 + 128], hn2T[:, kc, :], [w, hn2T], start=(kc == 0), stop=(kc == 7))
                          pv = psf.next()
                          for kc in range(8):
                              mm(pv, pv[:, :], w[:, kc * 256 + 128:kc * 256 + 256], hn2T[:, kc, :], [w, hn2T], start=(kc == 0), stop=(kc == 7))
                          res = []
                          for (pp, ch, ring_u, ring_a) in ((pg, f, UGr, AGr), (pv, 22 + f, UVr, AVr)):
                              ue = ring_u.next()
                              ac = ring_a.next()
                              cw = SF_CW + ch * 3
                              P.op("act", lambda h, ue=ue, ch=ch: h.activation(out=ue[:, 0:2], in_=CARRY[:, ch, :], func=AF.Copy),
                                   reads=[CARRY], writes=[ue])
                              P.op("act", lambda h, ue=ue, pp=pp: h.activation(out=ue[:, 2:514], in_=pp[:, :], func=AF.Copy),
                                   reads=[pp], writes=[ue])
                              P.op("act", lambda h, ac=ac, pp=pp, cw=cw, ch=ch: h.activation(
                                  out=ac[:, :], in_=pp[:, :], func=AF.Identity, scale=smallf[:, cw + 2:cw + 3],
                                  bias=smallf[:, SF_CB + ch:SF_CB + ch + 1]), reads=[pp, smallf], writes=[ac])
                              P.op("act", lambda h, ue=ue, ch=ch: h.activation(out=CARRY[:, ch, :], in_=ue[:, 512:514], func=AF.Copy),
                                   reads=[ue], writes=[CARRY])
                              P.op("dve", lambda h, ue=ue, ac=ac, cw=cw: h.scalar_tensor_tensor(
                                  out=ac[:, :], in0=ue[:, 1:513], scalar=smallf[:, cw + 1:cw + 2], in1=ac[:, :],
                                  op0=ALU.mult, op1=ALU.add), reads=[ue, smallf, ac], writes=[ac])
                              P.op("dve", lambda h, ue=ue, ac=ac, cw=cw: h.scalar_tensor_tensor(
                                  out=ac[:, :], in0=ue[:, 0:512], scalar=smallf[:, cw:cw + 1], in1=ac[:, :],
                                  op0=ALU.mult, op1=ALU.add), reads=[ue, smallf, ac], writes=[ac])
                              res.append(ac)
                          sg = SGr.next()
                          P.op("act", lambda h, sg=sg, a=res[0]: h.activation(out=sg[:, :], in_=a[:, :], func=AF.Silu),
                               reads=[res[0]], writes=[sg])
                          P.op("dve", lambda h, sg=sg, a=res[1], fi=fi: h.tensor_tensor(out=ACT_[:, fi, :], in0=sg[:, :], in1=a[:, :], op=ALU.mult),
                               reads=[sg, res[1]], writes=[ACT_])
                      for nq in range(4):
                          w = load_w(wmid, lambda w: w[:, :], "wd", fh * 4 + nq)
                          for r in range(4):
                              ps = psf.next()
                              for fi in range(11):
                                  mm(ps, ps[:, 0:256], ACT_[:, fi, r * 128:(r + 1) * 128], w[:, fi * 256:(fi + 1) * 256], [ACT_, w],
                                     start=(fi == 0), stop=(fi == 10))
                              P.op("dve", lambda h, ps=ps, r=r, nq=nq: h.tensor_tensor(
                                  out=XH[:, r, nq * 256:(nq + 1) * 256], in0=XH[:, r, nq * 256:(nq + 1) * 256], in1=ps[:, 0:256], op=ALU.add),
                                  reads=[XHr[r], ps], writes=[XHr[r]])
                  ckpt(8)
                  for r in range(4):
                      st = stats
                      P.op("act", lambda h, r=r: h.activation(out=HN2TOK[:, :], in_=XH[:, r, :], func=AF.Square, accum_out=st[:, 4:5]),
                           reads=[XHr[r]], writes=[HN2TOK, st])
                      P.op("act", lambda h: h.activation(out=st[:, 8:9], in_=st[:, 9:10], func=AF.Copy), reads=[], writes=[st])
                      P.op("act", lambda h: h.activation(out=st[:, 6:7], in_=st[:, 4:5], func=AF.Sqrt, scale=1.0 / D,
                                                         bias=constf[:, CF_POW + 30:CF_POW + 31]), reads=[st, constf], writes=[st])
                      P.op("dve", lambda h: h.reciprocal(out=st[:, 6:7], in_=st[:, 6:7]), reads=[st], writes=[st])
                      on = ONr[r % 2]
                      P.op("dve", lambda h, r=r: h.scalar_tensor_tensor(out=on[:, :], in0=XH[:, r, :], scalar=st[:, 6:7], in1=gfin[:, :],
                                                                        op0=ALU.mult, op1=ALU.mult), reads=[XHr[r], st, gfin], writes=[on])
                      d = P.dma("sp", out_d[b, t0 + r * 128:t0 + (r + 1) * 128, :], on[:, :], reads=[on])
                      out_deps.append(d)
                  if first_group[0]:
                      while pending_conv:
                          conv_one()
                      first_group[0] = False
        except _Stop:
            pass
        if taps and "c_wuq" in tap_d:
            P.dma("pool", tap_d["c_wuq"][:, :], wuq[:].rearrange("p a b c -> p (a b c)"), reads=[wuq])
            P.dma("pool", tap_d["c_wukT"][:, :], wukT[:].rearrange("p a b -> p (a b)"), reads=[wukT])
            P.dma("pool", tap_d["c_wuv"][:, :], wuv[:], reads=[wuv])
            P.dma("pool", tap_d["c_wmisc"][:, :], wmisc[:].rearrange("p a b -> p (a b)"), reads=[wmisc])
            P.dma("pool", tap_d["c_constb"][:, :], constb[:], reads=[constb])
            P.dma("sp", tap_d["c_constf"][:, :], constf[:], reads=[constf])
            P.dma("sp", tap_d["c_smallf"][:, :], smallf[:], reads=[smallf])
            P.dma("sp", tap_d["c_gfin"][:, :], gfin[:], reads=[gfin])
            out_deps += [(k, v) for k, v in P.dma_cnt.items()]
        P.wait_all("sp", out_deps)
        P.replay(es)
    return nc


def _consts():
    cb = np.zeros((128, CB_N), np.float32)
    cb[:, CB_ID:CB_ID + 128] = np.eye(128)
    pa = np.zeros((128, 128), np.float32)
    for hb in (0, 64):
        for j in range(8):
            pa[hb + j + 8, hb + j] = -1.0
            pa[hb + j, hb + j + 8] = 1.0
    cb[:, CB_PA:CB_PA + 128] = pa
    pb = np.zeros((32, 32), np.float32)
    for j in range(16):
        pb[j + 16, j] = -1.0
        pb[j, j + 16] = 1.0
    cb[0:32, CB_PB:CB_PB + 32] = pb
    k = np.arange(128)[:, None]
    q = np.arange(128)[None, :]
    cb[:, CB_TRI:CB_TRI + 128] = (k <= q).astype(np.float32)
    cb[:, CB_ONE:CB_ONE + 128] = 1.0
    cf = np.zeros((128, CF_N), np.float32)
    cf[:, CF_ONE:CF_ONE + 128] = 1.0
    bd = np.zeros((128, 128), np.float32)
    bd[0:64, 0:64] = 1.0
    bd[64:128, 64:128] = 1.0
    cf[:, CF_BD:CF_BD + 128] = bd
    cf[:, CF_NTRI:CF_NTRI + 128] = np.where(q.T >= k.T * 0 + np.arange(128)[None, :], 0.0, 0.0)
    qq = np.arange(128)[:, None]
    kk = np.arange(128)[None, :]
    cf[:, CF_NTRI:CF_NTRI + 128] = np.where(kk <= qq, 0.0, NEG)
    cf[:, CF_POW:CF_POW + 30] = 2.0 ** -(np.arange(30) + 1.0)
    cf[:, CF_POW + 30] = EPS
    cf[:, CF_POW + 31] = -0.5
    p = np.arange(128)
    j = p % 64
    inva = np.where(j < 16, THETA ** (-(2.0 * (j % 8)) / 16.0), 0.0)
    cf[:, CF_INVA] = inva / (2 * math.pi)
    invb = np.where(p < 32, THETA ** (-(2.0 * (p % 16)) / 32.0), 0.0)
    cf[:, CF_INVB] = invb / (2 * math.pi)
    cf[:, CF_NPI] = -math.pi * (1.0 - 2e-6)
    return cb, cf


def _prep_weights(norm_mix_g, w_in, idx_k_norm_g, q_a_norm_g, kv_a_norm_g, w_uq, w_uk, w_uv,
                  w_branch_a, w_branch_b, w_out, norm_ffn_g, w_up, conv_w, conv_b, w_down, norm_final_g):
    f = lambda a: np.ascontiguousarray(np.asarray(a, dtype=np.float32))
    w_in = f(w_in)[0]
    cols = []
    for c in range(4):
        cols.append(np.arange(c * 128, (c + 1) * 128))
    for kg in range(2):
        base = 512 + kg * 64
        cols.append(np.concatenate([np.arange(base, base + 64)] * 2))
    for c in range(4):
        cols.append(np.arange(768 + c * 128, 768 + (c + 1) * 128))
    cols.append(np.concatenate([np.arange(1280, 1344)] * 2))
    for c in range(2):
        cols.append(np.arange(1352 + c * 128, 1352 + (c + 1) * 128))
    cols.append(np.arange(1608, 1736))
    for c in range(8):
        cols.append(np.arange(1768 + c * 128, 1768 + (c + 1) * 128))
    for c in range(8):
        cols.append(np.arange(2792 + c * 128, 2792 + (c + 1) * 128))
    assert len(cols) == 30

    def fm(wc):
        M = wc.shape[1]
        return np.ascontiguousarray(wc.reshape(8, 128, M).transpose(1, 0, 2)).reshape(128, 8 * M)
    win = np.stack([fm(w_in[:, c]) for c in cols])
    misc_cols = np.concatenate([np.arange(1736, 1768), np.arange(640, 768), np.arange(1344, 1352)])
    wmisc = fm(w_in[:, misc_cols])
    wuq = np.ascontiguousarray(f(w_uq)[0].reshape(2, 128, 8, 96).transpose(1, 0, 2, 3)).reshape(128, 2 * 8 * 96)
    wukT = np.ascontiguousarray(f(w_uk)[0].transpose(2, 1, 0)).reshape(64, 8 * 128)
    wuv = f(w_uv)[0].reshape(128, 512)

    def br(w):
        w = f(w)[0].reshape(8, 64, 8, 128)
        return np.ascontiguousarray(w.transpose(2, 1, 0, 3)).reshape(8, 64, 8 * 128)
    wba, wbb = br(w_branch_a), br(w_branch_b)
    wo = np.ascontiguousarray(f(w_out)[0].reshape(8, 128, 4, 256).transpose(2, 1, 0, 3)).reshape(4, 128, 8 * 256)
    wu = f(w_up)[0].reshape(8, 128, 2, 22, 128)
    wup = np.ascontiguousarray(wu.transpose(3, 1, 0, 2, 4)).reshape(22, 128, 8 * 256)
    wdn = f(w_down)[0].reshape(2, 11, 128, 4, 256)
    wd = np.ascontiguousarray(wdn.transpose(0, 3, 2, 1, 4)).reshape(8, 128, 11 * 256)
    sf = np.zeros((128, SF_N), np.float32)
    sf[:, SF_GMIX:SF_GMIX + 8] = f(norm_mix_g)[0].reshape(8, 128).T
    sf[:, SF_GFFN:SF_GFFN + 8] = f(norm_ffn_g)[0].reshape(8, 128).T
    sf[:, SF_GIK] = np.concatenate([f(idx_k_norm_g)[0]] * 2)
    sf[:, SF_GQ:SF_GQ + 2] = f(q_a_norm_g)[0].reshape(2, 128).T
    sf[:, SF_GKV] = f(kv_a_norm_g)[0]
    cw = f(conv_w)[0].reshape(3, 44, 128)
    sf[:, SF_CW:SF_CW + 132] = cw.transpose(2, 1, 0).reshape(128, 132)
    sf[:, SF_CB:SF_CB + 44] = f(conv_b)[0].reshape(44, 128).T
    gfin = f(norm_final_g).reshape(1, D)
    return dict(win=win, wmisc=wmisc, wuq=wuq, wukT=wukT, wuv=wuv, wba=wba, wbb=wbb, wo=wo, wup=wup, wd=wd,
                smallf=sf, gfin=gfin)


_CACHE = {}


def kernel(x, positions, **weights):
    x = np.asarray(x, dtype=np.float32)
    positions = np.asarray(positions, dtype=np.int32)
    shared = _prep_weights(**weights)
    cb, cf = _consts()
    shared["constb"] = cb
    shared["constf"] = cf
    if "nc" not in _CACHE:
        _CACHE["nc"] = build_program()
    nc = _CACHE["nc"]
    in_maps = []
    for c in range(NCORES):
        m = dict(shared)
        m["x"] = np.ascontiguousarray(x[c * NSEQ:(c + 1) * NSEQ])
        m["pos"] = np.ascontiguousarray(positions[c * NSEQ:(c + 1) * NSEQ])
        in_maps.append(m)
    res = run_bass_kernel_spmd(nc, in_maps, core_ids=list(range(NCORES)))
    return np.concatenate([r["out"] for r in res.results], axis=0).astype(np.float32)
```

```python
import math
from contextlib import ExitStack

import ml_dtypes
import numpy as np

import concourse.bass as bass
import concourse.mybir as mybir
from concourse.bass_utils import run_bass_kernel_spmd

F32, BF16, I32 = mybir.dt.float32, mybir.dt.bfloat16, mybir.dt.int32
AF = mybir.ActivationFunctionType
ALU = mybir.AluOpType
AX = mybir.AxisListType

D = 1024
S = 2048
G = 512
NCORES = 8
NSEQ = 2
THETA = 500000.0
EPS = 1e-6
NIT = 14
TOPK = 256
DFF = 2816
NEG = -1.0e30
TAPG = 0

CB_ID, CB_PA, CB_PB, CB_TRI, CB_ONE, CB_N = 0, 128, 256, 288, 416, 544
CF_ONE, CF_BD, CF_NTRI, CF_POW, CF_INVA, CF_INVB, CF_NPI, CF_N = 0, 128, 256, 384, 416, 417, 418, 420
SF_GMIX, SF_GFFN, SF_GIK, SF_GQ, SF_GKV, SF_CW, SF_CB, SF_N = 0, 8, 16, 17, 19, 20, 152, 196


class Tk:
    def __init__(self, t, name=""):
        self.t = t
        self.name = name
        self.w = None
        self.r = {}

    def __getitem__(self, k):
        return self.t[k]


class Eng:
    def __init__(self, name):
        self.name = name
        self.ops = []
        self.cnt = 0
        self.seen = {}


class _Rec:
    def __getattr__(self, name):
        return lambda *a, **k: (name, a, k)


class Prog:
    CE = ("pe", "act", "dve", "pool")

    def __init__(self, nc):
        self.nc = nc
        self.E = {n: Eng(n) for n in ("pe", "act", "dve", "pool", "sp")}
        self.ndma = 0
        self.dma_cnt = {}
        self.semnames = set(self.CE)

    def _deps(self, eng, reads, writes):
        deps = {}

        def add(d):
            if d is None:
                return
            k, v = d
            if deps.get(k, 0) < v:
                deps[k] = v
        for t in reads:
            add(t.w)
            if getattr(t, "psum", False):
                for d in t.r.values():
                    if d[0] != eng.name:
                        add(d)
        same_ok = eng.name == "pe"
        for t in writes:
            if t.w is not None and (t.w[0] != eng.name or not same_ok):
                add(t.w)
            for d in t.r.values():
                if d[0] != eng.name or not same_ok:
                    add(d)
        for k, v in deps.items():
            if eng.seen.get(k, 0) < v:
                if k in self.CE:
                    assert v <= self.E[k].cnt, (k, v, self.E[k].cnt)
                eng.ops.append(("wait", k, v))
                eng.seen[k] = v

    def op(self, en, fn, reads=(), writes=(), inc=True):
        eng = self.E[en]
        name_, a_, k_ = fn(_Rec())
        fn = lambda h, name_=name_, a_=a_, k_=k_: getattr(h, name_)(*a_, **k_)
        self._deps(eng, reads, writes)
        if inc:
            eng.cnt += 1
            idx = eng.cnt
            eng.ops.append(("op", fn, en, 1))
        else:
            idx = eng.cnt + 1
            eng.ops.append(("op", fn, None, 0))
        d = (en, idx)
        for t in reads:
            t.r[en] = d
        for t in writes:
            t.w = d
            t.r = {}

    def dma(self, qn, out_ap, in_ap, reads=(), writes=()):
        eng = self.E[qn]
        self._deps(eng, reads, writes)
        tgt = writes[0] if writes else reads[0]
        if not hasattr(tgt, "dsem"):
            tgt.dsem = {}
        if qn not in tgt.dsem:
            tgt.dsem[qn] = "d%d" % self.ndma
            self.ndma += 1
        sem = tgt.dsem[qn]
        self.semnames.add(sem)
        v = self.dma_cnt.get(sem, 0) + 16
        self.dma_cnt[sem] = v
        eng.ops.append(("op", lambda h, o=out_ap, i=in_ap: h.dma_start(out=o, in_=i), sem, 16))
        d = (sem, v)
        for t in reads:
            t.r[sem] = d
        for t in writes:
            t.w = d
            t.r = {}
        return d

    def barrier(self, extra=()):
        for a in self.CE:
            ea = self.E[a]
            for t in extra:
                for d in list(t.r.values()) + ([t.w] if t.w else []):
                    if d[0] not in self.CE and ea.seen.get(d[0], 0) < d[1]:
                        ea.ops.append(("wait", d[0], d[1]))
                        ea.seen[d[0]] = d[1]
            for b in self.CE:
                if a == b:
                    continue
                v = self.E[b].cnt
                if v > 0 and ea.seen.get(b, 0) < v:
                    ea.ops.append(("wait", b, v))
                    ea.seen[b] = v

    def wait_all(self, en, deps):
        eng = self.E[en]
        for k, v in deps:
            if eng.seen.get(k, 0) < v:
                eng.ops.append(("wait", k, v))
                eng.seen[k] = v

    def replay(self, es):
        nc = self.nc
        sems = {}
        for n in sorted(self.semnames):
            sems[n] = es.enter_context(nc.semaphore("s_" + n))
        block = es.enter_context(nc.Block())

        def run(eng):
            def f(h):
                for o in eng.ops:
                    if o[0] == "wait":
                        h.wait_ge(sems[o[1]], o[2])
                    else:
                        ins = o[1](h)
                        if o[2] is not None:
                            ins.then_inc(sems[o[2]], o[3])
            return f
        block.tensor(run(self.E["pe"]))
        block.scalar(run(self.E["act"]))
        block.vector(run(self.E["dve"]))
        block.gpsimd(run(self.E["pool"]))
        block.sync(run(self.E["sp"]))


class Ring:
    def __init__(self, items):
        self.items = items
        self.i = 0

    def next(self):
        t = self.items[self.i % len(self.items)]
        self.i += 1
        return t


class _Stop(Exception):
    pass


def build_program(nseq=NSEQ, ngroups=S // G, taps=None, stage=99):
    nc = bass.Bass("TRN2", target_bir_lowering=False)
    SL = ngroups * G

    def din(name, shape, dt=F32):
        return nc.dram_tensor(name, list(shape), dt, kind="ExternalInput").ap()

    x_d = din("x", [nseq, SL, D])
    pos_d = din("pos", [nseq, SL], I32)
    constb_d = din("constb", [128, CB_N])
    constf_d = din("constf", [128, CF_N])
    smallf_d = din("smallf", [128, SF_N])
    gfin_d = din("gfin", [1, D])
    win_d = din("win", [30, 128, 8 * 128])
    wmisc_d = din("wmisc", [128, 8 * 168])
    wuq_d = din("wuq", [128, 2 * 8 * 96])
    wukT_d = din("wukT", [64, 8 * 128])
    wuv_d = din("wuv", [128, 512])
    wba_d = din("wba", [8, 64, 8 * 128])
    wbb_d = din("wbb", [8, 64, 8 * 128])
    wo_d = din("wo", [4, 128, 8 * 256])
    wup_d = din("wup", [22, 128, 8 * 256])
    wd_d = din("wd", [8, 128, 11 * 256])
    out_d = nc.dram_tensor("out", [nseq, SL, D], F32, kind="ExternalOutput").ap()
    tap_d = {}
    if taps:
        for k, shp in taps.items():
            tap_d[k] = nc.dram_tensor("tap_" + k, list(shp), F32, kind="ExternalOutput").ap()

    es = ExitStack()
    with es:
        P = Prog(nc)

        def SB(name, shape, dt):
            return Tk(es.enter_context(nc.sbuf_tensor("sb_" + name, list(shape), dt)), name)

        def PSM(name, shape, dt):
            t = Tk(es.enter_context(nc.psum_tensor(name, list(shape), dt)), name)
            t.psum = True
            return t

        constb = SB("constb", [128, CB_N], BF16)
        constf = SB("constf", [128, CF_N], F32)
        smallf = SB("smallf", [128, SF_N], F32)
        gfin = SB("gfin", [128, D], F32)
        wmisc = SB("wmisc", [128, 8, 168], BF16)
        wuq = SB("wuq", [128, 2, 8, 96], BF16)
        wukT = SB("wukT", [64, 8, 128], BF16)
        wuv = SB("wuv", [128, 512], BF16)
        AKT = SB("AKT", [128, 2, S], BF16)
        IKT = SB("IKT", [128, S], BF16)
        AV = SB("AV", [128, 16, 2, 65], BF16)
        CKVS = SB("CKVS", [128, S], BF16)
        KRT = SB("KRT", [32, S], BF16)
        VB = SB("VB", [128, 16, 8, 65], BF16)
        IW = SB("IW", [128, 16, 8], F32)
        CARRY = SB("CARRY", [128, 44, 2], F32)
        XH = SB("XH", [128, 4, D], F32)
        XHr = [Tk(XH.t, "XH%d" % i) for i in range(4)]
        hnT = SB("hnT", [128, 8, G], BF16)
        OAT = SB("OAT", [64, 8, G], BF16)
        OBT = SB("OBT", [64, 8, G], BF16)
        stats = SB("stats", [128, 16], F32)
        wsmall = Ring([SB("wsm%d" % i, [128, 1024], BF16) for i in range(3)])
        wmid = Ring([SB("wmid%d" % i, [128, 2816], BF16) for i in range(3)])
        SCRN = 20864
        scr = es.enter_context(nc.sbuf_tensor("scr", [128, SCRN], F32))
        off = [0]
        hiw = [0]

        def carve(name, nwords_unused, dt, shape):
            nel = int(np.prod(shape[1:]))
            nwords = (nel + 1) // 2 if dt == BF16 else nel
            a = off[0]
            off[0] += nwords
            assert off[0] <= SCRN, (name, off[0])
            hiw[0] = max(hiw[0], off[0])
            v = scr[:, a:a + nwords]
            if dt == BF16:
                v = v.bitcast(BF16)
            return Tk(_View(v, shape), name)

        class _View:
            def __init__(self, ap, shape):
                self.ap = ap
                if len(shape) == 3:
                    self.ap = ap.rearrange("p (a b) -> p a b", a=shape[1])
                elif len(shape) == 4:
                    self.ap = ap.rearrange("p (a b c) -> p a b c", a=shape[1], b=shape[2])

            def __getitem__(self, k):
                return self.ap[k]

        off[0] = 0
        AQT = carve("AQT", 2048, BF16, [128, 4, G])
        IQT = carve("IQT", 2048, BF16, [128, 4, G])
        MT = carve("MT", 4096, BF16, [128, 16, G])
        ACC = carve("ACC", 2048, F32, [128, S])
        MQ = carve("MQ", 1024, BF16, [128, S])
        PTr = Ring([carve("PT%d" % i, 256, BF16, [128, G]) for i in range(3)])
        CA = carve("CA", 512, F32, [128, G])
        SA = carve("SA", 512, F32, [128, G])
        CBt = carve("CBt", 512, F32, [128, G])
        SBt = carve("SBt", 512, F32, [128, G])
        RXB = carve("RXB", 256, BF16, [128, G])
        RT1 = carve("RT1", 512, F32, [128, G])
        RT2 = carve("RT2", 512, F32, [128, G])
        xf_off = [off[0]]
        XF = carve("XF", 1024, F32, [128, 2, G])
        SQ = carve("SQ", 512, F32, [128, G])
        RSTD = carve("RSTD", 512, F32, [128, G])
        rstd_end = [off[0]]
        CQT = carve("CQT", 512, BF16, [128, 2, G])
        RLr = Ring([carve("RL%d" % i, 512, F32, [128, G]) for i in range(2)])
        QNT = carve("QNT", 256, BF16, [128, G])
        QABr = Ring([carve("QAB%d" % i, 256, BF16, [128, G]) for i in range(2)])
        QRr = Ring([carve("QR%d" % i, 256, BF16, [128, G]) for i in range(2)])
        OSB = carve("OSB", 512, F32, [128, G])
        REC = carve("REC", 512, F32, [128, G])
        NRB = carve("NRB", 256, BF16, [128, G])
        RXB2 = carve("RXB2", 256, BF16, [128, G])
        QRF = carve("QRF", 512, F32, [128, G])
        POSF, TT1, TT2 = SQ, RT1, RT2
        HNTOK = carve("HNTOK", 512, BF16, [128, D])
        BIS = carve("BIS", 64, F32, [128, 64])
        BIS2 = carve("BIS2", 64, F32, [128, 64])
        p1_end = off[0]
        off[0] = xf_off[0]
        ACC2 = carve("ACC2", 2048, F32, [128, S])
        assert off[0] == rstd_end[0], (off[0], rstd_end[0])
        off[0] = p1_end
        off[0] = 0
        MRG = carve("MRG", 2048, BF16, [128, 8, G])
        hn2T = carve("hn2T", 2048, BF16, [128, 8, G])
        ACT_ = carve("ACTT", 2816, BF16, [128, 11, G])
        UGr = Ring([carve("UG%d" % i, 516, F32, [128, 516]) for i in range(2)])
        UVr = Ring([carve("UV%d" % i, 516, F32, [128, 516]) for i in range(2)])
        AGr = Ring([carve("AG%d" % i, 512, F32, [128, G]) for i in range(2)])
        AVr = Ring([carve("AVV%d" % i, 512, F32, [128, G]) for i in range(2)])
        SGr = Ring([carve("SG%d" % i, 512, F32, [128, G]) for i in range(2)])
        GT1 = carve("GT1", 512, F32, [128, G])
        GT2 = carve("GT2", 512, F32, [128, G])
        SGA = carve("SGA", 256, BF16, [128, G])
        SGB = carve("SGB", 256, BF16, [128, G])
        HN2TOK = carve("HN2TOK", 512, BF16, [128, D])
        ONORM = carve("ONORM", 1024, F32, [128, D])
        ONORM2 = carve("ONORM2", 1024, F32, [128, D])
        ONr = [ONORM, ONORM2]

        _banks = [PSM("psf%d" % i, [128, 512], F32) for i in range(4)]
        psf = Ring(list(_banks))
        psq = Ring(_banks[0:2])
        pso = Ring([PSM("pso%d" % i, [128, 512], F32) for i in range(2)])
        psb = Ring([PSM("psb%d" % i, [128, 1024], BF16) for i in range(2)])

        posi = SB("posi", [128, G], I32)
        tint = SB("tint", [128, G], I32)

        ident = constb[:, CB_ID:CB_ID + 128]

        P.dma("sp", constf[:], constf_d[:, :], writes=[constf])
        P.dma("sp", smallf[:], smallf_d[:, :], writes=[smallf])
        P.dma("sp", gfin[:], gfin_d[0:1, :].partition_broadcast(128), writes=[gfin])
        P.dma("pool", constb[:], constb_d[:, :], writes=[constb])
        P.dma("pool", wmisc[:].rearrange("p a b -> p (a b)"), wmisc_d[:, :], writes=[wmisc])
        P.dma("pool", wuq[:].rearrange("p a b c -> p (a b c)"), wuq_d[:, :], writes=[wuq])
        P.dma("pool", wukT[:].rearrange("p a b -> p (a b)"), wukT_d[:, :], writes=[wukT])
        P.dma("pool", wuv[:], wuv_d[:, :], writes=[wuv])
        for tix in range(16):
            for hh in range(2):
                P.op("dve", lambda h, tix=tix, hh=hh: h.memset(AV[:, tix, hh, 64:65], 1.0), writes=[AV])
            for hh in range(8):
                P.op("dve", lambda h, tix=tix, hh=hh: h.memset(VB[:, tix, hh, 64:65], 1.0), writes=[VB])

        P.op("dve", lambda h: h.memset(stats[:, :], 0.0), writes=[stats])

        def mm(ps_t, out_ap, lhsT_ap, rhs_ap, reads, start=True, stop=True):
            P.op("pe", lambda h: h.matmul(out_ap, lhsT_ap, rhs_ap, start=start, stop=stop),
                 reads=reads, writes=[ps_t], inc=stop)

        wsrc = {"win": win_d, "wba": wba_d, "wbb": wbb_d, "wo": wo_d, "wup": wup_d, "wd": wd_d}
        wscr = {}
        pending_conv = []
        for nm_ in ("win", "wba", "wbb", "wo", "wup", "wd"):
            shp = list(wsrc[nm_].shape)
            scr_ap = nc.dram_tensor(nm_ + "_bf", shp, BF16).ap()
            wscr[nm_] = (scr_ap, Tk(None, nm_ + "_bf"))
            for i_ in range(shp[0]):
                pending_conv.append((nm_, i_))
        first_group = [True]

        def conv_one():
            if pending_conv:
                nm_, i_ = pending_conv.pop(0)
                scr_ap, tk_ = wscr[nm_]
                P.dma("pool", scr_ap[i_, :, :], wsrc[nm_][i_, :, :], writes=[tk_])

        def load_w(ring, view_fn, nm_, i_):
            w = ring.next()
            if first_group[0]:
                P.dma("pool", view_fn(w), wsrc[nm_][i_, :, :], writes=[w])
                conv_one()
            else:
                scr_ap, tk_ = wscr[nm_]
                P.dma("sp", view_fn(w), scr_ap[i_, :, :], reads=[tk_], writes=[w])
            return w

        def rmsnorm_tok(src_t, src_ap, gcol, dstT, r, tokbuf):
            st = stats
            P.op("act", lambda h: h.activation(out=tokbuf[:, :], in_=src_ap, func=AF.Square,
                                               accum_out=st[:, 0:1]),
                 reads=[src_t], writes=[tokbuf, st])
            P.op("act", lambda h: h.activation(out=st[:, 8:9], in_=st[:, 9:10], func=AF.Copy), reads=[], writes=[st])
            P.op("act", lambda h: h.activation(out=st[:, 2:3], in_=st[:, 0:1], func=AF.Sqrt, scale=1.0 / D,
                                               bias=constf[:, CF_POW + 30:CF_POW + 31]), reads=[st, constf], writes=[st])
            P.op("dve", lambda h: h.reciprocal(out=st[:, 2:3], in_=st[:, 2:3]), reads=[st], writes=[st])
            P.op("dve", lambda h: h.tensor_scalar(out=tokbuf[:, :], in0=src_ap, scalar1=st[:, 2:3], scalar2=None,
                                                  op0=ALU.mult), reads=[src_t, st], writes=[tokbuf])
            pb = psb.next()
            for kc in range(8):
                P.op("pe", lambda h, kc=kc: h.transpose(out=pb[:, kc * 128:(kc + 1) * 128],
                                                        in_=tokbuf[:, kc * 128:(kc + 1) * 128], identity=ident),
                     reads=[tokbuf, constb], writes=[pb], inc=(kc == 7))
            for kc in range(8):
                eng = "act" if kc % 2 == 0 else "dve"
                if eng == "act":
                    P.op("act", lambda h, kc=kc: h.activation(out=dstT[:, kc, r * 128:(r + 1) * 128],
                                                              in_=pb[:, kc * 128:(kc + 1) * 128], func=AF.Copy,
                                                              scale=smallf[:, gcol + kc:gcol + kc + 1]),
                         reads=[pb, smallf], writes=[dstT])
                else:
                    P.op("dve", lambda h, kc=kc: h.tensor_scalar(out=dstT[:, kc, r * 128:(r + 1) * 128],
                                                                 in0=pb[:, kc * 128:(kc + 1) * 128],
                                                                 scalar1=smallf[:, gcol + kc:gcol + kc + 1], scalar2=None,
                                                                 op0=ALU.mult),
                         reads=[pb, smallf], writes=[dstT])

        def rope(src_t, src_ap, np_, Pm_ap, Ct, St, dst_t, dst_ap):
            P.op("act", lambda h: h.activation(out=RXB[0:np_, :], in_=src_ap, func=AF.Copy),
                 reads=[src_t], writes=[RXB])
            ckpt(1.011)
            ps2 = psf.next()
            mm(ps2, ps2[0:np_, :], Pm_ap, RXB[0:np_, :], [constb, RXB])
            ckpt(1.012)
            P.op("dve", lambda h: h.tensor_tensor(out=RT1[0:np_, :], in0=src_ap, in1=Ct[0:np_, :], op=ALU.mult),
                 reads=[src_t, Ct, RXB], writes=[RT1])
            ckpt(1.013)
            P.op("dve", lambda h: h.tensor_tensor(out=RT2[0:np_, :], in0=ps2[0:np_, :], in1=St[0:np_, :], op=ALU.mult),
                 reads=[ps2, St], writes=[RT2])
            ckpt(1.014)
            P.op("dve", lambda h: h.tensor_tensor(out=dst_ap, in0=RT1[0:np_, :], in1=RT2[0:np_, :], op=ALU.add),
                 reads=[RT1, RT2], writes=[dst_t])

        def rope_a(src_t, src_ap, np_, rxb):
            P.op("act", lambda h: h.activation(out=rxb[0:np_, :], in_=src_ap, func=AF.Copy), reads=[src_t], writes=[rxb])

        def rope_b(src_t, src_ap, np_, Pm_ap, Ct, St, dst_t, dst_ap, rxb):
            ps2 = psf.next()
            mm(ps2, ps2[0:np_, :], Pm_ap, rxb[0:np_, :], [constb, rxb])
            P.op("dve", lambda h: h.tensor_tensor(out=RT1[0:np_, :], in0=src_ap, in1=Ct[0:np_, :], op=ALU.mult),
                 reads=[src_t, Ct, rxb], writes=[RT1])
            P.op("dve", lambda h: h.tensor_tensor(out=RT2[0:np_, :], in0=ps2[0:np_, :], in1=St[0:np_, :], op=ALU.mult),
                 reads=[ps2, St], writes=[RT2])
            P.op("dve", lambda h: h.tensor_tensor(out=dst_ap, in0=RT1[0:np_, :], in1=RT2[0:np_, :], op=ALU.add),
                 reads=[RT1, RT2], writes=[dst_t])


        def trig_table(dst, invcol, offs):
            P.op("dve", lambda h: h.tensor_scalar(out=TT1[:, :], in0=POSF[:, :], scalar1=constf[:, invcol:invcol + 1],
                                                  scalar2=offs, op0=ALU.mult, op1=ALU.add),
                 reads=[POSF, constf], writes=[TT1])
            P.op("dve", lambda h: h.tensor_copy(out=tint[:], in_=TT1[:, :]), reads=[TT1], writes=[tint])
            P.op("dve", lambda h: h.tensor_tensor(out=TT2[:, :], in0=TT1[:, :], in1=tint[:], op=ALU.subtract),
                 reads=[TT1, tint], writes=[TT2])
            P.op("dve", lambda h: h.scalar_tensor_tensor(out=TT1[:, :], in0=TT2[:, :], scalar=0.0, in1=TT2[:, :],
                                                         op0=ALU.is_lt, op1=ALU.add), reads=[TT2], writes=[TT1])
            P.op("act", lambda h: h.activation(out=dst[:, :], in_=TT1[:, :], func=AF.Sin,
                                               bias=constf[:, CF_NPI:CF_NPI + 1], scale=2.0 * math.pi * (1.0 - 2e-6)),
                 reads=[TT1, constf], writes=[dst])

        def normalize_heads(po, dstT, h_):
            P.op("act", lambda h: h.activation(out=OSB[64:65, :], in_=po[64:65, :], func=AF.Ln), reads=[po], writes=[OSB])
            P.op("act", lambda h: h.activation(out=NRB[64:65, :], in_=OSB[64:65, :], func=AF.Exp, scale=-1.0),
                 reads=[OSB], writes=[NRB])
            pd = psf.next()
            mm(pd, pd[0:64, :], constb[64:65, CB_ONE:CB_ONE + 64], NRB[64:65, :], [constb, NRB])
            P.op("act", lambda h: h.activation(out=REC[0:64, :], in_=pd[0:64, :], func=AF.Copy), reads=[pd], writes=[REC])
            P.op("dve", lambda h: h.tensor_tensor(out=dstT[0:64, h_, :], in0=po[0:64, :], in1=REC[0:64, :], op=ALU.mult),
                 reads=[po, REC], writes=[dstT])


        out_deps = []

        def ckpt(n):
            if stage <= n:
                raise _Stop()
        try:
          for b in range(nseq):
              P.op("dve", lambda h: h.memset(CARRY[:, :, :], 0.0), writes=[CARRY])
              for g in range(ngroups):
                  t0 = g * G
                  P.barrier(extra=[ONORM, ONORM2])
                  psf.items = list(_banks)
                  for r in range(4):
                      P.dma("sp", XH[:, r, :], x_d[b, t0 + r * 128:t0 + (r + 1) * 128, :], writes=[XHr[r]])
                  P.dma("sp", posi[:], pos_d[b:b + 1, t0:t0 + G].partition_broadcast(128), writes=[posi])
                  P.op("dve", lambda h: h.tensor_copy(out=POSF[:, :], in_=posi[:]), reads=[posi], writes=[POSF])
                  for r in range(4):
                      rmsnorm_tok(XHr[r], XH[:, r, :], SF_GMIX, hnT, r, HNTOK)
                  trig_table(SA, CF_INVA, 0.5)
                  trig_table(CA, CF_INVA, 0.75)
                  trig_table(SBt, CF_INVB, 0.5)
                  trig_table(CBt, CF_INVB, 0.75)

                  ckpt(1)

                  def proj_fm(piece, M=128):
                      w = load_w(wsmall, lambda w: w[:, :], "win", piece)
                      ps = psf.next()
                      for kc in range(8):
                          mm(ps, ps[0:M, :], w[:, kc * 128:kc * 128 + M], hnT[:, kc, :], [w, hnT],
                             start=(kc == 0), stop=(kc == 7))
                      return ps

                  PA_ap = constb[:, CB_PA:CB_PA + 128]
                  PB_ap = constb[0:32, CB_PB:CB_PB + 32]
                  ckpt(1.1)
                  for c in range(4):
                      ps = proj_fm(6 + c)
                      rope(ps, ps[:, :], 128, PA_ap, CA, SA, IQT, IQT[:, c, :])
                  ckpt(1.2)
                  ps = proj_fm(10)
                  P.op("act", lambda h, ps=ps: h.activation(out=SQ[:, :], in_=ps[:, :], func=AF.Square), reads=[ps], writes=[SQ])
                  P.op("act", lambda h, ps=ps: h.activation(out=XF[:, 0, :], in_=ps[:, :], func=AF.Copy), reads=[ps], writes=[XF])
                  pn = psf.next()
                  mm(pn, pn[:, :], constf[:, CF_BD:CF_BD + 128], SQ[:, :], [constf, SQ])
                  P.op("act", lambda h, pn=pn: h.activation(out=RSTD[:, :], in_=pn[:, :], func=AF.Sqrt, scale=1.0 / 64,
                                                            bias=constf[:, CF_POW + 30:CF_POW + 31]), reads=[pn, constf], writes=[RSTD])
                  P.op("dve", lambda h: h.reciprocal(out=RSTD[:, :], in_=RSTD[:, :]), reads=[RSTD], writes=[RSTD])
                  P.op("dve", lambda h: h.scalar_tensor_tensor(out=XF[:, 1, :], in0=XF[:, 0, :], scalar=smallf[:, SF_GIK:SF_GIK + 1],
                                                               in1=RSTD[:, :], op0=ALU.mult, op1=ALU.mult),
                       reads=[XF, smallf, RSTD], writes=[XF])
                  rope(XF, XF[:, 1, :], 128, PA_ap, CA, SA, IKT, IKT[:, t0:t0 + G])
                  ckpt(1.3)
                  pn = pso.next()
                  for c in range(2):
                      ps = proj_fm(11 + c)
                      P.op("act", lambda h, ps=ps: h.activation(out=SQ[:, :], in_=ps[:, :], func=AF.Square), reads=[ps], writes=[SQ])
                      P.op("act", lambda h, ps=ps, c=c: h.activation(out=XF[:, c, :], in_=ps[:, :], func=AF.Copy), reads=[ps], writes=[XF])
                      mm(pn, pn[:, :], constf[:, CF_ONE:CF_ONE + 128], SQ[:, :], [constf, SQ], start=(c == 0), stop=(c == 1))
                  P.op("act", lambda h, pn=pn: h.activation(out=RSTD[:, :], in_=pn[:, :], func=AF.Sqrt, scale=1.0 / 256,
                                                            bias=constf[:, CF_POW + 30:CF_POW + 31]), reads=[pn, constf], writes=[RSTD])
                  P.op("dve", lambda h: h.reciprocal(out=RSTD[:, :], in_=RSTD[:, :]), reads=[RSTD], writes=[RSTD])
                  for c in range(2):
                      P.op("dve", lambda h, c=c: h.scalar_tensor_tensor(out=CQT[:, c, :], in0=XF[:, c, :],
                                                                        scalar=smallf[:, SF_GQ + c:SF_GQ + c + 1],
                                                                        in1=RSTD[:, :], op0=ALU.mult, op1=ALU.mult),
                           reads=[XF, smallf, RSTD], writes=[CQT])
                  ckpt(1.4)
                  ps = proj_fm(13)
                  P.op("act", lambda h, ps=ps: h.activation(out=SQ[:, :], in_=ps[:, :], func=AF.Square), reads=[ps], writes=[SQ])
                  P.op("act", lambda h, ps=ps: h.activation(out=XF[:, 0, :], in_=ps[:, :], func=AF.Copy), reads=[ps], writes=[XF])
                  pn = psf.next()
                  mm(pn, pn[:, :], constf[:, CF_ONE:CF_ONE + 128], SQ[:, :], [constf, SQ])
                  P.op("act", lambda h, pn=pn: h.activation(out=RSTD[:, :], in_=pn[:, :], func=AF.Sqrt, scale=1.0 / 128,
                                                            bias=constf[:, CF_POW + 30:CF_POW + 31]), reads=[pn, constf], writes=[RSTD])
                  P.op("dve", lambda h: h.reciprocal(out=RSTD[:, :], in_=RSTD[:, :]), reads=[RSTD], writes=[RSTD])
                  P.op("dve", lambda h: h.scalar_tensor_tensor(out=CKVS[:, t0:t0 + G], in0=XF[:, 0, :], scalar=smallf[:, SF_GKV:SF_GKV + 1],
                                                               in1=RSTD[:, :], op0=ALU.mult, op1=ALU.mult),
                       reads=[XF, smallf, RSTD], writes=[CKVS])
                  ckpt(1.5)
                  ps = psf.next()
                  for kc in range(8):
                      mm(ps, ps[0:32, :], wmisc[:, kc, 0:32], hnT[:, kc, :], [wmisc, hnT], start=(kc == 0), stop=(kc == 7))
                  rope(ps, ps[0:32, :], 32, PB_ap, CBt, SBt, KRT, KRT[0:32, t0:t0 + G])
                  ckpt(1.6)
                  for r in range(4):
                      tix = g * 4 + r
                      ps = psf.next()
                      for kc in range(8):
                          mm(ps, ps[:, 0:136], hnT[:, kc, r * 128:(r + 1) * 128], wmisc[:, kc, 32:168], [wmisc, hnT],
                             start=(kc == 0), stop=(kc == 7))
                      P.op("act", lambda h, ps=ps, tix=tix: h.activation(
                          out=AV[:, tix, :, 0:64], in_=ps[:, 0:128].rearrange("p (a b) -> p a b", a=2), func=AF.Copy),
                          reads=[ps], writes=[AV])
                      P.op("dve", lambda h, ps=ps, tix=tix: h.tensor_scalar(
                          out=IW[:, tix, :], in0=ps[:, 128:136], scalar1=0.125 * 8 ** -0.5, scalar2=None, op0=ALU.mult),
                          reads=[ps], writes=[IW])
                      ps2 = psf.next()
                      mm(ps2, ps2[:, :], CKVS[:, t0 + r * 128:t0 + (r + 1) * 128], wuv[:, :], [CKVS, wuv])
                      P.op("act", lambda h, ps2=ps2, tix=tix: h.activation(
                          out=VB[:, tix, :, 0:64], in_=ps2[:, :].rearrange("p (a b) -> p a b", a=8), func=AF.Copy),
                          reads=[ps2], writes=[VB])

                  ckpt(2)
                  nkt_g = 4 * g + 4

                  def gen_scores(r, ACCx):
                      qt = 4 * g + r
                      nk = 128 * (qt + 1)
                      for kb in range(0, nk, 512):
                          kw = min(512, nk - kb)
                          for hh in range(8):
                              c = hh // 2
                              pb_ = 64 * (hh % 2)
                              ps = psf.next()
                              mm(ps, ps[:, 0:kw], IQT[pb_:pb_ + 64, c, r * 128:(r + 1) * 128], IKT[pb_:pb_ + 64, kb:kb + kw],
                                 [IQT, IKT])
                              rl = RLr.next()
                              P.op("act", lambda h: h.activation(out=rl[:, 0:kw], in_=ps[:, 0:kw], func=AF.Relu),
                                   reads=[ps], writes=[rl])
                              if hh == 0:
                                  P.op("dve", lambda h: h.tensor_scalar(
                                      out=ACCx[:, kb:kb + kw], in0=rl[:, 0:kw], scalar1=IW[:, qt, hh:hh + 1], scalar2=None,
                                      op0=ALU.mult), reads=[rl, IW], writes=[ACCx])
                              else:
                                  P.op("dve", lambda h: h.scalar_tensor_tensor(
                                      out=ACCx[:, kb:kb + kw], in0=rl[:, 0:kw], scalar=IW[:, qt, hh:hh + 1],
                                      in1=ACCx[:, kb:kb + kw], op0=ALU.mult, op1=ALU.add), reads=[rl, IW, ACCx], writes=[ACCx])
                              yield 0.45

                  def gen_bisect(r, ACCx, BISx):
                      qt = 4 * g + r
                      nk = 128 * (qt + 1)
                      P.op("dve", lambda h: h.tensor_reduce(out=BISx[:, 0:1], in_=ACCx[:, 0:nk], axis=AX.X, op=ALU.max),
                           reads=[ACCx], writes=[BISx])
                      P.op("dve", lambda h: h.tensor_reduce(out=BISx[:, 1:2], in_=ACCx[:, 0:nk], axis=AX.X, op=ALU.min),
                           reads=[ACCx], writes=[BISx])
                      P.op("dve", lambda h: h.tensor_tensor(out=ACCx[:, nk - 128:nk], in0=ACCx[:, nk - 128:nk],
                                                            in1=constf[:, CF_NTRI:CF_NTRI + 128], op=ALU.add),
                           reads=[ACCx, constf], writes=[ACCx])
                      P.op("dve", lambda h: h.tensor_tensor(out=BISx[:, 2:3], in0=BISx[:, 0:1], in1=BISx[:, 1:2], op=ALU.subtract),
                           reads=[BISx], writes=[BISx])
                      P.op("dve", lambda h: h.scalar_tensor_tensor(out=BISx[:, 3:4], in0=BISx[:, 2:3], scalar=0.5, in1=BISx[:, 1:2],
                                                                   op0=ALU.mult, op1=ALU.add), reads=[BISx], writes=[BISx])
                      P.op("dve", lambda h: h.tensor_scalar(out=BISx[:, 8:8 + NIT], in0=constf[:, CF_POW:CF_POW + NIT],
                                                            scalar1=BISx[:, 2:3], scalar2=None, op0=ALU.mult),
                           reads=[BISx, constf], writes=[BISx])
                      P.op("dve", lambda h: h.tensor_scalar(out=BISx[:, 32:32 + NIT], in0=constf[:, CF_POW:CF_POW + NIT],
                                                            scalar1=BISx[:, 2:3], scalar2=-0.5, op0=ALU.mult, op1=ALU.mult),
                           reads=[BISx, constf], writes=[BISx])
                      for it in range(NIT):
                          yield 1.1
                          P.op("dve", lambda h: h.tensor_scalar(out=MQ[:, 0:nk], in0=ACCx[:, 0:nk], scalar1=BISx[:, 3:4],
                                                                scalar2=None, op0=ALU.is_ge, op1=ALU.add,
                                                                accum_out=BISx[:, 4:5]),
                               reads=[ACCx, BISx], writes=[MQ, BISx])
                          P.op("dve", lambda h: h.tensor_copy(out=BISx[:, 6:7], in_=BISx[:, 3:4]), reads=[], writes=[BISx])
                          P.op("dve", lambda h: h.tensor_scalar(out=BISx[:, 5:6], in0=BISx[:, 4:5], scalar1=float(TOPK) - 0.5,
                                                                scalar2=BISx[:, 8 + it:9 + it], op0=ALU.is_ge, op1=ALU.mult),
                               reads=[BISx], writes=[BISx])
                          if it < NIT - 1:
                              P.op("dve", lambda h: h.scalar_tensor_tensor(out=BISx[:, 3:4], in0=BISx[:, 5:6],
                                                                           scalar=BISx[:, 32 + it:33 + it], in1=BISx[:, 3:4],
                                                                           op0=ALU.add, op1=ALU.add),
                                   reads=[BISx], writes=[BISx])
                          else:
                              P.op("dve", lambda h: h.scalar_tensor_tensor(out=BISx[:, 3:4], in0=BISx[:, 5:6],
                                                                           scalar=BISx[:, 8 + it:9 + it], in1=BISx[:, 3:4],
                                                                           op0=ALU.subtract, op1=ALU.add),
                                   reads=[BISx], writes=[BISx])
                      P.op("dve", lambda h: h.tensor_scalar(out=MQ[:, 0:nk], in0=ACCx[:, 0:nk], scalar1=BISx[:, 3:4],
                                                            scalar2=None, op0=ALU.is_ge), reads=[ACCx, BISx], writes=[MQ])
                      for j0 in range(0, qt + 1, 8):
                          jn = min(8, qt + 1 - j0)
                          pb = psb.next()
                          for jj in range(jn):
                              j = j0 + jj
                              P.op("pe", lambda h: h.transpose(out=pb[:, jj * 128:(jj + 1) * 128],
                                                               in_=MQ[:, j * 128:(j + 1) * 128], identity=ident),
                                   reads=[MQ, constb], writes=[pb], inc=(jj == jn - 1))
                          P.op("act", lambda h: h.activation(
                              out=MT[:, j0:j0 + jn, r * 128:(r + 1) * 128],
                              in_=pb[:, 0:jn * 128].rearrange("p (a b) -> p a b", a=jn), func=AF.Copy),
                              reads=[pb], writes=[MT])
                      yield 0.5

                  def gen_indexer():
                      tiles = []
                      for r in range(4):
                          qt = 4 * g + r
                          if qt < 2:
                              for j in range(qt + 1):
                                  src = constb[:, CB_TRI:CB_TRI + 128] if j == qt else constb[:, CB_ONE:CB_ONE + 128]
                                  P.op("act", lambda h: h.activation(out=MT[:, j, r * 128:(r + 1) * 128], in_=src, func=AF.Copy),
                                       reads=[constb], writes=[MT])
                          else:
                              tiles.append(r)
                      for p0 in range(0, len(tiles), 2):
                          pair = [(r, (ACC, ACC2)[k], (BIS, BIS2)[k]) for k, r in enumerate(tiles[p0:p0 + 2])]
                          for (r, A_, B_) in pair:
                              for cst in gen_scores(r, A_):
                                  yield cst
                          sub = [[gen_bisect(*pr), 0.0, True] for pr in pair]
                          while any(x[2] for x in sub):
                              x = min([y for y in sub if y[2]], key=lambda q: q[1])
                              try:
                                  cst = next(x[0])
                                  x[1] += cst
                                  yield cst
                              except StopIteration:
                                  x[2] = False
                      yield 0.0

                  def gen_mla():
                      SK = 1
                      st = {}

                      def proj_stage(hh, stage):
                          if stage == 0:
                              ps = psf.next()
                              for rc in range(2):
                                  mm(ps, ps[0:64, :], wuq[:, rc, hh, 0:64], CQT[:, rc, :], [wuq, CQT], start=(rc == 0), stop=(rc == 1))
                              P.op("act", lambda h, ps=ps: h.activation(out=QNT[0:64, :], in_=ps[0:64, :], func=AF.Copy), reads=[ps], writes=[QNT])
                              psr = psf.next()
                              for rc in range(2):
                                  mm(psr, psr[0:32, :], wuq[:, rc, hh, 64:96], CQT[:, rc, :], [wuq, CQT], start=(rc == 0), stop=(rc == 1))
                              rope_a(psr, psr[0:32, :], 32, RXB2)
                              P.op("act", lambda h, psr=psr: h.activation(out=QRF[0:32, :], in_=psr[0:32, :], func=AF.Copy), reads=[psr], writes=[QRF])
                              st[hh] = {}
                          elif stage == 1:
                              ps = psf.next()
                              mm(ps, ps[:, :], wukT[0:64, hh, :], QNT[0:64, :], [wukT, QNT])
                              qab = QABr.next()
                              P.op("act", lambda h, ps=ps, qab=qab: h.activation(out=qab[:, :], in_=ps[:, :], func=AF.Copy), reads=[ps], writes=[qab])
                              qr = QRr.next()
                              rope_b(QRF, QRF[0:32, :], 32, PB_ap, CBt, SBt, qr, qr[0:32, :], RXB2)
                              st[hh].update(qab=qab, qr=qr)

                      proj_stage(0, 0)
                      proj_stage(0, 1)
                      pending = None
                      for hh in range(8):
                          qab, qr = st[hh]["qab"], st[hh]["qr"]
                          po = pso.next()
                          q = []
                          for jx in range(nkt_g + SK):
                              if jx < nkt_g:
                                  j = jx
                                  c0 = 128 * max(0, j - 4 * g)
                                  ps = psq.next()
                                  mm(ps, ps[:, c0:G], CKVS[:, j * 128:(j + 1) * 128], qab[:, c0:G], [CKVS, qab], start=True, stop=False)
                                  mm(ps, ps[:, c0:G], KRT[0:32, j * 128:(j + 1) * 128], qr[0:32, c0:G], [KRT, qr], start=False, stop=True)
                                  q.append((j, ps, c0))
                              if jx >= SK:
                                  j, ps, c0 = q.pop(0)
                                  pt = PTr.next()
                                  P.op("act", lambda h, ps=ps, pt=pt, c0=c0: h.activation(out=pt[:, c0:G], in_=ps[:, c0:G], func=AF.Exp,
                                                                                          scale=96 ** -0.5), reads=[ps], writes=[pt])
                                  if j >= 4 * g:
                                      P.op("dve", lambda h, pt=pt, c0=c0: h.tensor_tensor(out=pt[:, c0:c0 + 128], in0=pt[:, c0:c0 + 128],
                                                                                          in1=constb[:, CB_TRI:CB_TRI + 128], op=ALU.mult),
                                           reads=[pt, constb], writes=[pt])
                                  mm(po, po[0:65, c0:G], VB[:, j, hh, :], pt[:, c0:G], [VB, pt], start=(j == 0), stop=(j == nkt_g - 1))
                              if jx == 1 and hh + 1 < 8:
                                  proj_stage(hh + 1, 0)
                              if jx == 2 and pending is not None:
                                  pending()
                                  pending = None
                              if jx == 3 and hh + 1 < 8:
                                  proj_stage(hh + 1, 1)
                              yield 1.0
                          pending = (lambda po=po, hh=hh: normalize_heads(po, OBT, hh))
                      pending()
                      for c in range(4):
                          ps = proj_fm(c)
                          rope(ps, ps[:, :], 128, PA_ap, CA, SA, AQT, AQT[:, c, :])
                          yield 4.0
                      for kg in range(2):
                          ps = proj_fm(4 + kg)
                          rope(ps, ps[:, :], 128, PA_ap, CA, SA, AKT, AKT[:, kg, t0:t0 + G])
                          yield 4.0
                      yield 0.0

                  P.barrier()
                  psf.items = _banks[2:4]
                  gens = [[gen_indexer(), 0.0, True], [gen_mla(), 0.0, True]]
                  while any(gv[2] for gv in gens):
                      live = [gv for gv in gens if gv[2]]
                      gv = min(live, key=lambda q: q[1])
                      try:
                          gv[1] += next(gv[0])
                      except StopIteration:
                          gv[2] = False
                  ckpt(3)
                  SK = 2
                  psq.items = _banks[0:3]
                  psf.items = _banks[3:4]
                  pending = None
                  for hh in range(8):
                      c = hh // 2
                      pb_ = 64 * (hh % 2)
                      kg = hh // 4
                      po = pso.next()
                      q = []
                      for jx in range(nkt_g + SK):
                          if jx < nkt_g:
                              j = jx
                              c0 = 128 * max(0, j - 4 * g)
                              ps = psq.next()
                              mm(ps, ps[:, c0:G], AKT[pb_:pb_ + 64, kg, j * 128:(j + 1) * 128], AQT[pb_:pb_ + 64, c, c0:G], [AKT, AQT])
                              q.append((j, ps, c0))
                          if jx >= SK:
                              j, ps, c0 = q.pop(0)
                              pt = PTr.next()
                              P.op("act", lambda h, ps=ps, pt=pt, c0=c0: h.activation(out=pt[:, c0:G], in_=ps[:, c0:G], func=AF.Exp,
                                                                                      scale=0.125), reads=[ps], writes=[pt])
                              P.op("dve", lambda h, pt=pt, c0=c0, j=j: h.tensor_tensor(out=pt[:, c0:G], in0=pt[:, c0:G],
                                                                                       in1=MT[:, j, c0:G], op=ALU.mult),
                                   reads=[pt, MT], writes=[pt])
                              mm(po, po[0:65, c0:G], AV[:, j, kg, :], pt[:, c0:G], [AV, pt], start=(j == 0), stop=(j == nkt_g - 1))
                          if jx == 2 and pending is not None:
                              pending()
                              pending = None
                      pending = (lambda po=po, hh=hh: normalize_heads(po, OAT, hh))
                  pending()


                  psf.items = list(_banks)
                  psq.items = _banks[0:2]
                  ckpt(4)
                  if taps and b == nseq - 1 and g == TAPG:
                      P.dma("pool", tap_d["oat"][:, :], OAT[:].rearrange("p a b -> p (a b)"), reads=[OAT])
                      P.dma("pool", tap_d["obt"][:, :], OBT[:].rearrange("p a b -> p (a b)"), reads=[OBT])
                      P.dma("pool", tap_d["mt"][:, :], MT[:, :, :].rearrange("p a b -> p (a b)"), reads=[MT])
                      if "ckvs" in tap_d:
                          P.dma("pool", tap_d["ckvs"][:, :], CKVS[:, :], reads=[CKVS])
                          P.dma("pool", tap_d["krt"][:, :], KRT[:, :], reads=[KRT])
                          P.dma("pool", tap_d["vb"][:, :], VB[:].rearrange("p a b c -> p (a b c)"), reads=[VB])
                          P.dma("pool", tap_d["qab"][:, :], qab[:, :], reads=[qab])
                          P.dma("pool", tap_d["qr"][:, :], qr[0:32, :], reads=[qr])
                          P.dma("pool", tap_d["cqt"][:, :], CQT[:, :, :].rearrange("p a b -> p (a b)"), reads=[CQT])
                          P.dma("sp", tap_d["osb"][:, :], OSB[:, :], reads=[OSB])
                          P.dma("sp", tap_d["rec"][:, :], REC[:, :], reads=[REC])
                          P.dma("sp", tap_d["rstd"][:, :], RSTD[:, :], reads=[RSTD])
                          P.dma("sp", tap_d["xf0"][:, :], XF[:, 0, :], reads=[XF])
                          P.dma("sp", tap_d["sq"][:, :], SQ[:, :], reads=[SQ])
                  ckpt(5)
                  P.barrier()
                  psf.items = list(_banks) + list(pso.items)
                  for c in range(8):
                      wa = load_w(wsmall, lambda w: w[0:64, :], "wba", c)
                      pa = psf.next()
                      for hh in range(8):
                          mm(pa, pa[:, :], wa[0:64, hh * 128:(hh + 1) * 128], OAT[0:64, hh, :], [wa, OAT], start=(hh == 0), stop=(hh == 7))
                      wb = load_w(wsmall, lambda w: w[0:64, :], "wbb", c)
                      pb2 = psf.next()
                      for hh in range(8):
                          mm(pb2, pb2[:, :], wb[0:64, hh * 128:(hh + 1) * 128], OBT[0:64, hh, :], [wb, OBT], start=(hh == 0), stop=(hh == 7))
                      wga = load_w(wsmall, lambda w: w[:, :], "win", 14 + c)
                      pga = psf.next()
                      for kc in range(8):
                          mm(pga, pga[:, :], wga[:, kc * 128:(kc + 1) * 128], hnT[:, kc, :], [wga, hnT], start=(kc == 0), stop=(kc == 7))
                      P.op("act", lambda h, pga=pga: h.activation(out=SGA[:, :], in_=pga[:, :], func=AF.Sigmoid), reads=[pga], writes=[SGA])
                      wgb = load_w(wsmall, lambda w: w[:, :], "win", 22 + c)
                      pgb = psf.next()
                      for kc in range(8):
                          mm(pgb, pgb[:, :], wgb[:, kc * 128:(kc + 1) * 128], hnT[:, kc, :], [wgb, hnT], start=(kc == 0), stop=(kc == 7))
                      P.op("act", lambda h, pgb=pgb: h.activation(out=SGB[:, :], in_=pgb[:, :], func=AF.Sigmoid), reads=[pgb], writes=[SGB])
                      P.op("dve", lambda h, pa=pa: h.tensor_tensor(out=GT1[:, :], in0=pa[:, :], in1=SGA[:, :], op=ALU.mult),
                           reads=[pa, SGA], writes=[GT1])
                      P.op("dve", lambda h, pb2=pb2: h.tensor_tensor(out=GT2[:, :], in0=pb2[:, :], in1=SGB[:, :], op=ALU.mult),
                           reads=[pb2, SGB], writes=[GT2])
                      P.op("dve", lambda h, c=c: h.tensor_tensor(out=MRG[:, c, :], in0=GT1[:, :], in1=GT2[:, :], op=ALU.add),
                           reads=[GT1, GT2], writes=[MRG])
                  ckpt(6)
                  for nq in range(4):
                      w = load_w(wmid, lambda w: w[:, 0:2048], "wo", nq)
                      for r in range(4):
                          ps = psf.next()
                          for kc in range(8):
                              mm(ps, ps[:, 0:256], MRG[:, kc, r * 128:(r + 1) * 128], w[:, kc * 256:(kc + 1) * 256], [MRG, w],
                                 start=(kc == 0), stop=(kc == 7))
                          P.op("dve", lambda h, ps=ps, r=r, nq=nq: h.tensor_tensor(
                              out=XH[:, r, nq * 256:(nq + 1) * 256], in0=XH[:, r, nq * 256:(nq + 1) * 256], in1=ps[:, 0:256], op=ALU.add),
                              reads=[XHr[r], ps], writes=[XHr[r]])
                  for r in range(4):
                      rmsnorm_tok(XHr[r], XH[:, r, :], SF_GFFN, hn2T, r, HN2TOK)
                  ckpt(7)
                  for fh in range(2):
                      for fi in range(11):
                          f = fh * 11 + fi
                          w = load_w(wmid, lambda w: w[:, 0:2048], "wup", f)
                          pg = psf.next()
                          for kc in range(8):
                              mm(pg, pg[:, :], w[:, kc * 256:kc * 256 + 128], hn2T[:, kc, :], [w, hn2T], start=(kc == 0), stop=(kc == 7))
                          pv = psf.next()
                          for kc in range(8):
                              mm(pv, pv[:, :], w[:, kc * 256 + 128:kc * 256 + 256], hn2T[:, kc, :], [w, hn2T], start=(kc == 0), stop=(kc == 7))
                          res = []
                          for (pp, ch, ring_u, ring_a) in ((pg, f, UGr, AGr), (pv, 22 + f, UVr, AVr)):
                              ue = ring_u.next()
                              ac = ring_a.next()
                              cw = SF_CW + ch * 3
                              P.op("act", lambda h, ue=ue, ch=ch: h.activation(out=ue[:, 0:2], in_=CARRY[:, ch, :], func=AF.Copy),
                                   reads=[CARRY], writes=[ue])
                              P.op("act", lambda h, ue=ue, pp=pp: h.activation(out=ue[:, 2:514], in_=pp[:, :], func=AF.Copy),
                                   reads=[pp], writes=[ue])
                              P.op("act", lambda h, ac=ac, pp=pp, cw=cw, ch=ch: h.activation(
                                  out=ac[:, :], in_=pp[:, :], func=AF.Identity, scale=smallf[:, cw + 2:cw + 3],
                                  bias=smallf[:, SF_CB + ch:SF_CB + ch + 1]), reads=[pp, smallf], writes=[ac])
                              P.op("act", lambda h, ue=ue, ch=ch: h.activation(out=CARRY[:, ch, :], in_=ue[:, 512:514], func=AF.Copy),
                                   reads=[ue], writes=[CARRY])
                              P.op("dve", lambda h, ue=ue, ac=ac, cw=cw: h.scalar_tensor_tensor(
                                  out=ac[:, :], in0=ue[:, 1:513], scalar=smallf[:, cw + 1:cw + 2], in1=ac[:, :],
                                  op0=ALU.mult, op1=ALU.add), reads=[ue, smallf, ac], writes=[ac])
                              P.op("dve", lambda h, ue=ue, ac=ac, cw=cw: h.scalar_tensor_tensor(
                                  out=ac[:, :], in0=ue[:, 0:512], scalar=smallf[:, cw:cw + 1], in1=ac[:, :],
                                  op0=ALU.mult, op1=ALU.add), reads=[ue, smallf, ac], writes=[ac])
                              res.append(ac)
                          sg = SGr.next()
                          P.op("act", lambda h, sg=sg, a=res[0]: h.activation(out=sg[:, :], in_=a[:, :], func=AF.Silu),
                               reads=[res[0]], writes=[sg])
                          P.op("dve", lambda h, sg=sg, a=res[1], fi=fi: h.tensor_tensor(out=ACT_[:, fi, :], in0=sg[:, :], in1=a[:, :], op=ALU.mult),
                               reads=[sg, res[1]], writes=[ACT_])
                      for nq in range(4):
                          w = load_w(wmid, lambda w: w[:, :], "wd", fh * 4 + nq)
                          for r in range(4):
                              ps = psf.next()
                              for fi in range(11):
                                  mm(ps, ps[:, 0:256], ACT_[:, fi, r * 128:(r + 1) * 128], w[:, fi * 256:(fi + 1) * 256], [ACT_, w],
                                     start=(fi == 0), stop=(fi == 10))
                              P.op("dve", lambda h, ps=ps, r=r, nq=nq: h.tensor_tensor(
                                  out=XH[:, r, nq * 256:(nq + 1) * 256], in0=XH[:, r, nq * 256:(nq + 1) * 256], in1=ps[:, 0:256], op=ALU.add),
                                  reads=[XHr[r], ps], writes=[XHr[r]])
                  ckpt(8)
                  for r in range(4):
                      st = stats
                      P.op("act", lambda h, r=r: h.activation(out=HN2TOK[:, :], in_=XH[:, r, :], func=AF.Square, accum_out=st[:, 4:5]),
                           reads=[XHr[r]], writes=[HN2TOK, st])
                      P.op("act", lambda h: h.activation(out=st[:, 8:9], in_=st[:, 9:10], func=AF.Copy), reads=[], writes=[st])
                      P.op("act", lambda h: h.activation(out=st[:, 6:7], in_=st[:, 4:5], func=AF.Sqrt, scale=1.0 / D,
                                                         bias=constf[:, CF_POW + 30:CF_POW + 31]), reads=[st, constf], writes=[st])
                      P.op("dve", lambda h: h.reciprocal(out=st[:, 6:7], in_=st[:, 6:7]), reads=[st], writes=[st])
                      on = ONr[r % 2]
                      P.op("dve", lambda h, r=r: h.scalar_tensor_tensor(out=on[:, :], in0=XH[:, r, :], scalar=st[:, 6:7], in1=gfin[:, :],
                                                                        op0=ALU.mult, op1=ALU.mult), reads=[XHr[r], st, gfin], writes=[on])
                      d = P.dma("sp", out_d[b, t0 + r * 128:t0 + (r + 1) * 128, :], on[:, :], reads=[on])
                      out_deps.append(d)
                  if first_group[0]:
                      while pending_conv:
                          conv_one()
                      first_group[0] = False
        except _Stop:
            pass
        if taps and "c_wuq" in tap_d:
            P.dma("pool", tap_d["c_wuq"][:, :], wuq[:].rearrange("p a b c -> p (a b c)"), reads=[wuq])
            P.dma("pool", tap_d["c_wukT"][:, :], wukT[:].rearrange("p a b -> p (a b)"), reads=[wukT])
            P.dma("pool", tap_d["c_wuv"][:, :], wuv[:], reads=[wuv])
            P.dma("pool", tap_d["c_wmisc"][:, :], wmisc[:].rearrange("p a b -> p (a b)"), reads=[wmisc])
            P.dma("pool", tap_d["c_constb"][:, :], constb[:], reads=[constb])
            P.dma("sp", tap_d["c_constf"][:, :], constf[:], reads=[constf])
            P.dma("sp", tap_d["c_smallf"][:, :], smallf[:], reads=[smallf])
            P.dma("sp", tap_d["c_gfin"][:, :], gfin[:], reads=[gfin])
            out_deps += [(k, v) for k, v in P.dma_cnt.items()]
        P.wait_all("sp", out_deps)
        P.replay(es)
    return nc


def _consts():
    cb = np.zeros((128, CB_N), np.float32)
    cb[:, CB_ID:CB_ID + 128] = np.eye(128)
    pa = np.zeros((128, 128), np.float32)
    for hb in (0, 64):
        for j in range(8):
            pa[hb + j + 8, hb + j] = -1.0
            pa[hb + j, hb + j + 8] = 1.0
    cb[:, CB_PA:CB_PA + 128] = pa
    pb = np.zeros((32, 32), np.float32)
    for j in range(16):
        pb[j + 16, j] = -1.0
        pb[j, j + 16] = 1.0
    cb[0:32, CB_PB:CB_PB + 32] = pb
    k = np.arange(128)[:, None]
    q = np.arange(128)[None, :]
    cb[:, CB_TRI:CB_TRI + 128] = (k <= q).astype(np.float32)
    cb[:, CB_ONE:CB_ONE + 128] = 1.0
    cf = np.zeros((128, CF_N), np.float32)
    cf[:, CF_ONE:CF_ONE + 128] = 1.0
    bd = np.zeros((128, 128), np.float32)
    bd[0:64, 0:64] = 1.0
    bd[64:128, 64:128] = 1.0
    cf[:, CF_BD:CF_BD + 128] = bd
    cf[:, CF_NTRI:CF_NTRI + 128] = np.where(q.T >= k.T * 0 + np.arange(128)[None, :], 0.0, 0.0)
    qq = np.arange(128)[:, None]
    kk = np.arange(128)[None, :]
    cf[:, CF_NTRI:CF_NTRI + 128] = np.where(kk <= qq, 0.0, NEG)
    cf[:, CF_POW:CF_POW + 30] = 2.0 ** -(np.arange(30) + 1.0)
    cf[:, CF_POW + 30] = EPS
    cf[:, CF_POW + 31] = -0.5
    p = np.arange(128)
    j = p % 64
    inva = np.where(j < 16, THETA ** (-(2.0 * (j % 8)) / 16.0), 0.0)
    cf[:, CF_INVA] = inva / (2 * math.pi)
    invb = np.where(p < 32, THETA ** (-(2.0 * (p % 16)) / 32.0), 0.0)
    cf[:, CF_INVB] = invb / (2 * math.pi)
    cf[:, CF_NPI] = -math.pi * (1.0 - 2e-6)
    return cb, cf


def _prep_weights(norm_mix_g, w_in, idx_k_norm_g, q_a_norm_g, kv_a_norm_g, w_uq, w_uk, w_uv,
                  w_branch_a, w_branch_b, w_out, norm_ffn_g, w_up, conv_w, conv_b, w_down, norm_final_g):
    f = lambda a: np.ascontiguousarray(np.asarray(a, dtype=np.float32))
    w_in = f(w_in)[0]
    cols = []
    for c in range(4):
        cols.append(np.arange(c * 128, (c + 1) * 128))
    for kg in range(2):
        base = 512 + kg * 64
        cols.append(np.concatenate([np.arange(base, base + 64)] * 2))
    for c in range(4):
        cols.append(np.arange(768 + c * 128, 768 + (c + 1) * 128))
    cols.append(np.concatenate([np.arange(1280, 1344)] * 2))
    for c in range(2):
        cols.append(np.arange(1352 + c * 128, 1352 + (c + 1) * 128))
    cols.append(np.arange(1608, 1736))
    for c in range(8):
        cols.append(np.arange(1768 + c * 128, 1768 + (c + 1) * 128))
    for c in range(8):
        cols.append(np.arange(2792 + c * 128, 2792 + (c + 1) * 128))
    assert len(cols) == 30

    def fm(wc):
        M = wc.shape[1]
        return np.ascontiguousarray(wc.reshape(8, 128, M).transpose(1, 0, 2)).reshape(128, 8 * M)
    win = np.stack([fm(w_in[:, c]) for c in cols])
    misc_cols = np.concatenate([np.arange(1736, 1768), np.arange(640, 768), np.arange(1344, 1352)])
    wmisc = fm(w_in[:, misc_cols])
    wuq = np.ascontiguousarray(f(w_uq)[0].reshape(2, 128, 8, 96).transpose(1, 0, 2, 3)).reshape(128, 2 * 8 * 96)
    wukT = np.ascontiguousarray(f(w_uk)[0].transpose(2, 1, 0)).reshape(64, 8 * 128)
    wuv = f(w_uv)[0].reshape(128, 512)

    def br(w):
        w = f(w)[0].reshape(8, 64, 8, 128)
        return np.ascontiguousarray(w.transpose(2, 1, 0, 3)).reshape(8, 64, 8 * 128)
    wba, wbb = br(w_branch_a), br(w_branch_b)
    wo = np.ascontiguousarray(f(w_out)[0].reshape(8, 128, 4, 256).transpose(2, 1, 0, 3)).reshape(4, 128, 8 * 256)
    wu = f(w_up)[0].reshape(8, 128, 2, 22, 128)
    wup = np.ascontiguousarray(wu.transpose(3, 1, 0, 2, 4)).reshape(22, 128, 8 * 256)
    wdn = f(w_down)[0].reshape(2, 11, 128, 4, 256)
    wd = np.ascontiguousarray(wdn.transpose(0, 3, 2, 1, 4)).reshape(8, 128, 11 * 256)
    sf = np.zeros((128, SF_N), np.float32)
    sf[:, SF_GMIX:SF_GMIX + 8] = f(norm_mix_g)[0].reshape(8, 128).T
    sf[:, SF_GFFN:SF_GFFN + 8] = f(norm_ffn_g)[0].reshape(8, 128).T
    sf[:, SF_GIK] = np.concatenate([f(idx_k_norm_g)[0]] * 2)
    sf[:, SF_GQ:SF_GQ + 2] = f(q_a_norm_g)[0].reshape(2, 128).T
    sf[:, SF_GKV] = f(kv_a_norm_g)[0]
    cw = f(conv_w)[0].reshape(3, 44, 128)
    sf[:, SF_CW:SF_CW + 132] = cw.transpose(2, 1, 0).reshape(128, 132)
    sf[:, SF_CB:SF_CB + 44] = f(conv_b)[0].reshape(44, 128).T
    gfin = f(norm_final_g).reshape(1, D)
    return dict(win=win, wmisc=wmisc, wuq=wuq, wukT=wukT, wuv=wuv, wba=wba, wbb=wbb, wo=wo, wup=wup, wd=wd,
                smallf=sf, gfin=gfin)


_CACHE = {}


def kernel(x, positions, **weights):
    x = np.asarray(x, dtype=np.float32)
    positions = np.asarray(positions, dtype=np.int32)
    shared = _prep_weights(**weights)
    cb, cf = _consts()
    shared["constb"] = cb
    shared["constf"] = cf
    if "nc" not in _CACHE:
        _CACHE["nc"] = build_program()
    nc = _CACHE["nc"]
    in_maps = []
    for c in range(NCORES):
        m = dict(shared)
        m["x"] = np.ascontiguousarray(x[c * NSEQ:(c + 1) * NSEQ])
        m["pos"] = np.ascontiguousarray(positions[c * NSEQ:(c + 1) * NSEQ])
        in_maps.append(m)
    res = run_bass_kernel_spmd(nc, in_maps, core_ids=list(range(NCORES)))
    return np.concatenate([r["out"] for r in res.results], axis=0).astype(np.float32)
```
